# Optimizing a Trainium2 kernel written in Bass

```python
import math
import jax, jax.numpy as jnp
from jax import lax
import numpy as np

D_MODEL = 1024
BATCH = 8
SEQ = 8192
DEPTH = 1
DEC_BATCH = 8
DEC_SEQ = 2048
PAST_LEN = 128

N_MEM = 256
HEAD_DIM = 64
D_HY = 384
N_ATT_HEADS = 6
D_ATT = N_ATT_HEADS * HEAD_DIM
N_MEM_HEADS = 4
D_MEM = N_MEM_HEADS * HEAD_DIM
D_MIX = D_HY + D_ATT + D_MEM
D_IN = 3 * D_HY + 3 * D_ATT + D_MEM + D_MIX
FILTER_BANDS = 16
FILTER_EMB = 1 + 2 * FILTER_BANDS
FILTER_HIDDEN = 64
DECAY_TARGET = 1e-2
FAST_DECAY_PCT = 0.3
SLOW_DECAY_PCT = 1.5
DILATED_CONFIGS = ((128, 1), (512, 4), (2048, 16))
N_BUCKETS = 32
MAX_DISTANCE = 1024
RMS_EPS = 1e-6
NEG_INF = -1e30

kernel_name = 'hymba_hyena_dilated_memxattn_encoder'


def rms_norm(x, g):
    x32 = x.astype(jnp.float32)
    y = x32 * lax.rsqrt(jnp.mean(x32 * x32, axis=-1, keepdims=True) + RMS_EPS) * g.astype(jnp.float32)
    return y.astype(x.dtype)


def head_rms(x, g):
    x32 = x.astype(jnp.float32)
    return x32 * lax.rsqrt(jnp.mean(x32 * x32, axis=-1, keepdims=True) + RMS_EPS) * g.astype(jnp.float32)


def short_conv3(z, w, b):
    zp = jnp.pad(z, ((0, 0), (1, 1), (0, 0)))
    return zp[:, :-2] * w[0] + zp[:, 1:-1] * w[1] + zp[:, 2:] * w[2] + b


def hyena_filter(L, w1, b1, freq, w2, b2, w3):
    f32 = jnp.float32
    pos = jnp.arange(L, dtype=f32)[:, None]
    t = pos / max(L - 1, 1)
    bands = jnp.linspace(1e-4, FILTER_BANDS - 1, FILTER_BANDS, dtype=f32)[None, :]
    ang = (2.0 * math.pi / L) * pos * bands
    feats = jnp.concatenate([t, jnp.cos(ang), -jnp.sin(ang)], axis=-1)
    fr = freq.astype(f32)
    h = jnp.sin(fr * (feats @ w1.astype(f32) + b1.astype(f32)))
    h = jnp.sin(fr * (h @ w2.astype(f32) + b2.astype(f32)))
    h = (h @ w3.astype(f32)).reshape(L, 2, D_HY)
    deltas = jnp.abs(jnp.linspace(math.log(DECAY_TARGET) / SLOW_DECAY_PCT,
                                  math.log(DECAY_TARGET) / FAST_DECAY_PCT, D_HY, dtype=f32))
    h = h * jnp.exp(-t * deltas)[:, None, :]
    fwd, bwd = h[:, 0], h[:, 1]
    k = jnp.concatenate([fwd[:1] + bwd[:1], fwd[1:], jnp.zeros((1, D_HY), f32), bwd[:0:-1]], axis=0)
    return k * lax.rsqrt(jnp.sum(k * k, axis=0, keepdims=True) + 1e-12)


def hyena_branch(z_hy, conv_w, conv_b, w1, b1, freq, w2, b2, w3, skip):
    B, L, _ = z_hy.shape
    u = short_conv3(z_hy.astype(jnp.float32), conv_w.astype(jnp.float32), conv_b.astype(jnp.float32))
    x0, x1, v = jnp.split(u, 3, axis=-1)
    s = x1 * v
    k = hyena_filter(L, w1, b1, freq, w2, b2, w3)
    S = jnp.fft.rfft(s, n=2 * L, axis=1)
    K = jnp.fft.rfft(k, axis=0)
    conv = jnp.fft.irfft(S * K[None], n=2 * L, axis=1)[:, :L]
    return x0 * (conv + s * skip.astype(jnp.float32))


def t5_bucket(rel):
    half = N_BUCKETS // 2
    max_exact = half // 2
    ret = jnp.where(rel > 0, half, 0)
    n = jnp.abs(rel)
    large = max_exact + (jnp.log(jnp.maximum(n, 1).astype(jnp.float32) / max_exact)
                         / math.log(MAX_DISTANCE / max_exact) * (half - max_exact)).astype(jnp.int32)
    large = jnp.minimum(large, half - 1)
    return ret + jnp.where(n < max_exact, n, large)


def dilated_attention(q, k, v, rel_bias):
    B, L, H, E = q.shape
    scale = E ** -0.5
    outs, lses = [], []
    for window, dil in DILATED_CONFIGS:
        R = window // (2 * dil)
        blk = R
        n = L // dil
        nblk = -(-n // blk)
        pad = nblk * blk - n

        def to_blocks(a):
            a = a.reshape(B, n, dil, H, E).transpose(0, 2, 1, 3, 4)
            a = jnp.pad(a, ((0, 0), (0, 0), (0, pad), (0, 0), (0, 0)))
            return a.reshape(B, dil, nblk, blk, H, E)

        def band(a):
            ap = jnp.pad(a, ((0, 0), (0, 0), (1, 1), (0, 0), (0, 0), (0, 0)))
            return jnp.concatenate([ap[:, :, :-2], ap[:, :, 1:-1], ap[:, :, 2:]], axis=3)

        def from_blocks(a):
            a = a.reshape((B, dil, nblk * blk) + a.shape[4:])[:, :, :n]
            a = jnp.moveaxis(a, 1, 2)
            return a.reshape((B, L) + a.shape[3:])

        qb = to_blocks(q)
        kb = band(to_blocks(k))
        vb = band(to_blocks(v))
        qi = jnp.arange(blk)[:, None]
        kj = jnp.arange(3 * blk)[None, :]
        rel = kj - blk - qi
        kpos = jnp.arange(nblk)[:, None, None] * blk - blk + kj[None]
        valid = (jnp.abs(rel) <= R)[None] & (kpos >= 0) & (kpos < n)
        bias = rel_bias[t5_bucket(rel * dil)].astype(jnp.float32).transpose(2, 0, 1)
        logits = jnp.einsum('bdnqhe,bdnkhe->bdnhqk', qb, kb) * scale + bias
        logits = jnp.where(valid[None, None, :, None], logits, NEG_INF)
        m = jnp.max(logits, axis=-1, keepdims=True)
        p = jnp.exp(logits - m)
        s = jnp.sum(p, axis=-1, keepdims=True)
        o = jnp.einsum('bdnhqk,bdnkhe->bdnqhe', p, vb) / jnp.swapaxes(s, 3, 4)
        lse = jnp.swapaxes((m + jnp.log(s))[..., 0], 3, 4)
        outs.append(from_blocks(o))
        lses.append(from_blocks(lse))
    wts = jax.nn.softmax(jnp.stack(lses, axis=0), axis=0)
    return jnp.sum(wts[..., None] * jnp.stack(outs, axis=0), axis=0)


def memory_attention(qm, mem, mem_norm, w_mem_kv, k_gain):
    B, M, _ = mem.shape
    kv = (rms_norm(mem, mem_norm) @ w_mem_kv).reshape(B, M, 2, N_MEM_HEADS, HEAD_DIM)
    km = head_rms(kv[:, :, 0], k_gain)
    vm = kv[:, :, 1].astype(jnp.float32)
    logits = jnp.einsum('blhe,bmhe->bhlm', qm, km) * (HEAD_DIM ** -0.5)
    p = jax.nn.softmax(logits, axis=-1)
    return jnp.einsum('bhlm,bmhe->blhe', p, vm)


def encoder_layer(x, mem, norm_in, w_in, hy_conv_w, hy_conv_b, hy_filt_w1, hy_filt_b1, hy_filt_freq,
                  hy_filt_w2, hy_filt_b2, hy_filt_w3, hy_skip, att_q_norm, att_k_norm,
                  mem_norm, w_mem_kv, mem_q_norm, mem_k_norm, w_out, rel_bias):
    B, L, _ = x.shape
    h = rms_norm(x, norm_in)
    z = h @ w_in
    i0 = 3 * D_HY
    i1 = i0 + 3 * D_ATT
    i2 = i1 + D_MEM
    z_hy, z_att, z_mq, gate = z[..., :i0], z[..., i0:i1], z[..., i1:i2], z[..., i2:]
    y_hy = hyena_branch(z_hy, hy_conv_w, hy_conv_b, hy_filt_w1, hy_filt_b1, hy_filt_freq,
                        hy_filt_w2, hy_filt_b2, hy_filt_w3, hy_skip)
    qkv = z_att.reshape(B, L, 3, N_ATT_HEADS, HEAD_DIM)
    q = head_rms(qkv[:, :, 0], att_q_norm)
    k = head_rms(qkv[:, :, 1], att_k_norm)
    v = qkv[:, :, 2].astype(jnp.float32)
    y_att = dilated_attention(q, k, v, rel_bias).reshape(B, L, D_ATT)
    qm = head_rms(z_mq.reshape(B, L, N_MEM_HEADS, HEAD_DIM), mem_q_norm)
    y_mem = memory_attention(qm, mem, mem_norm, w_mem_kv, mem_k_norm).reshape(B, L, D_MEM)
    mixed = jnp.concatenate([y_hy, y_att, y_mem], axis=-1) * jax.nn.silu(gate.astype(jnp.float32))
    return x + mixed.astype(x.dtype) @ w_out


def setup_inputs(seed: int = 0) -> dict:
    key = jax.random.key(seed)
    ks = jax.random.split(key, 24)
    f32 = jnp.float32
    nrm = lambda k, shape, s: jax.random.normal(k, shape, f32) * s
    gain = lambda k, shape: 1.0 + 0.02 * jax.random.normal(k, shape, f32)
    return {
        'x_prompt': jax.random.normal(ks[0], (BATCH, SEQ, D_MODEL), f32),
        'x_sample': jax.random.normal(ks[1], (DEC_BATCH, DEC_SEQ, D_MODEL), f32),
        'mem_prompt': jax.random.normal(ks[2], (BATCH, N_MEM, D_MODEL), f32),
        'mem_sample': jax.random.normal(ks[3], (DEC_BATCH, N_MEM, D_MODEL), f32),
        'norm_in': gain(ks[4], (DEPTH, D_MODEL)),
        'w_in': nrm(ks[5], (DEPTH, D_MODEL, D_IN), D_MODEL ** -0.5),
        'hy_conv_w': nrm(ks[6], (DEPTH, 3, 3 * D_HY), 3 ** -0.5),
        'hy_conv_b': nrm(ks[7], (DEPTH, 3 * D_HY), 0.02),
        'hy_filt_w1': nrm(ks[8], (DEPTH, FILTER_EMB, FILTER_HIDDEN), FILTER_EMB ** -0.5),
        'hy_filt_b1': nrm(ks[9], (DEPTH, FILTER_HIDDEN), 0.02),
        'hy_filt_freq': gain(ks[10], (DEPTH, FILTER_HIDDEN)),
        'hy_filt_w2': nrm(ks[11], (DEPTH, FILTER_HIDDEN, FILTER_HIDDEN), FILTER_HIDDEN ** -0.5),
        'hy_filt_b2': nrm(ks[12], (DEPTH, FILTER_HIDDEN), 0.02),
        'hy_filt_w3': nrm(ks[13], (DEPTH, FILTER_HIDDEN, 2 * D_HY), FILTER_HIDDEN ** -0.5),
        'hy_skip': nrm(ks[14], (DEPTH, D_HY), 1.0),
        'att_q_norm': gain(ks[15], (DEPTH, HEAD_DIM)),
        'att_k_norm': gain(ks[16], (DEPTH, HEAD_DIM)),
        'mem_norm': gain(ks[17], (DEPTH, D_MODEL)),
        'w_mem_kv': nrm(ks[18], (DEPTH, D_MODEL, 2 * D_MEM), D_MODEL ** -0.5),
        'mem_q_norm': gain(ks[19], (DEPTH, HEAD_DIM)),
        'mem_k_norm': gain(ks[20], (DEPTH, HEAD_DIM)),
        'w_out': nrm(ks[21], (DEPTH, D_MIX, D_MODEL), D_MIX ** -0.5),
        'rel_bias': nrm(ks[22], (N_BUCKETS, N_ATT_HEADS), 0.1),
    }


def reference(x_prompt, x_sample, mem_prompt, mem_sample, norm_in, w_in, hy_conv_w, hy_conv_b,
              hy_filt_w1, hy_filt_b1, hy_filt_freq, hy_filt_w2, hy_filt_b2, hy_filt_w3, hy_skip,
              att_q_norm, att_k_norm, mem_norm, w_mem_kv, mem_q_norm, mem_k_norm, w_out, rel_bias):
    def trunk(x, mem):
        for l in range(DEPTH):
            x = encoder_layer(x, mem, norm_in[l], w_in[l], hy_conv_w[l], hy_conv_b[l],
                              hy_filt_w1[l], hy_filt_b1[l], hy_filt_freq[l], hy_filt_w2[l],
                              hy_filt_b2[l], hy_filt_w3[l], hy_skip[l], att_q_norm[l], att_k_norm[l],
                              mem_norm[l], w_mem_kv[l], mem_q_norm[l], mem_k_norm[l], w_out[l], rel_bias)
        return x

    y_prompt = trunk(x_prompt, mem_prompt)
    y_sample = trunk(x_sample, mem_sample)
    return (y_prompt, y_sample)
```

```python
import contextlib
import math
import numpy as np
import ml_dtypes
import concourse.bass as bass
import concourse.mybir as mybir
from concourse.bass_utils import run_bass_kernel_spmd

F32 = mybir.dt.float32
BF16 = mybir.dt.bfloat16
I32 = mybir.dt.int32
ALU = mybir.AluOpType
AF = mybir.ActivationFunctionType

NCORES = 8
D = 1024
DIN = 3584
DHY = 384
LP = 8192
LS = 2048
NMEM = 256
TWO_PI = 2.0 * math.pi


class Buf:
    __slots__ = ("name", "w", "r")

    def __init__(self, name=""):
        self.name = name
        self.w = None
        self.r = {}


class T:
    def __init__(self, handle, buf=None):
        self.t = handle
        self.b = buf if buf is not None else Buf(getattr(handle, "name", ""))

    def __getitem__(self, key):
        return self.t[key]


def _b(x):
    return x.b if isinstance(x, T) else x


class Sched:
    NDMA_SEMS = 20

    def __init__(self, nc):
        self.nc = nc
        self.engs = {n: dict(ops=[], count=0, seen={}, pend={}) for n in ("pe", "act", "dve", "pool", "sp")}
        self.sems = {}
        self.dma_pool = {}
        self.dma_rr = {}
        self.final = {}

    def _sem(self, key):
        if key not in self.sems:
            self.sems[key] = self.nc.alloc_semaphore(f"s_{key}")
        return self.sems[key]

    @staticmethod
    def _deps(reads, writes):
        need = {}
        for b in reads:
            b = _b(b)
            if b.w is not None:
                k, v = b.w
                if need.get(k, 0) < v:
                    need[k] = v
        for b in writes:
            b = _b(b)
            if b.w is not None:
                k, v = b.w
                if need.get(k, 0) < v:
                    need[k] = v
            for k, v in b.r.items():
                if need.get(k, 0) < v:
                    need[k] = v
        return need

    def _waits(self, e, need):
        for k, v in e["pend"].items():
            if need.get(k, 0) < v:
                need[k] = v
        e["pend"] = {}
        waits = []
        for k, v in need.items():
            if e["seen"].get(k, 0) >= v:
                continue
            e["seen"][k] = v
            waits.append((k, v))
        return waits

    EPOCH = 4000

    def op(self, eng, emit, reads=(), writes=()):
        e = self.engs[eng]
        need = self._deps(reads, writes)
        if eng == "pe":
            need = {k: v for k, v in need.items() if not k.startswith("pe")}
        waits = self._waits(e, need)
        ep = e["count"] // self.EPOCH
        e["count"] += 1
        idx = e["count"] - ep * self.EPOCH
        key = eng if ep == 0 else f"{eng}{ep}"
        e["cur"] = (key, idx)
        e["ops"].append((waits, emit, (key, 1)))
        for b in reads:
            _b(b).r[key] = idx
        for b in writes:
            b = _b(b)
            b.w = (key, idx)
            b.r = {}
        return idx

    def dma(self, queue, emit, reads=(), writes=(), final=False):
        e = self.engs[queue]
        pool = self.dma_pool.setdefault(queue, [[f"d{queue}{i}", 0] for i in range(self.NDMA_SEMS)])
        i = self.dma_rr.get(queue, 0)
        self.dma_rr[queue] = (i + 1) % self.NDMA_SEMS
        slot = pool[i]
        key = slot[0]
        need = self._deps(reads, writes)
        if slot[1] > 0:
            need[key] = max(need.get(key, 0), slot[1] * 16)
        waits = self._waits(e, need)
        slot[1] += 1
        val = slot[1] * 16
        e["ops"].append((waits, emit, (key, 16)))
        for b in reads:
            _b(b).r[key] = val
        for b in writes:
            b = _b(b)
            b.w = (key, val)
            b.r = {}
        if final:
            self.final[key] = max(self.final.get(key, 0), val)
        return key, val

    def barrier(self):
        state = {}
        for n, e in self.engs.items():
            if e["count"] > 0:
                k, v = e["cur"]
                state[k] = v
        for q, pool in self.dma_pool.items():
            for key, uses in pool:
                if uses > 0:
                    state[key] = uses * 16
        for n, e in self.engs.items():
            for k, v in state.items():
                if e["pend"].get(k, 0) < v:
                    e["pend"][k] = v

    def emit_all(self):
        nc = self.nc
        handles = {"pe": "tensor", "act": "scalar", "dve": "vector", "pool": "gpsimd", "sp": "sync"}
        with nc.Block() as block:
            for name, attr in handles.items():
                ops = self.engs[name]["ops"]
                extra = list(self.final.items()) if name == "sp" else []

                def body(eng, ops=ops, extra=extra):
                    for waits, emit, (skey, inc) in ops:
                        for k, v in waits:
                            eng.wait_ge(self._sem(k), v)
                        inst = emit(eng)
                        inst.then_inc(self._sem(skey), inc)
                    for k, v in extra:
                        eng.wait_ge(self._sem(k), v)

                getattr(block, attr)(body)


def _bf(a):
    return np.ascontiguousarray(np.asarray(a, dtype=np.float32)).astype(ml_dtypes.bfloat16)


def _f32(a):
    return np.ascontiguousarray(np.asarray(a, dtype=np.float32))


def fft_consts(L):
    N = 2 * L
    N1 = N // 128
    N1nz = N1 // 2
    CG = 64 // N1nz
    NK1 = N1 // 2 + 1
    RK = CG * NK1
    k1 = np.arange(NK1, dtype=np.float64)
    n1 = np.arange(N1, dtype=np.float64)
    n2 = np.arange(128, dtype=np.float64)
    th1 = TWO_PI * np.outer(n1, k1) / N1
    m1re = np.zeros((CG * N1, RK)); m1im = np.zeros((CG * N1, RK))
    m1re_nz = np.zeros((64, RK)); m1im_nz = np.zeros((64, RK))
    for c in range(CG):
        m1re[c * N1:(c + 1) * N1, c * NK1:(c + 1) * NK1] = np.cos(th1)
        m1im[c * N1:(c + 1) * N1, c * NK1:(c + 1) * NK1] = -np.sin(th1)
        m1re_nz[c * N1nz:(c + 1) * N1nz, c * NK1:(c + 1) * NK1] = np.cos(th1[:N1nz])
        m1im_nz[c * N1nz:(c + 1) * N1nz, c * NK1:(c + 1) * NK1] = -np.sin(th1[:N1nz])
    M1 = np.concatenate([m1re_nz, m1im_nz], axis=1)
    M1full = np.concatenate([m1re, m1im], axis=1)
    thw = TWO_PI * np.outer(n2, k1) / N
    thw = np.tile(thw, (1, CG))
    twA = np.cos(thw)
    twB = np.stack([-np.sin(thw), np.sin(thw)], axis=1)
    tiA = np.cos(thw).T.copy()
    tiB = np.sin(thw).T.copy()
    cw = np.full(NK1, 2.0); cw[0] = 1.0; cw[-1] = 1.0
    thi = TWO_PI * np.outer(k1, n1[:N1nz]) / N1
    GR = np.zeros((RK, 64)); GI = np.zeros((RK, 64))
    for c in range(CG):
        GR[c * NK1:(c + 1) * NK1, c * N1nz:(c + 1) * N1nz] = (cw[:, None] / N) * np.cos(thi)
        GI[c * NK1:(c + 1) * NK1, c * N1nz:(c + 1) * N1nz] = (cw[:, None] / N) * np.sin(thi)
    G3 = np.stack([GR, -GR, -GI], axis=1)
    return dict(N=N, N1=N1, N1nz=N1nz, CG=CG, NK1=NK1, RK=RK, U=DHY // CG,
                M1=_bf(M1), M1full=_bf(M1full), twA=_f32(twA), twB=_f32(twB),
                tiA=_f32(tiA), tiB=_f32(tiB), G3=_bf(G3))


def f2_consts():
    n = np.arange(128, dtype=np.float64)
    th = TWO_PI * np.outer(n, n) / 128
    C = np.cos(th); S = np.sin(th)
    F2 = np.stack([C, S, -S], axis=1)
    IM = np.stack([np.concatenate([C, S], 1), np.concatenate([-C, -S], 1), np.concatenate([-S, C], 1)], axis=1)
    return _bf(F2), _bf(IM)


def filter_consts(L):
    pos = np.arange(L, dtype=np.float64)
    bands = np.linspace(1e-4, 15.0, 16)

    def feats(p):
        t = p / max(L - 1, 1)
        ang = (TWO_PI / L) * p[:, None] * bands[None, :]
        return np.concatenate([t[:, None], np.cos(ang), -np.sin(ang)], axis=1)
    prev = (L - pos) % L
    ff = feats(pos).T
    fr = feats(prev).T
    deltas = np.abs(np.linspace(math.log(1e-2) / 1.5, math.log(1e-2) / 0.3, DHY))
    t = pos / max(L - 1, 1)
    dec_f = np.exp(-t[None, :] * deltas[:, None])
    dec_r = np.exp(-(prev / max(L - 1, 1))[None, :] * deltas[:, None])
    dec_r[:, 0] = 0.0
    return _f32(np.concatenate([ff, fr], axis=1)), _f32(np.concatenate([dec_f, dec_r], axis=1))


OFFS = (-128, -64, 0, 64, 128)
DILS = (1, 4, 16)


def t5_bucket_np(rel):
    half = 16
    max_exact = 8
    ret = np.where(rel > 0, half, 0)
    n = np.abs(rel)
    large = max_exact + (np.log(np.maximum(n, 1).astype(np.float32) / max_exact)
                         / math.log(1024 / max_exact) * (half - max_exact)).astype(np.int32)
    large = np.minimum(large, half - 1)
    return ret + np.where(n < max_exact, n, large)


def bias_onehot():
    oh = np.zeros((33, 3, 512), dtype=np.float32)
    for ci, dil in enumerate(DILS):
        v = np.arange(512)
        rel = 255 - v
        valid = np.abs(rel) <= 64
        bk = t5_bucket_np(rel * dil)
        for vv in range(512):
            if valid[vv]:
                oh[bk[vv], ci, vv] = 1.0
            else:
                oh[32, ci, vv] = -10000.0
    return oh.reshape(33, 3 * 512)


def build_program(debug=False, phases=(0, 1, 2, 3, 4, 5), only_seq=None, sub2=(1, 2, 3, 4, 5), cfgs=(0, 1, 2), p3=9):
    nc = bass.Bass("TRN2", target_bir_lowering=False)
    S = Sched(nc)
    dbg_outs = []

    def din(name, shape, dt=F32):
        return nc.dram_tensor(name, list(shape), dt, kind="ExternalInput").ap()

    def dscr(name, shape, dt):
        kind = "ExternalOutput" if debug else "Internal"
        if debug:
            dbg_outs.append(name)
        return nc.dram_tensor(name, list(shape), dt, kind=kind).ap()

    seqs = [("p", LP), ("s", LS)]
    run_seqs = [q for q in seqs if only_seq is None or q[0] == only_seq]
    FC = {L: fft_consts(L) for _, L in seqs}

    x_d = {"p": din("x_p", [LP, D]), "s": din("x_s", [LS, D])}
    mem_d = {"p": din("mem_p", [NMEM, D]), "s": din("mem_s", [NMEM, D])}
    y_d = {"p": nc.dram_tensor("y_p", [LP, D], F32, kind="ExternalOutput").ap(),
           "s": nc.dram_tensor("y_s", [LS, D], F32, kind="ExternalOutput").ap()}
    w_in_d = din("w_in", [D, DIN])
    w_out_d = din("w_out", [D, D])
    wkv_d = din("w_mem_kv", [D, 512])
    g_in_d = din("g_in", [128, 8]); g_mem_d = din("g_mem", [128, 8])
    gains_d = din("gains", [128, 4])
    convw_d = din("convw", [128, 9, 3]); convb_d = din("convb", [128, 9]); skip_d = din("skipv", [128, 3])
    fw1_d = din("fw1", [33, 64]); fw2_d = din("fw2", [64, 64]); fw3_d = din("fw3", [64, 768])
    fvec_d = din("fvec", [64, 3])
    relb_d = din("rel_bias", [32, 6])
    ident_d = din("ident", [128, 128], BF16); jmat_d = din("jmat", [128, 128], BF16)
    onesblk_d = din("onesblk", [128, 128], BF16)
    f2_d = din("f2", [128, 3, 128], BF16); im_d = din("imat", [128, 3, 256], BF16)
    oh_d = din("bias_oh", [33, 1536])
    fcd = {}
    for sn, L in seqs:
        c = FC[L]
        RK = c["RK"]
        fcd[sn] = dict(M1=din(f"M1_{sn}", [64, 2 * RK], BF16), M1full=din(f"M1f_{sn}", [128, 2 * RK], BF16),
                       twA=din(f"twA_{sn}", [128, RK]), twB=din(f"twB_{sn}", [128, 2, RK]),
                       tiA=din(f"tiA_{sn}", [RK, 128]), tiB=din(f"tiB_{sn}", [RK, 128]),
                       G3=din(f"G3_{sn}", [RK, 3, 64], BF16),
                       feats=din(f"feats_{sn}", [33, 2 * L]), dec=din(f"dec_{sn}", [DHY, 2 * L]))
    scr = {}
    for sn, L in seqs:
        c = FC[L]
        nb = c["U"] // 6
        scr[sn] = dict(
            zhy=dscr(f"zhy_{sn}", [1152, L], BF16), qT=dscr(f"qT_{sn}", [384, L], BF16),
            kT=dscr(f"kT_{sn}", [384, L], BF16), vtok=dscr(f"vtok_{sn}", [L, 768], BF16),
            mqT=dscr(f"mqT_{sn}", [256, L], BF16), gT=dscr(f"gT_{sn}", [1024, L], BF16),
            sT=dscr(f"sT_{sn}", [384, L], BF16), kfilt=dscr(f"kfilt_{sn}", [384, 2 * L], BF16),
            kspec=dscr(f"kspec_{sn}", [nb, 128, 2, 6 * c["RK"]], F32),
            conv=dscr(f"conv_{sn}", [384, L], F32), mixed=dscr(f"mixed_{sn}", [1024, L], BF16))
    htab_d = dscr("htab", [6, 1536], BF16)
    ebd_d = dscr("ebd", [36, 128, 512], BF16)

    def sb(name, shape, dt):
        return T(nc.alloc_sbuf_tensor(name, list(shape), dt))

    _uid = [0]

    def uniq(name):
        _uid[0] += 1
        return f"{name}_{_uid[0]}"

    w_out_bf = sb("w_out_bf", [128, 8, D], BF16)
    ident = sb("ident_sb", [128, 128], BF16); jmat = sb("jmat_sb", [128, 128], BF16)
    onesblk = sb("onesblk_sb", [128, 128], BF16)
    ones_bf = sb("ones_bf", [128, 64], BF16)
    g_in = sb("g_in_sb", [128, 8], F32); g_mem = sb("g_mem_sb", [128, 8], F32)
    gains = sb("gains_sb", [128, 4], F32)
    convw = sb("convw_sb", [128, 9, 3], F32); convb = sb("convb_sb", [128, 9], F32); skipv = sb("skip_sb", [128, 3], F32)
    cst = sb("cst_sb", [128, 4], F32)
    fw1 = sb("fw1_sb", [33, 64], F32); fw2 = sb("fw2_sb", [64, 64], F32); fw3 = sb("fw3_sb", [64, 768], BF16)
    fvec = sb("fvec_sb", [64, 3], F32); fab = sb("fab_sb", [64, 3], F32)
    kmT = {sn: sb(f"kmT_{sn}", [128, 2, NMEM], BF16) for sn, _ in seqs}
    vm = {sn: sb(f"vm_{sn}", [128, 2, 256], BF16) for sn, _ in seqs}
    f2m = sb("f2_sb", [128, 3, 128], BF16); imat = sb("im_sb", [128, 3, 256], BF16)
    pb = [T(nc.alloc_psum_tensor(f"pb{i}", [128, 512], F32)) for i in range(8)]
    pb16 = [T(p.t.bitcast(BF16), p.b) for p in pb]
    es01 = contextlib.ExitStack()
    w_in_bf = T(es01.enter_context(nc.sbuf_tensor("w_in_bf", [128, 8, DIN], BF16)))
    wkv_bf = T(es01.enter_context(nc.sbuf_tensor("wkv_bf", [128, 8, 512], BF16)))

    LD = "sp"
    ST = "pool"

    def ld(out, in_, reads=(), writes=(), q=LD):
        S.dma(q, lambda e: e.dma_start(out=out, in_=in_), reads=reads, writes=writes)

    def st(out, in_, reads=(), writes=(), final=False, q=ST, slow=False):
        if slow:
            S.dma(q, lambda e: e.dma_start(out=out, in_=in_, allow_slow_non_contiguous=True), reads=reads, writes=writes, final=final)
        else:
            S.dma(q, lambda e: e.dma_start(out=out, in_=in_), reads=reads, writes=writes, final=final)

    def act(out, in_, func, reads, writes, bias=None, scale=None, accum_out=None):
        kw = {}
        if bias is not None:
            kw["bias"] = bias
        if scale is not None:
            kw["scale"] = scale
        if accum_out is not None:
            kw["accum_out"] = accum_out
        S.op("act", lambda e: e.activation(out=out, in_=in_, func=func, **kw), reads=reads, writes=writes)

    def tsc(eng, out, in0, s1, s2, op0, op1, reads, writes):
        if s2 is None:
            S.op(eng, lambda e: e.tensor_scalar(out=out, in0=in0, scalar1=s1, scalar2=None, op0=op0), reads=reads, writes=writes)
        else:
            S.op(eng, lambda e: e.tensor_scalar(out=out, in0=in0, scalar1=s1, scalar2=s2, op0=op0, op1=op1), reads=reads, writes=writes)

    def tt(eng, out, in0, in1, op, reads, writes):
        S.op(eng, lambda e: e.tensor_tensor(out=out, in0=in0, in1=in1, op=op), reads=reads, writes=writes)

    def stt(eng, out, in0, scalar, in1, op0, op1, reads, writes):
        S.op(eng, lambda e: e.scalar_tensor_tensor(out=out, in0=in0, scalar=scalar, in1=in1, op0=op0, op1=op1),
             reads=reads, writes=writes)

    def recip(eng, out, in_, reads, writes):
        S.op(eng, lambda e: e.reciprocal(out=out, in_=in_), reads=reads, writes=writes)

    def cp(eng, out, in_, reads, writes):
        S.op(eng, lambda e: e.tensor_copy(out=out, in_=in_), reads=reads, writes=writes)

    def mset(eng, ap, val, writes):
        S.op(eng, lambda e: e.memset(ap, val), writes=writes)

    def rsum(eng, out, in_, reads, writes):
        S.op(eng, lambda e: e.reduce_sum(out=out, in_=in_, axis=mybir.AxisListType.X), reads=reads, writes=writes)

    def mm(out, lhsT, rhs, start, stop, reads, writes):
        def emit(e):
            try:
                return e.matmul(out=out, lhsT=lhsT, rhs=rhs, start=start, stop=stop, skip_group_check=True)
            except Exception:
                print("MATMUL FAIL out", out, "\nlhsT", lhsT, "\nrhs", rhs)
                raise
        S.op("pe", emit, reads=reads, writes=writes)

    def tr(out, in_, reads, writes):
        S.op("pe", lambda e: e.transpose(out=out, in_=in_, identity=ident[:, :]), reads=list(reads) + [ident], writes=writes)

    EBVAR = {"int": (1, 3), "first": (2, 4), "last": (0, 2), "only": (2, 2)}
    EBIDX = {}
    for ci_ in range(3):
        for vn_ in EBVAR:
            for pr_ in range(3):
                EBIDX[(ci_, vn_, pr_)] = len(EBIDX)

    eb_built = [False]

    def build_eb_tiles(tmp):
        items = [(ci, vn, offs, pr) for ci in range(3) for vn, offs in EBVAR.items() for pr in range(3)]
        hm = {}
        for ci in range(3):
            for hd in range(6):
                t = tmp(f"hm{ci}{hd}", [128, 384], BF16)
                hm[(ci, hd)] = t
                ld(t[:, :], bass.AP(htab_d.tensor, hd * 1536 + ci * 512, [[1, 128], [1, 384]]), writes=[t])
        ebs = [tmp(f"ebs{i}", [128, 512], BF16) for i in range(4)]

        def slot(k):
            ci, vn, offs, pr = items[k]
            nk = 1 if vn == "only" else 2
            W2 = 128 * 2 * nk
            pz = pb[5 + k % 3]
            t = ebs[k % 4]
            for ab in range(2):
                for kt in range(nk):
                    sft = 128 - OFFS[offs[kt]]
                    bi = nk * ab + kt
                    h = hm[(ci, 2 * pr + ab)]
                    mm(pz[:, 128 * bi:128 * bi + 128], jmat[:, :], h[:, sft:sft + 128], True, True, [jmat, h], [pz])
            if W2 < 512:
                mset("pool", t[:, W2:512], 0.0, [t])
            act(t[:, 0:W2], pz[:, 0:W2], AF.Exp, [pz], [t])
            st(ebd_d[EBIDX[(ci, vn, pr)]], t[:, :], reads=[t])
        return [(lambda k=k: slot(k)) for k in range(len(items))]

    with contextlib.ExitStack() as es:
        def tmp(name, shape, dt):
            return T(es.enter_context(nc.sbuf_tensor(uniq(name), list(shape), dt)))

        for dst, src in ((ident, ident_d), (jmat, jmat_d), (onesblk, onesblk_d), (g_in, g_in_d), (g_mem, g_mem_d),
                         (gains, gains_d), (convb, convb_d), (skipv, skip_d), (fw1, fw1_d), (fw2, fw2_d), (fvec, fvec_d)):
            ld(dst[:, :], src[:, :], writes=[dst])
        ld(convw[:, :, :], convw_d[:, :, :], writes=[convw])
        ld(f2m[:, :, :], f2_d[:, :, :], writes=[f2m])
        ld(imat[:, :, :], im_d[:, :, :], writes=[imat])
        mset("pool", ones_bf[:, :], 1.0, [ones_bf])
        mset("pool", cst[:, 0:1], 1e-6, [cst])
        mset("pool", cst[:, 1:2], -math.pi, [cst])
        mset("pool", cst[:, 2:3], 1e-12, [cst])
        mset("pool", cst[:, 3:4], 0.0, [cst])
        tsc("dve", gains[:, 0:1], gains[:, 0:1], 0.125, None, ALU.mult, None, [gains], [gains])
        tsc("dve", gains[:, 2:3], gains[:, 2:3], 0.125, None, ALU.mult, None, [gains], [gains])
        tsc("dve", fab[:, 0:1], fvec[:, 1:2], 1.0 / TWO_PI, None, ALU.mult, None, [fvec], [fab])
        tt("dve", fab[:, 1:2], fvec[:, 0:1], fab[:, 0:1], ALU.mult, [fvec, fab], [fab])
        tt("dve", fab[:, 2:3], fvec[:, 2:3], fab[:, 0:1], ALU.mult, [fvec, fab], [fab])
        w3st = tmp("w3st", [64, 768], F32)
        ld(w3st[:, :], fw3_d[:, :], writes=[w3st])
        cp("dve", fw3[:, :], w3st[:, :], [w3st], [fw3])
        wst = [tmp(f"wst{i}", [128, DIN], F32) for i in range(2)]
        n = 0
        for k in range(8):
            w = wst[n % 2]; n += 1
            ld(w[:, :], w_in_d[128 * k:128 * k + 128, :], writes=[w])
            if k % 2 == 0:
                tsc("dve", w_in_bf[:, k, :], w[:, :], g_in[:, k:k + 1], None, ALU.mult, None, [w, g_in], [w_in_bf])
            else:
                act(w_in_bf[:, k, :], w[:, :], AF.Copy, [w, g_in], [w_in_bf], scale=g_in[:, k:k + 1])
        for k in range(8):
            w = wst[n % 2]; n += 1
            ld(w[:, 0:D], w_out_d[128 * k:128 * k + 128, :], writes=[w])
            ld(w[:, D:D + 512], wkv_d[128 * k:128 * k + 128, :], writes=[w])
            cp("dve", w_out_bf[:, k, :], w[:, 0:D], [w], [w_out_bf])
            act(wkv_bf[:, k, :], w[:, D:D + 512], AF.Copy, [w, g_mem], [wkv_bf], scale=g_mem[:, k:k + 1])
        relb = tmp("relb", [33, 6], F32)
        ohs = tmp("ohs", [33, 1536], F32)
        hts = tmp("hts", [6, 1536], BF16)
        mset("pool", relb[:, :], 1.0, [relb])
        ld(relb[0:32, :], relb_d[:, :], writes=[relb])
        ld(ohs[:, :], oh_d[:, :], writes=[ohs])
        for j in range(3):
            mm(pb[j % 2][0:6, 0:512], relb[:, :], ohs[:, 512 * j:512 * j + 512], True, True, [relb, ohs], [pb[j % 2]])
            act(hts[:, 512 * j:512 * j + 512], pb[j % 2][0:6, 0:512], AF.Copy, [pb[j % 2]], [hts])
        st(htab_d[:, :], hts[:, :], reads=[hts])
        S.barrier()

    if 1 in phases:
        with contextlib.ExitStack() as es:
            def tmp(name, shape, dt):
                return T(es.enter_context(nc.sbuf_tensor(uniq(name), list(shape), dt)))

            xin = [tmp(f"xin{i}", [128, D], F32) for i in range(2)]
            ssq = [tmp(f"ssq{i}", [128, 1], F32) for i in range(3)]
            xs = [tmp(f"xs{i}", [128, D], BF16) for i in range(4)]
            xT = [tmp(f"xT{i}", [128, 8, 512], BF16) for i in range(2)]
            Zb = [tmp(f"Zb{j}", [128, 514], F32) for j in range(9)]
            u1b = [tmp(f"u1b{i}", [128, 512], F32) for i in range(3)]
            uvb = [tmp(f"uvb{i}", [128, 512], F32) for i in range(2)]
            x0_st = [tmp(f"x0st{i}", [128, 3, 512], BF16) for i in range(2)]
            s_st = [tmp(f"sst{i}", [128, 3, 512], BF16) for i in range(2)]
            ulast = tmp("ulast", [128, 9], F32)
            lst = tmp("lst", [128, 6, 1], BF16)
            qk_st = [tmp(f"qkst{i}", [128, 6, 512], BF16) for i in range(2)]
            mq_st = [tmp(f"mqst{i}", [128, 2, 512], BF16) for i in range(2)]
            g_st = [tmp(f"gst{i}", [128, 8, 512], BF16) for i in range(2)]
            v_st = [tmp(f"vst{i}", [128, 4, 768], BF16) for i in range(1)]
            for v_ in v_st:
                mset("pool", v_[:, :, :], 1.0, [v_])
            sqb = [tmp(f"sqb{i}", [128, 512], BF16) for i in range(2)]
            rrb = [tmp(f"rrb{i}", [128, 512], F32) for i in range(2)]
            cnt = dict(x=0, pz=0, hn=0, pt=0, uv=0)

            def prep_a(src_rows, slot):
                i = cnt["x"]; cnt["x"] += 1
                xi = xin[i % 2]; sq = ssq[i % 3]; xsb = xs[slot]
                ld(xi[:, :], src_rows, writes=[xi])
                mset("pool", sq[:, :], 0.0, [sq])
                act(xsb[:, :], xi[:, :], AF.Square, [xi], [xsb, sq], accum_out=sq[:, :])
                act(sq[:, :], sq[:, :], AF.Ln, [sq, cst], [sq], bias=cst[:, 0:1], scale=1.0 / D)
                act(sq[:, :], sq[:, :], AF.Exp, [sq], [sq], scale=-0.5)
                tsc("dve", xsb[:, :], xi[:, :], sq[:, 0:1], None, ALU.mult, None, [xi, sq], [xsb])

            def prep_b(slot, xT_t, col0):
                xsb = xs[slot]
                p = cnt["pt"] % 2; cnt["pt"] += 1
                for k in range(8):
                    tr(pb16[p][:, 128 * k:128 * k + 128], xsb[:, 128 * k:128 * k + 128], [xsb], [pb16[p]])
                act(xT_t[:, :, col0:col0 + 128], pb16[p][:, :].rearrange("p (k t) -> p k t", k=8), AF.Copy, [pb16[p]], [xT_t])

            def prep_tile(src_rows, xT_t, col0, ncols_total):
                prep_a(src_rows, cnt["x"] % 4)
                prep_b((cnt["x"] - 1) % 4, xT_t, col0)

            pending = []

            def headnorm(pz, gcol, out_ap, ncols, out_t):
                h = cnt["hn"] % 2; cnt["hn"] += 1
                sq = sqb[h]; rr = rrb[h]; ph = pb[6 + h]
                act(sq[:, 0:ncols], pz[:, 0:ncols], AF.Square, [pz], [sq])

                def part_b():
                    mm(ph[:, 0:ncols], onesblk[:, :], sq[:, 0:ncols], True, True, [onesblk, sq], [ph])
                    act(rr[:, 0:ncols], ph[:, 0:ncols], AF.Ln, [ph, cst], [rr], bias=cst[:, 0:1], scale=1.0 / 64)
                    act(rr[:, 0:ncols], rr[:, 0:ncols], AF.Exp, [rr], [rr], scale=-0.5)
                    stt("dve", out_ap, pz[:, 0:ncols], gains[:, gcol:gcol + 1], rr[:, 0:ncols], ALU.mult, ALU.mult,
                        [pz, gains, rr], [out_t])
                pending.append(part_b)

            def flush_pending():
                while pending:
                    pending.pop(0)()

            def next_pz():
                p = pb[2 + cnt["pz"] % 4]; cnt["pz"] += 1
                return p

            for sn, L in run_seqs:
                sc = scr[sn]
                mT = xT[0]
                for i in range(2):
                    prep_tile(mem_d[sn][128 * i:128 * i + 128, :], mT, 128 * i, 256)
                for j in range(2):
                    pz = next_pz()
                    for k in range(8):
                        mm(pz[:, 0:256], wkv_bf[:, k, 128 * j:128 * j + 128], mT[:, k, 0:256], k == 0, k == 7, [wkv_bf, mT], [pz])
                    headnorm(pz, 3, kmT[sn][:, j, :], 256, kmT[sn])
                    flush_pending()
                for i in range(2):
                    pz = next_pz()
                    for k in range(8):
                        mm(pz[:, 0:256], mT[:, k, 128 * i:128 * i + 128], wkv_bf[:, k, 256:512], k == 0, k == 7, [wkv_bf, mT], [pz])
                    act(vm[sn][:, i, :], pz[:, 0:256], AF.Copy, [pz], [vm[sn]])
                nch = L // 512

                def prep_a_tile(c, i):
                    r0 = 512 * c + 128 * i
                    prep_a(x_d[sn][r0:r0 + 128, :], i)

                def prep_b_chunk(c):
                    for i in range(4):
                        prep_b(i, xT[(c + 1) % 2], 128 * i)

                def main_chunk(c):
                    xt = xT[(c + 1) % 2]
                    c0 = 512 * c
                    x0s = x0_st[c % 2]; sst = s_st[c % 2]
                    qs = qk_st[c % 2]; ms = mq_st[c % 2]; gs = g_st[c % 2]; vs = v_st[0]
                    order = [9, 0, 10, 1, 11, 2, 12, 3, 13, 4, 14, 5, 18, 6, 19, 7, 8] + list(range(20, 28))
                    for jn, j in enumerate(order):
                        if jn in (3, 9, 15, 21) and c + 2 < nch:
                            prep_a_tile(c + 2, (jn - 3) // 6)
                        pz = next_pz()
                        for k in range(8):
                            mm(pz[:, :], w_in_bf[:, k, 128 * j:128 * j + 128], xt[:, k, :], k == 0, k == 7, [w_in_bf, xt], [pz])
                        flush_pending()
                        if j < 9:
                            zb = Zb[j]
                            act(zb[:, 2:514], pz[:, :], AF.Copy, [pz], [zb])
                            if j < 3:
                                u = uvb[cnt["uv"] % 2]; cnt["uv"] += 1
                            elif j < 6:
                                u = u1b[j - 3]
                            else:
                                u = uvb[cnt["uv"] % 2]; cnt["uv"] += 1
                            if j % 3 != 0:
                                tsc("dve", u[:, :], zb[:, 1:513], convw[:, j, 1:2], convb[:, j:j + 1], ALU.mult, ALU.add, [zb, convw, convb], [u])
                            else:
                                act(u[:, :], zb[:, 1:513], AF.Identity, [zb, convw, convb], [u], bias=convb[:, j:j + 1], scale=convw[:, j, 1:2])
                            stt("dve", u[:, :], zb[:, 0:512], convw[:, j, 0:1], u[:, :], ALU.mult, ALU.add, [zb, convw, u], [u])
                            if j < 3:
                                stt("dve", x0s[:, j, :], zb[:, 2:514], convw[:, j, 2:3], u[:, :], ALU.mult, ALU.add, [zb, convw, u], [x0s])
                            else:
                                stt("dve", u[:, :], zb[:, 2:514], convw[:, j, 2:3], u[:, :], ALU.mult, ALU.add, [zb, convw, u], [u])
                            if j >= 6:
                                tt("pool", sst[:, j - 6, :], u1b[j - 6][:, :], u[:, :], ALU.mult, [u1b[j - 6], u], [sst])
                            cp("dve", zb[:, 0:2], zb[:, 512:514], [zb], [zb])
                        elif j < 12:
                            headnorm(pz, 0, qs[:, j - 9, :], 512, qs)
                        elif j < 15:
                            headnorm(pz, 1, qs[:, j - 9, :], 512, qs)
                        elif j < 20:
                            headnorm(pz, 2, ms[:, j - 18, :], 512, ms)
                        else:
                            act(gs[:, j - 20, :], pz[:, :], AF.Silu, [pz], [gs])
                    flush_pending()
                    for i in range(4):
                        pz = next_pz()
                        for k in range(8):
                            mm(pz[:, 0:384], xt[:, k, 128 * i:128 * i + 128], w_in_bf[:, k, 1920:2304], k == 0, k == 7, [w_in_bf, xt], [pz])
                        vo = bass.AP(vs.t, i * 768, [[4 * 768, 128], [256, 3], [192, 2], [1, 64]])
                        act(vo, pz[:, 0:384].rearrange("p (a b e) -> p a b e", a=3, b=2), AF.Copy, [pz], [vs])
                    if c == 0:
                        st(sc["zhy"][0:384, 0:511].rearrange("(j p) t -> p j t", p=128), x0s[:, :, 1:512], reads=[x0s])
                        st(sc["sT"][:, 0:511].rearrange("(j p) t -> p j t", p=128), sst[:, :, 1:512], reads=[sst])
                    else:
                        st(sc["zhy"][0:384, c0 - 1:c0 + 511].rearrange("(j p) t -> p j t", p=128), x0s[:, :, :], reads=[x0s])
                        st(sc["sT"][:, c0 - 1:c0 + 511].rearrange("(j p) t -> p j t", p=128), sst[:, :, :], reads=[sst])
                    st(sc["qT"][:, c0:c0 + 512].rearrange("(j p) t -> p j t", p=128), qs[:, 0:3, :], reads=[qs])
                    st(sc["kT"][:, c0:c0 + 512].rearrange("(j p) t -> p j t", p=128), qs[:, 3:6, :], reads=[qs])
                    st(sc["mqT"][:, c0:c0 + 512].rearrange("(j p) t -> p j t", p=128), ms[:, :, :], reads=[ms])
                    st(sc["gT"][:, c0:c0 + 512].rearrange("(j p) t -> p j t", p=128), gs[:, :, :], reads=[gs])
                    st(sc["vtok"][c0:c0 + 512, :].rearrange("(i p) e -> p i e", p=128), vs[:, :, :], reads=[vs])

                for zb in Zb:
                    mset("pool", zb[:, 0:2], 0.0, [zb])
                for i in range(4):
                    prep_a_tile(0, i)
                prep_b_chunk(0)
                if nch > 1:
                    for i in range(4):
                        prep_a_tile(1, i)
                for c in range(nch):
                    if c + 1 < nch:
                        prep_b_chunk(c + 1)
                    main_chunk(c)
                for j in range(9):
                    act(ulast[:, j:j + 1], Zb[j][:, 1:2], AF.Identity, [Zb[j], convw, convb], [ulast],
                        bias=convb[:, j:j + 1], scale=convw[:, j, 1:2])
                    stt("dve", ulast[:, j:j + 1], Zb[j][:, 0:1], convw[:, j, 0:1], ulast[:, j:j + 1], ALU.mult, ALU.add,
                        [Zb[j], convw, ulast], [ulast])
                cp("dve", lst[:, 0:3, 0], ulast[:, 0:3], [ulast], [lst])
                tt("dve", lst[:, 3:6, 0], ulast[:, 3:6], ulast[:, 6:9], ALU.mult, [ulast], [lst])
                st(sc["zhy"][0:384, L - 1:L].rearrange("(j p) o -> p j o", p=128), lst[:, 0:3, :], reads=[lst], slow=True)
                st(sc["sT"][:, L - 1:L].rearrange("(j p) o -> p j o", p=128), lst[:, 3:6, :], reads=[lst], slow=True)
            S.barrier()

    es01.close()

    if 2 in phases:
        for sn, L in run_seqs:
            sc = scr[sn]; fc = FC[L]; fd = fcd[sn]
            RK = fc["RK"]; U = fc["U"]; NB = U // 6; N2L = 2 * L
            with contextlib.ExitStack() as es:
                def tmp(name, shape, dt):
                    return T(es.enter_context(nc.sbuf_tensor(uniq(name), list(shape), dt)))
                h2 = tmp("h2", [64, N2L], BF16)
                fts = [tmp(f"fts{i}", [33, 512], F32) for i in range(2)]
                ysb = [tmp(f"ysb{i}", [64, 512], F32) for i in range(2)]
                kib = [tmp(f"kib{i}", [64, 512], I32) for i in range(2)]
                msk = [tmp(f"msk{i}", [64, 512], F32) for i in range(2)]
                h1 = [tmp(f"h1{i}", [64, 512], F32) for i in range(2)]
                kraw = tmp("kraw", [128, N2L], F32)
                dcs = [tmp(f"dcs{i}", [128, 2048], F32) for i in range(3)]
                ndc = [0]
                kst = [tmp(f"kst{i}", [128, 2048], BF16) for i in range(2)]
                nrm = tmp("nrm", [128, 16], F32)
                nch2 = N2L // 512

                ysb2 = [tmp(f"ysc{i}", [64, 512], F32) for i in range(2)]
                kib2 = [tmp(f"kic{i}", [64, 512], I32) for i in range(2)]
                msk2 = [tmp(f"msc{i}", [64, 512], F32) for i in range(2)]

                def sin_ops(pz, bcol, out_ap, out_t, y, ki, m):
                    return [
                        lambda: tsc("dve", y[:, :], pz[0:64, :], fab[:, 0:1], fab[:, bcol:bcol + 1], ALU.mult, ALU.add, [pz, fab], [y]),
                        lambda: cp("dve", ki[:, :], y[:, :], [y], [ki]),
                        lambda: tt("dve", y[:, :], y[:, :], ki[:, :], ALU.subtract, [y, ki], [y]),
                        lambda: act(out_ap, y[:, :], AF.Sin, [y], [out_t], scale=6.28318),
                    ]

                def l1_ops(c):
                    f = fts[c % 2]; pz = pb[c % 2]
                    return [lambda: (ld(f[:, :], fd["feats"][:, 512 * c:512 * c + 512], writes=[f]),
                                     mm(pz[0:64, :], fw1[:, :], f[:, :], True, True, [fw1, f], [pz]))] + \
                        sin_ops(pz, 1, h1[c % 2][:, :], h1[c % 2], ysb[c % 2], kib[c % 2], msk[c % 2])

                def l2_ops(c):
                    pz2 = pb[2 + c % 2]
                    return [lambda: mm(pz2[0:64, :], fw2[:, :], h1[c % 2][:, :], True, True, [fw2, h1[c % 2]], [pz2])] + \
                        sin_ops(pz2, 2, h2[:, 512 * c:512 * c + 512], h2, ysb2[c % 2], kib2[c % 2], msk2[c % 2])

                for op_ in l1_ops(0):
                    op_()
                for c in range(nch2):
                    A = l1_ops(c + 1) if c + 1 < nch2 else []
                    B = l2_ops(c)
                    for i in range(max(len(A), len(B))):
                        if i < len(A):
                            A[i]()
                        if i < len(B):
                            B[i]()
                for ct in range(3):
                    for c in range(nch2):
                        dc = dcs[(ndc[0] + c // 4) % 3]
                        if c % 4 == 0:
                            ld(dc[:, :], fd["dec"][128 * ct:128 * ct + 128, 512 * c:512 * c + 2048], writes=[dc])
                        pz = pb[c % 4]
                        col = 128 * ct if c < nch2 // 2 else 384 + 128 * ct
                        mm(pz[:, :], fw3[:, col:col + 128], h2[:, 512 * c:512 * c + 512], True, True, [fw3, h2], [pz])
                        tt("dve", kraw[:, 512 * c:512 * c + 512], pz[:, :], dc[:, 512 * (c % 4):512 * (c % 4) + 512], ALU.mult, [pz, dc], [kraw])
                    ndc[0] += nch2 // 4
                    pz = pb[6]
                    mm(pz[:, 0:8], fw3[:, 384 + 128 * ct:384 + 128 * ct + 128], h2[:, 0:8], True, True, [fw3, h2], [pz])
                    tt("dve", kraw[:, 0:1], kraw[:, 0:1], pz[:, 0:1], ALU.add, [kraw, pz], [kraw])
                    npc = N2L // 2048
                    mset("pool", nrm[:, :], 0.0, [nrm])
                    for q in range(npc):
                        act(kst[q % 2][:, :], kraw[:, 2048 * q:2048 * q + 2048], AF.Square, [kraw], [kst[q % 2], nrm],
                            accum_out=nrm[:, q:q + 1])
                    rsum("dve", nrm[:, 15:16], nrm[:, 0:npc], [nrm], [nrm])
                    act(nrm[:, 14:15], nrm[:, 15:16], AF.Sqrt, [nrm, cst], [nrm], bias=cst[:, 2:3], scale=1.0)
                    recip("dve", nrm[:, 13:14], nrm[:, 14:15], [nrm], [nrm])
                    for q in range(npc):
                        ks = kst[q % 2]
                        if q % 2 == 0:
                            tsc("dve", ks[:, :], kraw[:, 2048 * q:2048 * q + 2048], nrm[:, 13:14], None, ALU.mult, None, [kraw, nrm], [ks])
                        else:
                            act(ks[:, :], kraw[:, 2048 * q:2048 * q + 2048], AF.Copy, [kraw, nrm], [ks], scale=nrm[:, 13:14])
                        st(sc["kfilt"][128 * ct:128 * ct + 128, 2048 * q:2048 * q + 2048], ks[:, :], reads=[ks])
            S.barrier()
            with contextlib.ExitStack() as es:
                def tmp(name, shape, dt):
                    return T(es.enter_context(nc.sbuf_tensor(uniq(name), list(shape), dt)))
                M1 = tmp("M1", [64, 2 * RK], BF16); M1f = tmp("M1f", [128, 2 * RK], BF16)
                twA = tmp("twA", [128, RK], F32); twB = tmp("twB", [128, 2, RK], F32)
                tiA = tmp("tiA", [RK, 128], F32); tiB = tmp("tiB", [RK, 128], F32)
                G3 = tmp("G3", [RK, 3, 64], BF16)
                for dst, key in ((M1, "M1"), (M1f, "M1full"), (twA, "twA"), (tiA, "tiA"), (tiB, "tiB")):
                    ld(dst[:, :], fd[key][:, :], writes=[dst])
                ld(twB[:, :, :], fd["twB"][:, :, :], writes=[twB])
                ld(G3[:, :, :], fd["G3"][:, :, :], writes=[G3])
                xu = [tmp(f"xu{i}", [128, 6, 128], BF16) for i in range(2)]
                Tb = [tmp(f"Tb{i}", [128, 4, 6, RK], BF16) for i in range(2)]
                ksp = [tmp(f"ksp{i}", [128, 2, 6 * RK], F32) for i in range(2)]
                Pb = [tmp(f"Pb{i}", [128, 4, 6 * RK], BF16) for i in range(2)]
                Qb = [tmp(f"Qb{i}", [RK, 4, 6, 128], BF16) for i in range(2)]
                yst = [tmp(f"yst{i}", [64, 6, 128], F32) for i in range(2)]
                W6 = 6 * RK

                def bc_ap(t, dims):
                    h = t.t
                    return bass.AP(h, 0, [[int(np.prod(h.shape[1:])), int(h.shape[0])]] + [[s, n] for s, n in dims])

                def stage_a(b, src_view, Mmat, npart):
                    x = xu[b % 2]; Tt = Tb[b % 2]
                    ld(x[0:npart, :, :], src_view, writes=[x])
                    for g in range(2):
                        pa = pb[g]
                        for j in range(3):
                            u = 3 * g + j
                            mm(pa[:, 2 * RK * j:2 * RK * (j + 1)], x[0:npart, u, :], Mmat[0:npart, :], True, True, [x, Mmat], [pa])
                        in0 = pa[:, 0:6 * RK].rearrange("p (u r k) -> p u r k", u=3, r=2)
                        o1 = bass.AP(Tt.t, 3 * g * RK, [[4 * 6 * RK, 128], [RK, 3], [3 * 6 * RK, 2], [1, RK]])
                        i1 = bass.AP(twA.t, 0, [[RK, 128], [0, 3], [0, 2], [1, RK]])
                        tt("dve", o1, in0, i1, ALU.mult, [pa, twA], [Tt])
                        o2 = bass.AP(Tt.t, 6 * RK + 3 * g * RK, [[4 * 6 * RK, 128], [RK, 3], [6 * RK, 2], [1, RK]])
                        i2 = bass.AP(twB.t, 0, [[2 * RK, 128], [0, 3], [RK, 2], [1, RK]])
                        tt("dve", o2, in0, i2, ALU.mult, [pa, twB], [Tt])

                def stage_b_mm(b):
                    Tt = Tb[b % 2]
                    xr = pb[2]; xi = pb[3]

                    def blk(i):
                        return Tt[:, i, :, :].rearrange("p u k -> p (u k)")
                    seq = [(0, 0, xr), (0, 2, xr), (1, 1, xr), (1, 3, xr), (0, 1, xi), (0, 3, xi), (2, 0, xi), (2, 2, xi)]
                    started = set()
                    for mi, bi, dst in seq:
                        first = id(dst) not in started
                        started.add(id(dst))
                        mm(dst[:, 0:W6], f2m[:, mi, :], blk(bi), first, False, [f2m, Tt], [dst])
                    return xr, xi

                kf_view = sc["kfilt"].rearrange("c (n1 n2) -> (c n1) n2", n2=128).rearrange("(u p) n2 -> p u n2", p=128)

                def filt_b(b):
                    xr, xi = stage_b_mm(b)
                    ks = ksp[b % 2]
                    act(ks[:, 0, :], xr[:, 0:W6], AF.Copy, [xr], [ks])
                    act(ks[:, 1, :], xi[:, 0:W6], AF.Copy, [xi], [ks])
                    st(sc["kspec"][b], ks[:, :, :], reads=[ks], writes=[ksbuf[b]])

                ksbuf = [Buf(f"kspec{b}") for b in range(NB)]
                for t in range(NB + 1):
                    if t < NB:
                        stage_a(t, kf_view[:, 6 * t:6 * t + 6, :], M1f, 128)
                    if t >= 1:
                        filt_b(t - 1)
                s_view = sc["sT"].rearrange("c (n1 n2) -> (c n1) n2", n2=128).rearrange("(u p) n2 -> p u n2", p=64)
                c_view = sc["conv"].rearrange("c (n1 n2) -> (c n1) n2", n2=128).rearrange("(u p) n2 -> p u n2", p=64)

                def sig_a(b):
                    ks = ksp[b % 2]
                    ld(ks[:, :, :], sc["kspec"][b], reads=[ksbuf[b]], writes=[ks])
                    stage_a(b, s_view[:, 6 * b:6 * b + 6, :], M1, 64)

                def sig_b(b):
                    ks = ksp[b % 2]
                    xr, xi = stage_b_mm(b)
                    Pt = Pb[b % 2]
                    tt("dve", Pt[:, 0, :], xr[:, 0:W6], ks[:, 0, :], ALU.mult, [xr, ks], [Pt])
                    tt("dve", Pt[:, 2, :], xr[:, 0:W6], ks[:, 1, :], ALU.mult, [xr, ks], [Pt])
                    tt("dve", Pt[:, 1, :], xi[:, 0:W6], ks[:, 1, :], ALU.mult, [xi, ks], [Pt])
                    tt("dve", Pt[:, 3, :], xi[:, 0:W6], ks[:, 0, :], ALU.mult, [xi, ks], [Pt])

                def sig_c(b):
                    Pt = Pb[b % 2]; Qt = Qb[b % 2]
                    for g in range(3):
                        pc = pb[4 + g]
                        for j in range(2):
                            u = 2 * g + j
                            for bi, mi in ((0, 0), (1, 1), (2, 2), (3, 2)):
                                mm(pc[0:RK, 256 * j:256 * j + 256], Pt[:, bi, RK * u:RK * u + RK], imat[:, mi, :],
                                   bi == 0, bi == 3, [Pt, imat], [pc])
                        in0 = pc[0:RK, :].rearrange("p (u r n) -> p u r n", u=2, r=2)
                        oA = bass.AP(Qt.t, 2 * g * 128, [[4 * 6 * 128, RK], [128, 2], [6 * 128, 2], [1, 128]])
                        iA = bass.AP(tiA.t, 0, [[128, RK], [0, 2], [0, 2], [1, 128]])
                        tt("dve", oA, in0, iA, ALU.mult, [pc, tiA], [Qt])
                        oB = bass.AP(Qt.t, 2 * 6 * 128 + 2 * g * 128, [[4 * 6 * 128, RK], [128, 2], [6 * 128, 2], [1, 128]])
                        iB = bass.AP(tiB.t, 0, [[128, RK], [0, 2], [0, 2], [1, 128]])
                        tt("dve", oB, in0, iB, ALU.mult, [pc, tiB], [Qt])

                def sig_d(b):
                    Qt = Qb[b % 2]
                    ys = yst[b % 2]
                    py = pb[7]
                    for hlf in range(2):
                        for bi, gi in ((0, 0), (3, 1), (2, 2), (1, 2)):
                            mm(py[0:64, 0:384], G3[:, gi, :], Qt[:, bi, 3 * hlf:3 * hlf + 3, :].rearrange("p u n -> p (u n)"),
                               bi == 0, bi == 1, [G3, Qt], [py])
                        act(ys[:, 3 * hlf:3 * hlf + 3, :], py[0:64, 0:384].rearrange("p (u n) -> p u n", u=3), AF.Copy, [py], [ys])
                    st(c_view[:, 6 * b:6 * b + 6, :], ys[:, :, :], reads=[ys])

                for t in range(NB + 3):
                    if t < NB:
                        sig_a(t)
                    if 0 <= t - 1 < NB:
                        sig_b(t - 1)
                    if 0 <= t - 2 < NB:
                        sig_c(t - 2)
                    if 0 <= t - 3 < NB:
                        sig_d(t - 3)
            S.barrier()
            with contextlib.ExitStack() as es:
                def tmp(name, shape, dt):
                    return T(es.enter_context(nc.sbuf_tensor(uniq(name), list(shape), dt)))
                PC = 2048
                zp = [tmp(f"zq{i}", [128, PC + 2], BF16) for i in range(3)]
                uc = [tmp(f"ue{i}", [128, PC], F32) for i in range(3)]
                sq_ = [tmp(f"sq{i}", [128, PC], BF16) for i in range(3)]
                gq_ = [tmp(f"gq{i}", [128, PC], BF16) for i in range(3)]
                cq_ = [tmp(f"cq{i}", [128, PC], F32) for i in range(3)]
                mq_ = [tmp(f"mq{i}", [128, PC], BF16) for i in range(3)]
                n2e = 0
                for ct in range(3):
                    for piece in range(L // PC):
                        a = PC * piece
                        z = zp[n2e % 3]; u = uc[n2e % 3]; sv = sq_[n2e % 3]; gv = gq_[n2e % 3]; cv = cq_[n2e % 3]; mv = mq_[n2e % 3]
                        n2e += 1
                        ld(z[:, 0:PC], sc["zhy"][128 * ct:128 * ct + 128, a:a + PC], writes=[z])
                        ld(sv[:, :], sc["sT"][128 * ct:128 * ct + 128, a:a + PC], writes=[sv])
                        ld(gv[:, :], sc["gT"][128 * ct:128 * ct + 128, a:a + PC], writes=[gv])
                        ld(cv[:, :], sc["conv"][128 * ct:128 * ct + 128, a:a + PC], writes=[cv])
                        stt("dve", cv[:, :], sv[:, :], skipv[:, ct:ct + 1], cv[:, :], ALU.mult, ALU.add, [sv, skipv, cv], [cv])
                        tt("pool", u[:, :], z[:, 0:PC], cv[:, :], ALU.mult, [z, cv], [u])
                        tt("dve", mv[:, :], u[:, :], gv[:, :], ALU.mult, [u, gv], [mv])
                        st(sc["mixed"][128 * ct:128 * ct + 128, a:a + PC], mv[:, :], reads=[mv])
            S.barrier()

    if 3 in phases:
        with contextlib.ExitStack() as es:
            def tmp(name, shape, dt):
                return T(es.enter_context(nc.sbuf_tensor(uniq(name), list(shape), dt)))
            if not eb_built[0]:
                eb_built[0] = True
                with contextlib.ExitStack() as es2:
                    for sl in build_eb_tiles(lambda name, shape, dt: T(es2.enter_context(nc.sbuf_tensor(uniq(name), list(shape), dt)))):
                        sl()
                    S.barrier()
            eb = {}
            for key, idx in EBIDX.items():
                t = tmp(f"eb{idx}", [128, 4, 128], BF16)
                eb[key] = t
                ld(t[:, :, :].rearrange("p a b -> p (a b)"), ebd_d[idx], writes=[t])
            SR = 2048
            qsb = tmp("qsb", [128, 3, SR], BF16)
            ksb = tmp("ksb", [128, 3, 3 * SR], BF16)
            gsb = tmp("gsb", [128, 3, SR], BF16)
            acc = tmp("acc", [128, 3, 2, SR], F32)
            dal = tmp("dal", [128, SR], F32)
            msb = tmp("msb", [128, SR], BF16)
            NV = 8
            vsb = [tmp(f"vsb{i}", [128, 768], BF16) for i in range(NV)]
            pra = [tmp(f"pra{i}", [128, 512], BF16) for i in range(4)]
            ptb = [tmp(f"ptb{i}", [128, 512], BF16) for i in range(4)]
            vn_ctr = [0]
            it = [0]
            for sn, L in run_seqs:
                sc = scr[sn]
                for sr in range(L // SR):
                    T0 = sr * SR
                    w0 = max(0, T0 - 2048); w1 = min(L, T0 + SR + 2048)
                    ld(qsb[:, :, :], sc["qT"][:, T0:T0 + SR].rearrange("(j p) t -> p j t", p=128), writes=[qsb])
                    ld(ksb[:, :, 0:w1 - w0], sc["kT"][:, w0:w1].rearrange("(j p) t -> p j t", p=128), writes=[ksb])
                    ld(gsb[:, :, :], sc["gT"][384:768, T0:T0 + SR].rearrange("(j p) t -> p j t", p=128), writes=[gsb])
                    jobs = []
                    for ci, dil in enumerate(DILS):
                        n = L // dil
                        for r in range(dil):
                            for jt in range(SR // dil // 128):
                                bq = T0 // dil + 128 * jt
                                if n == 128:
                                    vn, kstart, nkt = "only", 0, 1
                                elif bq == 0:
                                    vn, kstart, nkt = "first", 0, 2
                                elif bq + 128 == n:
                                    vn, kstart, nkt = "last", n - 256, 2
                                else:
                                    vn, kstart, nkt = "int", bq - 64, 2
                                for pr in range(3):
                                    jobs.append((ci, dil, r, bq, vn, kstart, nkt, pr))
                    vcache = {}
                    state = {}

                    def s_stage(j):
                        ci, dil, r, bq, vn, kstart, nkt, pr = jobs[j]
                        vts = []
                        for kt in range(nkt):
                            key = (ci, r, kstart + 128 * kt)
                            if key not in vcache:
                                vt = vsb[vn_ctr[0] % NV]; vn_ctr[0] += 1
                                for kk in [k for k, v in vcache.items() if v is vt]:
                                    del vcache[kk]
                                tok0 = r + dil * (kstart + 128 * kt)
                                src = bass.AP(sc["vtok"].tensor, tok0 * 768, [[dil * 768, 128], [1, 768]])
                                ld(vt[:, :], src, writes=[vt])
                                vcache[key] = vt
                            vts.append(vcache[key])
                        qcol0 = r + dil * (bq - T0 // dil)
                        qsl = slice(qcol0, qcol0 + dil * 127 + 1, dil)
                        i = it[0]; it[0] += 1
                        psab = (pb[2 * (i % 3)], pb[2 * (i % 3) + 1])
                        pr_raw = pra[i % 4]; pt = ptb[i % 4]
                        W = 128 * nkt
                        for kt in range(nkt):
                            for ab in range(2):
                                kc0 = r + dil * (kstart + 128 * kt) - w0
                                assert kc0 >= 0 and kc0 + dil * 127 < w1 - w0
                                ksl = slice(kc0, kc0 + dil * 127 + 1, dil)
                                mm(psab[ab][:, 128 * kt:128 * kt + 128],
                                   ksb[64 * ab:64 * ab + 64, pr, ksl], qsb[64 * ab:64 * ab + 64, pr, qsl],
                                   True, True, [ksb, qsb], [psab[ab]])
                        for ab in range(2):
                            act(pr_raw[:, W * ab:W * ab + W], psab[ab][:, 0:W], AF.Exp, [psab[ab]], [pr_raw])
                        e = eb[(ci, vn, pr)]
                        tt("dve", pt[:, 0:2 * W], pr_raw[:, 0:2 * W], e[:, :, :].rearrange("p a b -> p (a b)")[:, 0:2 * W], ALU.mult,
                           [pr_raw, e], [pt])
                        state[j] = (vts, pt, qcol0, i)

                    def pv_stage(j):
                        ci, dil, r, bq, vn, kstart, nkt, pr = jobs[j]
                        vts, pt, qcol0, i = state.pop(j)
                        pnd = pb[6 + i % 2]
                        first = True
                        for ab in range(2):
                            for kt in range(nkt):
                                bi = nkt * ab + kt
                                hcol = (2 * pr + ab) * 128
                                mm(pnd[:, 128 * ab:128 * ab + 128], vts[kt][:, hcol:hcol + 128], pt[:, 128 * bi:128 * bi + 128],
                                   first, False, [vts[kt], pt], [pnd])
                                first = False
                        av = bass.AP(acc.t, pr * 2 * SR + qcol0, [[3 * 2 * SR, 128], [SR, 2], [dil, 128]])
                        pv2 = pnd[:, 0:256].rearrange("p (a q) -> p a q", a=2)
                        if ci == 0:
                            act(av, pv2, AF.Copy, [pnd], [acc])
                        else:
                            tt("dve", av, pv2, av, ALU.add, [pnd, acc], [acc])

                    s_stage(0)
                    s_stage(1)
                    for j in range(len(jobs)):
                        if j + 2 < len(jobs):
                            s_stage(j + 2)
                        pv_stage(j)
                    for pr in range(3):
                        ld(dal[0:64, :], acc[64:128, pr, 0, :], reads=[acc], writes=[dal])
                        ld(dal[64:128, :], acc[0:64, pr, 1, :], reads=[acc], writes=[dal])
                        act(dal[:, :], dal[:, :], AF.Ln, [dal], [dal])
                        act(dal[:, :], dal[:, :], AF.Exp, [dal], [dal], scale=-1.0)
                        tt("dve", dal[0:64, :], acc[0:64, pr, 0, :], dal[0:64, :], ALU.mult, [acc, dal], [dal])
                        tt("pool", dal[64:128, :], acc[64:128, pr, 1, :], dal[64:128, :], ALU.mult, [acc, dal], [dal])
                        tt("dve", msb[:, :], dal[:, :], gsb[:, pr, :], ALU.mult, [dal, gsb], [msb])
                        st(sc["mixed"][384 + 128 * pr:384 + 128 * pr + 128, T0:T0 + SR], msb[:, :], reads=[msb])
            S.barrier()

    if 4 in phases or 5 in phases:
        with contextlib.ExitStack() as es:
            def tmp(name, shape, dt):
                return T(es.enter_context(nc.sbuf_tensor(uniq(name), list(shape), dt)))
            mqs = [tmp(f"mqs{i}", [128, 2, 512], BF16) for i in range(2)]
            gms = [tmp(f"gms{i}", [128, 2, 512], BF16) for i in range(2)]
            pms = [tmp(f"pms{i}", [128, 4, 512], BF16) for i in range(2)]
            rdn = [tmp(f"rdn{i}", [128, 512], F32) for i in range(2)]
            mx = [tmp(f"mx{i}", [128, 8, 512], BF16) for i in range(2)]
            xr = [tmp(f"xr{i}", [128, D], F32) for i in range(3)]
            yo = [tmp(f"yo{i}", [128, D], F32) for i in range(2)]
            chunks = [(sn, L, c) for sn, L in run_seqs for c in range(L // 512)]
            n5 = [0]

            def p4_s(ci):
                sn, L, c = chunks[ci]
                sc = scr[sn]; c0 = 512 * c
                mq = mqs[ci % 2]; gm = gms[ci % 2]; m = mx[ci % 2]
                ld(mq[:, :, :], sc["mqT"][:, c0:c0 + 512].rearrange("(j p) t -> p j t", p=128), writes=[mq])
                ld(gm[:, :, :], sc["gT"][768:1024, c0:c0 + 512].rearrange("(j p) t -> p j t", p=128), writes=[gm])
                ld(m[:, 0:6, :], sc["mixed"][0:768, c0:c0 + 512].rearrange("(k p) t -> p k t", p=128), writes=[m])

            def p4_qk(ci, pr):
                sn, L, c = chunks[ci]
                mq = mqs[ci % 2]; pm = pms[pr]
                for ab in range(2):
                    for mt in range(2):
                        ps = pb[2 * ab + mt]
                        mm(ps[:, :], kmT[sn][64 * ab:64 * ab + 64, pr, 128 * mt:128 * mt + 128], mq[64 * ab:64 * ab + 64, pr, :],
                           True, True, [kmT[sn], mq], [ps])
                        act(pm[:, 2 * ab + mt, :], ps[:, :], AF.Exp, [ps], [pm])

            def p4_pv(ci, pr):
                sn, L, c = chunks[ci]
                gm = gms[ci % 2]; m = mx[ci % 2]
                pm = pms[pr]; rd = rdn[pr]
                pn = pb[4]; pd = pb[5]
                for ab in range(2):
                    for mt in range(2):
                        mm(pn[64 * ab:64 * ab + 64, :], vm[sn][:, mt, 128 * pr + 64 * ab:128 * pr + 64 * ab + 64], pm[:, 2 * ab + mt, :],
                           mt == 0, mt == 1, [vm[sn], pm], [pn])
                        mm(pd[64 * ab:64 * ab + 64, :], ones_bf[:, :], pm[:, 2 * ab + mt, :], mt == 0, mt == 1, [ones_bf, pm], [pd])
                act(rd[:, :], pd[:, :], AF.Ln, [pd], [rd])
                act(rd[:, :], rd[:, :], AF.Exp, [rd], [rd], scale=-1.0)
                tt("dve", rd[:, :], pn[:, :], rd[:, :], ALU.mult, [pn, rd], [rd])
                tt("pool", m[:, 6 + pr, :], rd[:, :], gm[:, pr, :], ALU.mult, [rd, gm], [m])

            def p5_tile(ci, i):
                sn, L, c = chunks[ci]
                c0 = 512 * c
                m = mx[ci % 2]
                r0 = c0 + 128 * i
                x = xr[n5[0] % 3]; y = yo[n5[0] % 2]
                ld(x[:, :], x_d[sn][r0:r0 + 128, :], writes=[x])
                for hf in range(2):
                    pz = pb[6 + hf]
                    for k in range(8):
                        mm(pz[:, :], m[:, k, 128 * i:128 * i + 128], w_out_bf[:, k, 512 * hf:512 * hf + 512], k == 0, k == 7,
                           [m, w_out_bf], [pz])
                    tt("dve", y[:, 512 * hf:512 * hf + 512], pz[:, :], x[:, 512 * hf:512 * hf + 512], ALU.add, [pz, x], [y])
                st(y_d[sn][r0:r0 + 128, :], y[:, :], reads=[y], final=True)
                n5[0] += 1

            p4_s(0)
            for pr in range(2):
                p4_qk(0, pr)
                p4_pv(0, pr)
            for ci in range(len(chunks)):
                nxt = ci + 1 < len(chunks)
                if nxt:
                    p4_s(ci + 1)
                    p4_qk(ci + 1, 0)
                p5_tile(ci, 0)
                p5_tile(ci, 1)
                if nxt:
                    p4_pv(ci + 1, 0)
                    p4_qk(ci + 1, 1)
                p5_tile(ci, 2)
                p5_tile(ci, 3)
                if nxt:
                    p4_pv(ci + 1, 1)

    S.emit_all()
    return nc, dbg_outs


def make_in_maps(inp):
    f2, im = f2_consts()
    shared = {}
    shared["w_in"] = _f32(inp["w_in"][0]); shared["w_out"] = _f32(inp["w_out"][0]); shared["w_mem_kv"] = _f32(inp["w_mem_kv"][0])
    shared["g_in"] = _f32(np.asarray(inp["norm_in"][0]).reshape(8, 128).T)
    shared["g_mem"] = _f32(np.asarray(inp["mem_norm"][0]).reshape(8, 128).T)
    gn = np.stack([np.tile(np.asarray(inp[k][0]), 2) for k in ("att_q_norm", "att_k_norm", "mem_q_norm", "mem_k_norm")], axis=1)
    shared["gains"] = _f32(gn)
    cw = np.asarray(inp["hy_conv_w"][0])
    shared["convw"] = _f32(cw.T.reshape(9, 128, 3).transpose(1, 0, 2))
    shared["convb"] = _f32(np.asarray(inp["hy_conv_b"][0]).reshape(9, 128).T)
    shared["skipv"] = _f32(np.asarray(inp["hy_skip"][0]).reshape(3, 128).T)
    shared["fw1"] = _f32(inp["hy_filt_w1"][0]); shared["fw2"] = _f32(inp["hy_filt_w2"][0]); shared["fw3"] = _f32(inp["hy_filt_w3"][0])
    shared["fvec"] = _f32(np.stack([np.asarray(inp["hy_filt_b1"][0]), np.asarray(inp["hy_filt_freq"][0]),
                                    np.asarray(inp["hy_filt_b2"][0])], axis=1))
    shared["rel_bias"] = _f32(inp["rel_bias"])
    shared["ident"] = _bf(np.eye(128)); shared["jmat"] = _bf(np.eye(128)[::-1])
    ob = np.zeros((128, 128)); ob[:64, :64] = 1; ob[64:, 64:] = 1
    shared["onesblk"] = _bf(ob)
    shared["f2"] = f2; shared["imat"] = im
    shared["bias_oh"] = _f32(bias_onehot())
    for sn, L in (("p", LP), ("s", LS)):
        c = fft_consts(L)
        shared[f"M1_{sn}"] = c["M1"]; shared[f"M1f_{sn}"] = c["M1full"]
        shared[f"twA_{sn}"] = c["twA"]; shared[f"twB_{sn}"] = c["twB"]
        shared[f"tiA_{sn}"] = c["tiA"]; shared[f"tiB_{sn}"] = c["tiB"]; shared[f"G3_{sn}"] = c["G3"]
        ft, dec = filter_consts(L)
        shared[f"feats_{sn}"] = ft; shared[f"dec_{sn}"] = dec
    maps = []
    for i in range(NCORES):
        m = dict(shared)
        m["x_p"] = _f32(inp["x_prompt"][i]); m["x_s"] = _f32(inp["x_sample"][i])
        m["mem_p"] = _f32(inp["mem_prompt"][i]); m["mem_s"] = _f32(inp["mem_sample"][i])
        maps.append(m)
    return maps


_CACHE = {}


def kernel(**inputs):
    inp = {k: np.asarray(v) for k, v in inputs.items()}
    maps = make_in_maps(inp)
    if "nc" not in _CACHE:
        _CACHE["nc"] = build_program()[0]
    res = run_bass_kernel_spmd(_CACHE["nc"], maps, core_ids=list(range(NCORES)))
    y_p = np.stack([np.asarray(res.results[i]["y_p"], dtype=np.float32) for i in range(NCORES)], axis=0)
    y_s = np.stack([np.asarray(res.results[i]["y_s"], dtype=np.float32) for i in range(NCORES)], axis=0)
    return (y_p, y_s)
```

```python
import contextlib
import math
import numpy as np
import ml_dtypes
import concourse.bass as bass
import concourse.mybir as mybir
from concourse.bass_utils import run_bass_kernel_spmd

F32 = mybir.dt.float32
BF16 = mybir.dt.bfloat16
I32 = mybir.dt.int32
ALU = mybir.AluOpType
AF = mybir.ActivationFunctionType

NCORES = 8
D = 1024
DIN = 3584
DHY = 384
LP = 8192
LS = 2048
NMEM = 256
TWO_PI = 2.0 * math.pi


class Buf:
    __slots__ = ("name", "w", "r")

    def __init__(self, name=""):
        self.name = name
        self.w = None
        self.r = {}


class T:
    def __init__(self, handle, buf=None):
        self.t = handle
        self.b = buf if buf is not None else Buf(getattr(handle, "name", ""))

    def __getitem__(self, key):
        return self.t[key]


def _b(x):
    return x.b if isinstance(x, T) else x


class Sched:
    NDMA_SEMS = 20

    def __init__(self, nc):
        self.nc = nc
        self.engs = {n: dict(ops=[], count=0, seen={}, pend={}) for n in ("pe", "act", "dve", "pool", "sp")}
        self.sems = {}
        self.dma_pool = {}
        self.dma_rr = {}
        self.final = {}

    def _sem(self, key):
        if key not in self.sems:
            self.sems[key] = self.nc.alloc_semaphore(f"s_{key}")
        return self.sems[key]

    @staticmethod
    def _deps(reads, writes):
        need = {}
        for b in reads:
            b = _b(b)
            if b.w is not None:
                k, v = b.w
                if need.get(k, 0) < v:
                    need[k] = v
        for b in writes:
            b = _b(b)
            if b.w is not None:
                k, v = b.w
                if need.get(k, 0) < v:
                    need[k] = v
            for k, v in b.r.items():
                if need.get(k, 0) < v:
                    need[k] = v
        return need

    def _waits(self, e, need):
        for k, v in e["pend"].items():
            if need.get(k, 0) < v:
                need[k] = v
        e["pend"] = {}
        waits = []
        for k, v in need.items():
            if e["seen"].get(k, 0) >= v:
                continue
            e["seen"][k] = v
            waits.append((k, v))
        return waits

    EPOCH = 4000

    def op(self, eng, emit, reads=(), writes=()):
        e = self.engs[eng]
        need = self._deps(reads, writes)
        if eng == "pe":
            need = {k: v for k, v in need.items() if not k.startswith("pe")}
        waits = self._waits(e, need)
        ep = e["count"] // self.EPOCH
        e["count"] += 1
        idx = e["count"] - ep * self.EPOCH
        key = eng if ep == 0 else f"{eng}{ep}"
        e["cur"] = (key, idx)
        e["ops"].append((waits, emit, (key, 1)))
        for b in reads:
            _b(b).r[key] = idx
        for b in writes:
            b = _b(b)
            b.w = (key, idx)
            b.r = {}
        return idx

    def dma(self, queue, emit, reads=(), writes=(), final=False):
        e = self.engs[queue]
        pool = self.dma_pool.setdefault(queue, [[f"d{queue}{i}", 0] for i in range(self.NDMA_SEMS)])
        i = self.dma_rr.get(queue, 0)
        self.dma_rr[queue] = (i + 1) % self.NDMA_SEMS
        slot = pool[i]
        key = slot[0]
        need = self._deps(reads, writes)
        if slot[1] > 0:
            need[key] = max(need.get(key, 0), slot[1] * 16)
        waits = self._waits(e, need)
        slot[1] += 1
        val = slot[1] * 16
        e["ops"].append((waits, emit, (key, 16)))
        for b in reads:
            _b(b).r[key] = val
        for b in writes:
            b = _b(b)
            b.w = (key, val)
            b.r = {}
        if final:
            self.final[key] = max(self.final.get(key, 0), val)
        return key, val

    def barrier(self):
        state = {}
        for n, e in self.engs.items():
            if e["count"] > 0:
                k, v = e["cur"]
                state[k] = v
        for q, pool in self.dma_pool.items():
            for key, uses in pool:
                if uses > 0:
                    state[key] = uses * 16
        for n, e in self.engs.items():
            for k, v in state.items():
                if e["pend"].get(k, 0) < v:
                    e["pend"][k] = v

    def emit_all(self):
        nc = self.nc
        handles = {"pe": "tensor", "act": "scalar", "dve": "vector", "pool": "gpsimd", "sp": "sync"}
        with nc.Block() as block:
            for name, attr in handles.items():
                ops = self.engs[name]["ops"]
                extra = list(self.final.items()) if name == "sp" else []

                def body(eng, ops=ops, extra=extra):
                    for waits, emit, (skey, inc) in ops:
                        for k, v in waits:
                            eng.wait_ge(self._sem(k), v)
                        inst = emit(eng)
                        inst.then_inc(self._sem(skey), inc)
                    for k, v in extra:
                        eng.wait_ge(self._sem(k), v)

                getattr(block, attr)(body)


def _bf(a):
    return np.ascontiguousarray(np.asarray(a, dtype=np.float32)).astype(ml_dtypes.bfloat16)


def _f32(a):
    return np.ascontiguousarray(np.asarray(a, dtype=np.float32))


def fft_consts(L):
    N = 2 * L
    N1 = N // 128
    N1nz = N1 // 2
    CG = 64 // N1nz
    NK1 = N1 // 2 + 1
    RK = CG * NK1
    k1 = np.arange(NK1, dtype=np.float64)
    n1 = np.arange(N1, dtype=np.float64)
    n2 = np.arange(128, dtype=np.float64)
    th1 = TWO_PI * np.outer(n1, k1) / N1
    m1re = np.zeros((CG * N1, RK)); m1im = np.zeros((CG * N1, RK))
    m1re_nz = np.zeros((64, RK)); m1im_nz = np.zeros((64, RK))
    for c in range(CG):
        m1re[c * N1:(c + 1) * N1, c * NK1:(c + 1) * NK1] = np.cos(th1)
        m1im[c * N1:(c + 1) * N1, c * NK1:(c + 1) * NK1] = -np.sin(th1)
        m1re_nz[c * N1nz:(c + 1) * N1nz, c * NK1:(c + 1) * NK1] = np.cos(th1[:N1nz])
        m1im_nz[c * N1nz:(c + 1) * N1nz, c * NK1:(c + 1) * NK1] = -np.sin(th1[:N1nz])
    M1 = np.concatenate([m1re_nz, m1im_nz], axis=1)
    M1full = np.concatenate([m1re, m1im], axis=1)
    thw = TWO_PI * np.outer(n2, k1) / N
    thw = np.tile(thw, (1, CG))
    twA = np.cos(thw)
    twB = np.stack([-np.sin(thw), np.sin(thw)], axis=1)
    tiA = np.cos(thw).T.copy()
    tiB = np.sin(thw).T.copy()
    cw = np.full(NK1, 2.0); cw[0] = 1.0; cw[-1] = 1.0
    thi = TWO_PI * np.outer(k1, n1[:N1nz]) / N1
    GR = np.zeros((RK, 64)); GI = np.zeros((RK, 64))
    for c in range(CG):
        GR[c * NK1:(c + 1) * NK1, c * N1nz:(c + 1) * N1nz] = (cw[:, None] / N) * np.cos(thi)
        GI[c * NK1:(c + 1) * NK1, c * N1nz:(c + 1) * N1nz] = (cw[:, None] / N) * np.sin(thi)
    G3 = np.stack([GR, -GR, -GI], axis=1)
    return dict(N=N, N1=N1, N1nz=N1nz, CG=CG, NK1=NK1, RK=RK, U=DHY // CG,
                M1=_bf(M1), M1full=_bf(M1full), twA=_f32(twA), twB=_f32(twB),
                tiA=_f32(tiA), tiB=_f32(tiB), G3=_bf(G3))


def f2_consts():
    n = np.arange(128, dtype=np.float64)
    th = TWO_PI * np.outer(n, n) / 128
    C = np.cos(th); S = np.sin(th)
    F2 = np.stack([C, S, -S], axis=1)
    IM = np.stack([np.concatenate([C, S], 1), np.concatenate([-C, -S], 1), np.concatenate([-S, C], 1)], axis=1)
    return _bf(F2), _bf(IM)


def filter_consts(L):
    pos = np.arange(L, dtype=np.float64)
    bands = np.linspace(1e-4, 15.0, 16)

    def feats(p):
        t = p / max(L - 1, 1)
        ang = (TWO_PI / L) * p[:, None] * bands[None, :]
        return np.concatenate([t[:, None], np.cos(ang), -np.sin(ang)], axis=1)
    prev = (L - pos) % L
    ff = feats(pos).T
    fr = feats(prev).T
    deltas = np.abs(np.linspace(math.log(1e-2) / 1.5, math.log(1e-2) / 0.3, DHY))
    t = pos / max(L - 1, 1)
    dec_f = np.exp(-t[None, :] * deltas[:, None])
    dec_r = np.exp(-(prev / max(L - 1, 1))[None, :] * deltas[:, None])
    dec_r[:, 0] = 0.0
    return _f32(np.concatenate([ff, fr], axis=1)), _f32(np.concatenate([dec_f, dec_r], axis=1))


OFFS = (-128, -64, 0, 64, 128)
DILS = (1, 4, 16)


def t5_bucket_np(rel):
    half = 16
    max_exact = 8
    ret = np.where(rel > 0, half, 0)
    n = np.abs(rel)
    large = max_exact + (np.log(np.maximum(n, 1).astype(np.float32) / max_exact)
                         / math.log(1024 / max_exact) * (half - max_exact)).astype(np.int32)
    large = np.minimum(large, half - 1)
    return ret + np.where(n < max_exact, n, large)


def bias_onehot():
    oh = np.zeros((33, 3, 512), dtype=np.float32)
    for ci, dil in enumerate(DILS):
        v = np.arange(512)
        rel = 255 - v
        valid = np.abs(rel) <= 64
        bk = t5_bucket_np(rel * dil)
        for vv in range(512):
            if valid[vv]:
                oh[bk[vv], ci, vv] = 1.0
            else:
                oh[32, ci, vv] = -10000.0
    return oh.reshape(33, 3 * 512)


def build_program(debug=False, phases=(0, 1, 2, 3, 4, 5), only_seq=None, sub2=(1, 2, 3, 4, 5), cfgs=(0, 1, 2), p3=9):
    nc = bass.Bass("TRN2", target_bir_lowering=False)
    S = Sched(nc)
    dbg_outs = []

    def din(name, shape, dt=F32):
        return nc.dram_tensor(name, list(shape), dt, kind="ExternalInput").ap()

    def dscr(name, shape, dt):
        kind = "ExternalOutput" if debug else "Internal"
        if debug:
            dbg_outs.append(name)
        return nc.dram_tensor(name, list(shape), dt, kind=kind).ap()

    seqs = [("p", LP), ("s", LS)]
    run_seqs = [q for q in seqs if only_seq is None or q[0] == only_seq]
    FC = {L: fft_consts(L) for _, L in seqs}

    x_d = {"p": din("x_p", [LP, D]), "s": din("x_s", [LS, D])}
    mem_d = {"p": din("mem_p", [NMEM, D]), "s": din("mem_s", [NMEM, D])}
    y_d = {"p": nc.dram_tensor("y_p", [LP, D], F32, kind="ExternalOutput").ap(),
           "s": nc.dram_tensor("y_s", [LS, D], F32, kind="ExternalOutput").ap()}
    w_in_d = din("w_in", [D, DIN])
    w_out_d = din("w_out", [D, D])
    wkv_d = din("w_mem_kv", [D, 512])
    g_in_d = din("g_in", [128, 8]); g_mem_d = din("g_mem", [128, 8])
    gains_d = din("gains", [128, 4])
    convw_d = din("convw", [128, 9, 3]); convb_d = din("convb", [128, 9]); skip_d = din("skipv", [128, 3])
    fw1_d = din("fw1", [33, 64]); fw2_d = din("fw2", [64, 64]); fw3_d = din("fw3", [64, 768])
    fvec_d = din("fvec", [64, 3])
    relb_d = din("rel_bias", [32, 6])
    ident_d = din("ident", [128, 128], BF16); jmat_d = din("jmat", [128, 128], BF16)
    onesblk_d = din("onesblk", [128, 128], BF16)
    f2_d = din("f2", [128, 3, 128], BF16); im_d = din("imat", [128, 3, 256], BF16)
    oh_d = din("bias_oh", [33, 1536])
    fcd = {}
    for sn, L in seqs:
        c = FC[L]
        RK = c["RK"]
        fcd[sn] = dict(M1=din(f"M1_{sn}", [64, 2 * RK], BF16), M1full=din(f"M1f_{sn}", [128, 2 * RK], BF16),
                       twA=din(f"twA_{sn}", [128, RK]), twB=din(f"twB_{sn}", [128, 2, RK]),
                       tiA=din(f"tiA_{sn}", [RK, 128]), tiB=din(f"tiB_{sn}", [RK, 128]),
                       G3=din(f"G3_{sn}", [RK, 3, 64], BF16),
                       feats=din(f"feats_{sn}", [33, 2 * L]), dec=din(f"dec_{sn}", [DHY, 2 * L]))
    scr = {}
    for sn, L in seqs:
        c = FC[L]
        nb = c["U"] // 6
        scr[sn] = dict(
            zhy=dscr(f"zhy_{sn}", [1152, L], BF16), qT=dscr(f"qT_{sn}", [384, L], BF16),
            kT=dscr(f"kT_{sn}", [384, L], BF16), vtok=dscr(f"vtok_{sn}", [L, 768], BF16),
            mqT=dscr(f"mqT_{sn}", [256, L], BF16), gT=dscr(f"gT_{sn}", [1024, L], BF16),
            sT=dscr(f"sT_{sn}", [384, L], BF16), kfilt=dscr(f"kfilt_{sn}", [384, 2 * L], BF16),
            kspec=dscr(f"kspec_{sn}", [nb, 128, 2, 6 * c["RK"]], F32),
            conv=dscr(f"conv_{sn}", [384, L], F32), mixed=dscr(f"mixed_{sn}", [1024, L], BF16))
    htab_d = dscr("htab", [6, 1536], BF16)
    ebd_d = dscr("ebd", [36, 128, 512], BF16)

    def sb(name, shape, dt):
        return T(nc.alloc_sbuf_tensor(name, list(shape), dt))

    _uid = [0]

    def uniq(name):
        _uid[0] += 1
        return f"{name}_{_uid[0]}"

    w_out_bf = sb("w_out_bf", [128, 8, D], BF16)
    ident = sb("ident_sb", [128, 128], BF16); jmat = sb("jmat_sb", [128, 128], BF16)
    onesblk = sb("onesblk_sb", [128, 128], BF16)
    ones_bf = sb("ones_bf", [128, 64], BF16)
    g_in = sb("g_in_sb", [128, 8], F32); g_mem = sb("g_mem_sb", [128, 8], F32)
    gains = sb("gains_sb", [128, 4], F32)
    convw = sb("convw_sb", [128, 9, 3], F32); convb = sb("convb_sb", [128, 9], F32); skipv = sb("skip_sb", [128, 3], F32)
    cst = sb("cst_sb", [128, 4], F32)
    fw1 = sb("fw1_sb", [33, 64], F32); fw2 = sb("fw2_sb", [64, 64], F32); fw3 = sb("fw3_sb", [64, 768], BF16)
    fvec = sb("fvec_sb", [64, 3], F32); fab = sb("fab_sb", [64, 3], F32)
    kmT = {sn: sb(f"kmT_{sn}", [128, 2, NMEM], BF16) for sn, _ in seqs}
    vm = {sn: sb(f"vm_{sn}", [128, 2, 256], BF16) for sn, _ in seqs}
    f2m = sb("f2_sb", [128, 3, 128], BF16); imat = sb("im_sb", [128, 3, 256], BF16)
    pb = [T(nc.alloc_psum_tensor(f"pb{i}", [128, 512], F32)) for i in range(8)]
    pb16 = [T(p.t.bitcast(BF16), p.b) for p in pb]
    es01 = contextlib.ExitStack()
    w_in_bf = T(es01.enter_context(nc.sbuf_tensor("w_in_bf", [128, 8, DIN], BF16)))
    wkv_bf = T(es01.enter_context(nc.sbuf_tensor("wkv_bf", [128, 8, 512], BF16)))

    LD = "sp"
    ST = "pool"

    def ld(out, in_, reads=(), writes=(), q=LD):
        S.dma(q, lambda e: e.dma_start(out=out, in_=in_), reads=reads, writes=writes)

    def st(out, in_, reads=(), writes=(), final=False, q=ST, slow=False):
        if slow:
            S.dma(q, lambda e: e.dma_start(out=out, in_=in_, allow_slow_non_contiguous=True), reads=reads, writes=writes, final=final)
        else:
            S.dma(q, lambda e: e.dma_start(out=out, in_=in_), reads=reads, writes=writes, final=final)

    def act(out, in_, func, reads, writes, bias=None, scale=None, accum_out=None):
        kw = {}
        if bias is not None:
            kw["bias"] = bias
        if scale is not None:
            kw["scale"] = scale
        if accum_out is not None:
            kw["accum_out"] = accum_out
        S.op("act", lambda e: e.activation(out=out, in_=in_, func=func, **kw), reads=reads, writes=writes)

    def tsc(eng, out, in0, s1, s2, op0, op1, reads, writes):
        if s2 is None:
            S.op(eng, lambda e: e.tensor_scalar(out=out, in0=in0, scalar1=s1, scalar2=None, op0=op0), reads=reads, writes=writes)
        else:
            S.op(eng, lambda e: e.tensor_scalar(out=out, in0=in0, scalar1=s1, scalar2=s2, op0=op0, op1=op1), reads=reads, writes=writes)

    def tt(eng, out, in0, in1, op, reads, writes):
        S.op(eng, lambda e: e.tensor_tensor(out=out, in0=in0, in1=in1, op=op), reads=reads, writes=writes)

    def stt(eng, out, in0, scalar, in1, op0, op1, reads, writes):
        S.op(eng, lambda e: e.scalar_tensor_tensor(out=out, in0=in0, scalar=scalar, in1=in1, op0=op0, op1=op1),
             reads=reads, writes=writes)

    def recip(eng, out, in_, reads, writes):
        S.op(eng, lambda e: e.reciprocal(out=out, in_=in_), reads=reads, writes=writes)

    def cp(eng, out, in_, reads, writes):
        S.op(eng, lambda e: e.tensor_copy(out=out, in_=in_), reads=reads, writes=writes)

    def mset(eng, ap, val, writes):
        S.op(eng, lambda e: e.memset(ap, val), writes=writes)

    def rsum(eng, out, in_, reads, writes):
        S.op(eng, lambda e: e.reduce_sum(out=out, in_=in_, axis=mybir.AxisListType.X), reads=reads, writes=writes)

    def mm(out, lhsT, rhs, start, stop, reads, writes):
        def emit(e):
            try:
                return e.matmul(out=out, lhsT=lhsT, rhs=rhs, start=start, stop=stop, skip_group_check=True)
            except Exception:
                print("MATMUL FAIL out", out, "\nlhsT", lhsT, "\nrhs", rhs)
                raise
        S.op("pe", emit, reads=reads, writes=writes)

    def tr(out, in_, reads, writes):
        S.op("pe", lambda e: e.transpose(out=out, in_=in_, identity=ident[:, :]), reads=list(reads) + [ident], writes=writes)

    EBVAR = {"int": (1, 3), "first": (2, 4), "last": (0, 2), "only": (2, 2)}
    EBIDX = {}
    for ci_ in range(3):
        for vn_ in EBVAR:
            for pr_ in range(3):
                EBIDX[(ci_, vn_, pr_)] = len(EBIDX)

    eb_built = [False]

    def build_eb_tiles(tmp):
        items = [(ci, vn, offs, pr) for ci in range(3) for vn, offs in EBVAR.items() for pr in range(3)]
        hm = {}
        for ci in range(3):
            for hd in range(6):
                t = tmp(f"hm{ci}{hd}", [128, 384], BF16)
                hm[(ci, hd)] = t
                ld(t[:, :], bass.AP(htab_d.tensor, hd * 1536 + ci * 512, [[1, 128], [1, 384]]), writes=[t])
        ebs = [tmp(f"ebs{i}", [128, 512], BF16) for i in range(4)]

        def slot(k):
            ci, vn, offs, pr = items[k]
            nk = 1 if vn == "only" else 2
            W2 = 128 * 2 * nk
            pz = pb[5 + k % 3]
            t = ebs[k % 4]
            for ab in range(2):
                for kt in range(nk):
                    sft = 128 - OFFS[offs[kt]]
                    bi = nk * ab + kt
                    h = hm[(ci, 2 * pr + ab)]
                    mm(pz[:, 128 * bi:128 * bi + 128], jmat[:, :], h[:, sft:sft + 128], True, True, [jmat, h], [pz])
            if W2 < 512:
                mset("pool", t[:, W2:512], 0.0, [t])
            act(t[:, 0:W2], pz[:, 0:W2], AF.Exp, [pz], [t])
            st(ebd_d[EBIDX[(ci, vn, pr)]], t[:, :], reads=[t])
        return [(lambda k=k: slot(k)) for k in range(len(items))]

    with contextlib.ExitStack() as es:
        def tmp(name, shape, dt):
            return T(es.enter_context(nc.sbuf_tensor(uniq(name), list(shape), dt)))

        for dst, src in ((ident, ident_d), (jmat, jmat_d), (onesblk, onesblk_d), (g_in, g_in_d), (g_mem, g_mem_d),
                         (gains, gains_d), (convb, convb_d), (skipv, skip_d), (fw1, fw1_d), (fw2, fw2_d), (fvec, fvec_d)):
            ld(dst[:, :], src[:, :], writes=[dst])
        ld(convw[:, :, :], convw_d[:, :, :], writes=[convw])
        ld(f2m[:, :, :], f2_d[:, :, :], writes=[f2m])
        ld(imat[:, :, :], im_d[:, :, :], writes=[imat])
        mset("pool", ones_bf[:, :], 1.0, [ones_bf])
        mset("pool", cst[:, 0:1], 1e-6, [cst])
        mset("pool", cst[:, 1:2], -math.pi, [cst])
        mset("pool", cst[:, 2:3], 1e-12, [cst])
        mset("pool", cst[:, 3:4], 0.0, [cst])
        tsc("dve", gains[:, 0:1], gains[:, 0:1], 0.125, None, ALU.mult, None, [gains], [gains])
        tsc("dve", gains[:, 2:3], gains[:, 2:3], 0.125, None, ALU.mult, None, [gains], [gains])
        tsc("dve", fab[:, 0:1], fvec[:, 1:2], 1.0 / TWO_PI, None, ALU.mult, None, [fvec], [fab])
        tt("dve", fab[:, 1:2], fvec[:, 0:1], fab[:, 0:1], ALU.mult, [fvec, fab], [fab])
        tt("dve", fab[:, 2:3], fvec[:, 2:3], fab[:, 0:1], ALU.mult, [fvec, fab], [fab])
        w3st = tmp("w3st", [64, 768], F32)
        ld(w3st[:, :], fw3_d[:, :], writes=[w3st])
        cp("dve", fw3[:, :], w3st[:, :], [w3st], [fw3])
        wst = [tmp(f"wst{i}", [128, DIN], F32) for i in range(2)]
        n = 0
        for k in range(8):
            w = wst[n % 2]; n += 1
            ld(w[:, :], w_in_d[128 * k:128 * k + 128, :], writes=[w])
            if k % 2 == 0:
                tsc("dve", w_in_bf[:, k, :], w[:, :], g_in[:, k:k + 1], None, ALU.mult, None, [w, g_in], [w_in_bf])
            else:
                act(w_in_bf[:, k, :], w[:, :], AF.Copy, [w, g_in], [w_in_bf], scale=g_in[:, k:k + 1])
        for k in range(8):
            w = wst[n % 2]; n += 1
            ld(w[:, 0:D], w_out_d[128 * k:128 * k + 128, :], writes=[w])
            ld(w[:, D:D + 512], wkv_d[128 * k:128 * k + 128, :], writes=[w])
            cp("dve", w_out_bf[:, k, :], w[:, 0:D], [w], [w_out_bf])
            act(wkv_bf[:, k, :], w[:, D:D + 512], AF.Copy, [w, g_mem], [wkv_bf], scale=g_mem[:, k:k + 1])
        relb = tmp("relb", [33, 6], F32)
        ohs = tmp("ohs", [33, 1536], F32)
        hts = tmp("hts", [6, 1536], BF16)
        mset("pool", relb[:, :], 1.0, [relb])
        ld(relb[0:32, :], relb_d[:, :], writes=[relb])
        ld(ohs[:, :], oh_d[:, :], writes=[ohs])
        for j in range(3):
            mm(pb[j % 2][0:6, 0:512], relb[:, :], ohs[:, 512 * j:512 * j + 512], True, True, [relb, ohs], [pb[j % 2]])
            act(hts[:, 512 * j:512 * j + 512], pb[j % 2][0:6, 0:512], AF.Copy, [pb[j % 2]], [hts])
        st(htab_d[:, :], hts[:, :], reads=[hts])
        S.barrier()

    if 1 in phases:
        with contextlib.ExitStack() as es:
            def tmp(name, shape, dt):
                return T(es.enter_context(nc.sbuf_tensor(uniq(name), list(shape), dt)))

            xin = [tmp(f"xin{i}", [128, D], F32) for i in range(2)]
            ssq = [tmp(f"ssq{i}", [128, 1], F32) for i in range(3)]
            xs = [tmp(f"xs{i}", [128, D], BF16) for i in range(4)]
            xT = [tmp(f"xT{i}", [128, 8, 512], BF16) for i in range(2)]
            Zb = [tmp(f"Zb{j}", [128, 514], F32) for j in range(9)]
            u1b = [tmp(f"u1b{i}", [128, 512], F32) for i in range(3)]
            uvb = [tmp(f"uvb{i}", [128, 512], F32) for i in range(2)]
            x0_st = [tmp(f"x0st{i}", [128, 3, 512], BF16) for i in range(2)]
            s_st = [tmp(f"sst{i}", [128, 3, 512], BF16) for i in range(2)]
            ulast = tmp("ulast", [128, 9], F32)
            lst = tmp("lst", [128, 6, 1], BF16)
            qk_st = [tmp(f"qkst{i}", [128, 6, 512], BF16) for i in range(2)]
            mq_st = [tmp(f"mqst{i}", [128, 2, 512], BF16) for i in range(2)]
            g_st = [tmp(f"gst{i}", [128, 8, 512], BF16) for i in range(2)]
            v_st = [tmp(f"vst{i}", [128, 4, 768], BF16) for i in range(1)]
            for v_ in v_st:
                mset("pool", v_[:, :, :], 1.0, [v_])
            sqb = [tmp(f"sqb{i}", [128, 512], BF16) for i in range(2)]
            rrb = [tmp(f"rrb{i}", [128, 512], F32) for i in range(2)]
            cnt = dict(x=0, pz=0, hn=0, pt=0, uv=0)

            def prep_a(src_rows, slot):
                i = cnt["x"]; cnt["x"] += 1
                xi = xin[i % 2]; sq = ssq[i % 3]; xsb = xs[slot]
                ld(xi[:, :], src_rows, writes=[xi])
                mset("pool", sq[:, :], 0.0, [sq])
                act(xsb[:, :], xi[:, :], AF.Square, [xi], [xsb, sq], accum_out=sq[:, :])
                act(sq[:, :], sq[:, :], AF.Ln, [sq, cst], [sq], bias=cst[:, 0:1], scale=1.0 / D)
                act(sq[:, :], sq[:, :], AF.Exp, [sq], [sq], scale=-0.5)
                tsc("dve", xsb[:, :], xi[:, :], sq[:, 0:1], None, ALU.mult, None, [xi, sq], [xsb])

            def prep_b(slot, xT_t, col0):
                xsb = xs[slot]
                p = cnt["pt"] % 2; cnt["pt"] += 1
                for k in range(8):
                    tr(pb16[p][:, 128 * k:128 * k + 128], xsb[:, 128 * k:128 * k + 128], [xsb], [pb16[p]])
                cp("dve", xT_t[:, :, col0:col0 + 128], pb16[p][:, :].rearrange("p (k t) -> p k t", k=8), [pb16[p]], [xT_t])

            def prep_tile(src_rows, xT_t, col0, ncols_total):
                prep_a(src_rows, cnt["x"] % 4)
                prep_b((cnt["x"] - 1) % 4, xT_t, col0)

            pending = []

            def headnorm(pz, gcol, out_ap, ncols, out_t):
                h = cnt["hn"] % 2; cnt["hn"] += 1
                sq = sqb[h]; rr = rrb[h]; ph = pb[6 + h]
                act(sq[:, 0:ncols], pz[:, 0:ncols], AF.Square, [pz], [sq])

                def part_b():
                    mm(ph[:, 0:ncols], onesblk[:, :], sq[:, 0:ncols], True, True, [onesblk, sq], [ph])
                    act(rr[:, 0:ncols], ph[:, 0:ncols], AF.Ln, [ph, cst], [rr], bias=cst[:, 0:1], scale=1.0 / 64)
                    act(rr[:, 0:ncols], rr[:, 0:ncols], AF.Exp, [rr], [rr], scale=-0.5)
                    stt("dve", out_ap, pz[:, 0:ncols], gains[:, gcol:gcol + 1], rr[:, 0:ncols], ALU.mult, ALU.mult,
                        [pz, gains, rr], [out_t])
                pending.append(part_b)

            def flush_pending():
                while pending:
                    pending.pop(0)()

            def next_pz():
                p = pb[2 + cnt["pz"] % 4]; cnt["pz"] += 1
                return p

            for sn, L in run_seqs:
                sc = scr[sn]
                mT = xT[0]
                for i in range(2):
                    prep_tile(mem_d[sn][128 * i:128 * i + 128, :], mT, 128 * i, 256)
                for j in range(2):
                    pz = next_pz()
                    for k in range(8):
                        mm(pz[:, 0:256], wkv_bf[:, k, 128 * j:128 * j + 128], mT[:, k, 0:256], k == 0, k == 7, [wkv_bf, mT], [pz])
                    headnorm(pz, 3, kmT[sn][:, j, :], 256, kmT[sn])
                    flush_pending()
                for i in range(2):
                    pz = next_pz()
                    for k in range(8):
                        mm(pz[:, 0:256], mT[:, k, 128 * i:128 * i + 128], wkv_bf[:, k, 256:512], k == 0, k == 7, [wkv_bf, mT], [pz])
                    act(vm[sn][:, i, :], pz[:, 0:256], AF.Copy, [pz], [vm[sn]])
                nch = L // 512

                def prep_a_tile(c, i):
                    r0 = 512 * c + 128 * i
                    prep_a(x_d[sn][r0:r0 + 128, :], i)

                def prep_b_chunk(c):
                    for i in range(4):
                        prep_b(i, xT[(c + 1) % 2], 128 * i)

                def main_chunk(c):
                    xt = xT[(c + 1) % 2]
                    c0 = 512 * c
                    x0s = x0_st[c % 2]; sst = s_st[c % 2]
                    qs = qk_st[c % 2]; ms = mq_st[c % 2]; gs = g_st[c % 2]; vs = v_st[0]
                    order = [9, 0, 10, 1, 11, 2, 12, 3, 13, 4, 14, 5, 18, 6, 19, 7, 8] + list(range(20, 28))
                    for jn, j in enumerate(order):
                        if jn in (3, 9, 15, 21) and c + 2 < nch:
                            prep_a_tile(c + 2, (jn - 3) // 6)
                        pz = next_pz()
                        for k in range(8):
                            mm(pz[:, :], w_in_bf[:, k, 128 * j:128 * j + 128], xt[:, k, :], k == 0, k == 7, [w_in_bf, xt], [pz])
                        flush_pending()
                        if j < 9:
                            zb = Zb[j]
                            act(zb[:, 2:514], pz[:, :], AF.Copy, [pz], [zb])
                            if j < 3:
                                u = uvb[cnt["uv"] % 2]; cnt["uv"] += 1
                            elif j < 6:
                                u = u1b[j - 3]
                            else:
                                u = uvb[cnt["uv"] % 2]; cnt["uv"] += 1
                            if j % 3 == 1:
                                tsc("dve", u[:, :], zb[:, 1:513], convw[:, j, 1:2], convb[:, j:j + 1], ALU.mult, ALU.add, [zb, convw, convb], [u])
                            else:
                                act(u[:, :], zb[:, 1:513], AF.Identity, [zb, convw, convb], [u], bias=convb[:, j:j + 1], scale=convw[:, j, 1:2])
                            stt("dve", u[:, :], zb[:, 0:512], convw[:, j, 0:1], u[:, :], ALU.mult, ALU.add, [zb, convw, u], [u])
                            if j < 3:
                                stt("dve", x0s[:, j, :], zb[:, 2:514], convw[:, j, 2:3], u[:, :], ALU.mult, ALU.add, [zb, convw, u], [x0s])
                            else:
                                stt("dve", u[:, :], zb[:, 2:514], convw[:, j, 2:3], u[:, :], ALU.mult, ALU.add, [zb, convw, u], [u])
                            if j >= 6:
                                tt("pool", sst[:, j - 6, :], u1b[j - 6][:, :], u[:, :], ALU.mult, [u1b[j - 6], u], [sst])
                            cp("dve", zb[:, 0:2], zb[:, 512:514], [zb], [zb])
                        elif j < 12:
                            headnorm(pz, 0, qs[:, j - 9, :], 512, qs)
                        elif j < 15:
                            headnorm(pz, 1, qs[:, j - 9, :], 512, qs)
                        elif j < 20:
                            headnorm(pz, 2, ms[:, j - 18, :], 512, ms)
                        else:
                            act(gs[:, j - 20, :], pz[:, :], AF.Silu, [pz], [gs])
                    flush_pending()
                    for i in range(4):
                        pz = next_pz()
                        for k in range(8):
                            mm(pz[:, 0:384], xt[:, k, 128 * i:128 * i + 128], w_in_bf[:, k, 1920:2304], k == 0, k == 7, [w_in_bf, xt], [pz])
                        vo = bass.AP(vs.t, i * 768, [[4 * 768, 128], [256, 3], [192, 2], [1, 64]])
                        act(vo, pz[:, 0:384].rearrange("p (a b e) -> p a b e", a=3, b=2), AF.Copy, [pz], [vs])
                    if c == 0:
                        st(sc["zhy"][0:384, 0:511].rearrange("(j p) t -> p j t", p=128), x0s[:, :, 1:512], reads=[x0s])
                        st(sc["sT"][:, 0:511].rearrange("(j p) t -> p j t", p=128), sst[:, :, 1:512], reads=[sst])
                    else:
                        st(sc["zhy"][0:384, c0 - 1:c0 + 511].rearrange("(j p) t -> p j t", p=128), x0s[:, :, :], reads=[x0s])
                        st(sc["sT"][:, c0 - 1:c0 + 511].rearrange("(j p) t -> p j t", p=128), sst[:, :, :], reads=[sst])
                    st(sc["qT"][:, c0:c0 + 512].rearrange("(j p) t -> p j t", p=128), qs[:, 0:3, :], reads=[qs])
                    st(sc["kT"][:, c0:c0 + 512].rearrange("(j p) t -> p j t", p=128), qs[:, 3:6, :], reads=[qs])
                    st(sc["mqT"][:, c0:c0 + 512].rearrange("(j p) t -> p j t", p=128), ms[:, :, :], reads=[ms])
                    st(sc["gT"][:, c0:c0 + 512].rearrange("(j p) t -> p j t", p=128), gs[:, :, :], reads=[gs])
                    st(sc["vtok"][c0:c0 + 512, :].rearrange("(i p) e -> p i e", p=128), vs[:, :, :], reads=[vs])

                for zb in Zb:
                    mset("pool", zb[:, 0:2], 0.0, [zb])
                for i in range(4):
                    prep_a_tile(0, i)
                prep_b_chunk(0)
                if nch > 1:
                    for i in range(4):
                        prep_a_tile(1, i)
                for c in range(nch):
                    if c + 1 < nch:
                        prep_b_chunk(c + 1)
                    main_chunk(c)
                for j in range(9):
                    act(ulast[:, j:j + 1], Zb[j][:, 1:2], AF.Identity, [Zb[j], convw, convb], [ulast],
                        bias=convb[:, j:j + 1], scale=convw[:, j, 1:2])
                    stt("dve", ulast[:, j:j + 1], Zb[j][:, 0:1], convw[:, j, 0:1], ulast[:, j:j + 1], ALU.mult, ALU.add,
                        [Zb[j], convw, ulast], [ulast])
                cp("dve", lst[:, 0:3, 0], ulast[:, 0:3], [ulast], [lst])
                tt("dve", lst[:, 3:6, 0], ulast[:, 3:6], ulast[:, 6:9], ALU.mult, [ulast], [lst])
                st(sc["zhy"][0:384, L - 1:L].rearrange("(j p) o -> p j o", p=128), lst[:, 0:3, :], reads=[lst], slow=True)
                st(sc["sT"][:, L - 1:L].rearrange("(j p) o -> p j o", p=128), lst[:, 3:6, :], reads=[lst], slow=True)
            S.barrier()

    es01.close()

    if 2 in phases:
        for sn, L in run_seqs:
            sc = scr[sn]; fc = FC[L]; fd = fcd[sn]
            RK = fc["RK"]; U = fc["U"]; NB = U // 6; N2L = 2 * L
            with contextlib.ExitStack() as es:
                def tmp(name, shape, dt):
                    return T(es.enter_context(nc.sbuf_tensor(uniq(name), list(shape), dt)))
                h2 = tmp("h2", [64, N2L], BF16)
                fts = [tmp(f"fts{i}", [33, 512], F32) for i in range(2)]
                ysb = [tmp(f"ysb{i}", [64, 512], F32) for i in range(2)]
                kib = [tmp(f"kib{i}", [64, 512], I32) for i in range(2)]
                msk = [tmp(f"msk{i}", [64, 512], F32) for i in range(2)]
                h1 = [tmp(f"h1{i}", [64, 512], F32) for i in range(2)]
                kraw = tmp("kraw", [128, N2L], F32)
                dcs = [tmp(f"dcs{i}", [128, 2048], F32) for i in range(3)]
                ndc = [0]
                kst = [tmp(f"kst{i}", [128, 2048], BF16) for i in range(2)]
                nrm = tmp("nrm", [128, 16], F32)
                nch2 = N2L // 512

                ysb2 = [tmp(f"ysc{i}", [64, 512], F32) for i in range(2)]
                kib2 = [tmp(f"kic{i}", [64, 512], I32) for i in range(2)]
                msk2 = [tmp(f"msc{i}", [64, 512], F32) for i in range(2)]

                def sin_ops(pz, bcol, out_ap, out_t, y, ki, m):
                    return [
                        lambda: tsc("dve", y[:, :], pz[0:64, :], fab[:, 0:1], fab[:, bcol:bcol + 1], ALU.mult, ALU.add, [pz, fab], [y]),
                        lambda: cp("dve", ki[:, :], y[:, :], [y], [ki]),
                        lambda: tt("dve", y[:, :], y[:, :], ki[:, :], ALU.subtract, [y, ki], [y]),
                        lambda: act(out_ap, y[:, :], AF.Sin, [y], [out_t], scale=6.28318),
                    ]

                def l1_ops(c):
                    f = fts[c % 2]; pz = pb[c % 2]
                    return [lambda: (ld(f[:, :], fd["feats"][:, 512 * c:512 * c + 512], writes=[f]),
                                     mm(pz[0:64, :], fw1[:, :], f[:, :], True, True, [fw1, f], [pz]))] + \
                        sin_ops(pz, 1, h1[c % 2][:, :], h1[c % 2], ysb[c % 2], kib[c % 2], msk[c % 2])

                def l2_ops(c):
                    pz2 = pb[2 + c % 2]
                    return [lambda: mm(pz2[0:64, :], fw2[:, :], h1[c % 2][:, :], True, True, [fw2, h1[c % 2]], [pz2])] + \
                        sin_ops(pz2, 2, h2[:, 512 * c:512 * c + 512], h2, ysb2[c % 2], kib2[c % 2], msk2[c % 2])

                for op_ in l1_ops(0):
                    op_()
                for c in range(nch2):
                    A = l1_ops(c + 1) if c + 1 < nch2 else []
                    B = l2_ops(c)
                    for i in range(max(len(A), len(B))):
                        if i < len(A):
                            A[i]()
                        if i < len(B):
                            B[i]()
                for ct in range(3):
                    for c in range(nch2):
                        dc = dcs[(ndc[0] + c // 4) % 3]
                        if c % 4 == 0:
                            ld(dc[:, :], fd["dec"][128 * ct:128 * ct + 128, 512 * c:512 * c + 2048], writes=[dc])
                        pz = pb[c % 4]
                        col = 128 * ct if c < nch2 // 2 else 384 + 128 * ct
                        mm(pz[:, :], fw3[:, col:col + 128], h2[:, 512 * c:512 * c + 512], True, True, [fw3, h2], [pz])
                        tt("dve", kraw[:, 512 * c:512 * c + 512], pz[:, :], dc[:, 512 * (c % 4):512 * (c % 4) + 512], ALU.mult, [pz, dc], [kraw])
                    ndc[0] += nch2 // 4
                    pz = pb[6]
                    mm(pz[:, 0:8], fw3[:, 384 + 128 * ct:384 + 128 * ct + 128], h2[:, 0:8], True, True, [fw3, h2], [pz])
                    tt("dve", kraw[:, 0:1], kraw[:, 0:1], pz[:, 0:1], ALU.add, [kraw, pz], [kraw])
                    npc = N2L // 2048
                    mset("pool", nrm[:, :], 0.0, [nrm])
                    for q in range(npc):
                        act(kst[q % 2][:, :], kraw[:, 2048 * q:2048 * q + 2048], AF.Square, [kraw], [kst[q % 2], nrm],
                            accum_out=nrm[:, q:q + 1])
                    rsum("dve", nrm[:, 15:16], nrm[:, 0:npc], [nrm], [nrm])
                    act(nrm[:, 14:15], nrm[:, 15:16], AF.Sqrt, [nrm, cst], [nrm], bias=cst[:, 2:3], scale=1.0)
                    recip("dve", nrm[:, 13:14], nrm[:, 14:15], [nrm], [nrm])
                    for q in range(npc):
                        ks = kst[q % 2]
                        if q % 2 == 0:
                            tsc("dve", ks[:, :], kraw[:, 2048 * q:2048 * q + 2048], nrm[:, 13:14], None, ALU.mult, None, [kraw, nrm], [ks])
                        else:
                            act(ks[:, :], kraw[:, 2048 * q:2048 * q + 2048], AF.Copy, [kraw, nrm], [ks], scale=nrm[:, 13:14])
                        st(sc["kfilt"][128 * ct:128 * ct + 128, 2048 * q:2048 * q + 2048], ks[:, :], reads=[ks])
            S.barrier()
            with contextlib.ExitStack() as es:
                def tmp(name, shape, dt):
                    return T(es.enter_context(nc.sbuf_tensor(uniq(name), list(shape), dt)))
                M1 = tmp("M1", [64, 2 * RK], BF16); M1f = tmp("M1f", [128, 2 * RK], BF16)
                twA = tmp("twA", [128, RK], F32); twB = tmp("twB", [128, 2, RK], F32)
                tiA = tmp("tiA", [RK, 128], F32); tiB = tmp("tiB", [RK, 128], F32)
                G3 = tmp("G3", [RK, 3, 64], BF16)
                for dst, key in ((M1, "M1"), (M1f, "M1full"), (twA, "twA"), (tiA, "tiA"), (tiB, "tiB")):
                    ld(dst[:, :], fd[key][:, :], writes=[dst])
                ld(twB[:, :, :], fd["twB"][:, :, :], writes=[twB])
                ld(G3[:, :, :], fd["G3"][:, :, :], writes=[G3])
                xu = [tmp(f"xu{i}", [128, 6, 128], BF16) for i in range(2)]
                Tb = [tmp(f"Tb{i}", [128, 4, 6, RK], BF16) for i in range(2)]
                ksp = [tmp(f"ksp{i}", [128, 2, 6 * RK], F32) for i in range(2)]
                Pb = [tmp(f"Pb{i}", [128, 4, 6 * RK], BF16) for i in range(2)]
                Qb = [tmp(f"Qb{i}", [RK, 4, 6, 128], BF16) for i in range(2)]
                yst = [tmp(f"yst{i}", [64, 6, 128], F32) for i in range(2)]
                W6 = 6 * RK

                def bc_ap(t, dims):
                    h = t.t
                    return bass.AP(h, 0, [[int(np.prod(h.shape[1:])), int(h.shape[0])]] + [[s, n] for s, n in dims])

                def stage_a(b, src_view, Mmat, npart):
                    x = xu[b % 2]; Tt = Tb[b % 2]
                    ld(x[0:npart, :, :], src_view, writes=[x])
                    for g in range(2):
                        pa = pb[g]
                        for j in range(3):
                            u = 3 * g + j
                            mm(pa[:, 2 * RK * j:2 * RK * (j + 1)], x[0:npart, u, :], Mmat[0:npart, :], True, True, [x, Mmat], [pa])
                        in0 = pa[:, 0:6 * RK].rearrange("p (u r k) -> p u r k", u=3, r=2)
                        o1 = bass.AP(Tt.t, 3 * g * RK, [[4 * 6 * RK, 128], [RK, 3], [3 * 6 * RK, 2], [1, RK]])
                        i1 = bass.AP(twA.t, 0, [[RK, 128], [0, 3], [0, 2], [1, RK]])
                        tt("dve", o1, in0, i1, ALU.mult, [pa, twA], [Tt])
                        o2 = bass.AP(Tt.t, 6 * RK + 3 * g * RK, [[4 * 6 * RK, 128], [RK, 3], [6 * RK, 2], [1, RK]])
                        i2 = bass.AP(twB.t, 0, [[2 * RK, 128], [0, 3], [RK, 2], [1, RK]])
                        tt("dve", o2, in0, i2, ALU.mult, [pa, twB], [Tt])

                def stage_b_mm(b):
                    Tt = Tb[b % 2]
                    xr = pb[2]; xi = pb[3]

                    def blk(i):
                        return Tt[:, i, :, :].rearrange("p u k -> p (u k)")
                    seq = [(0, 0, xr), (0, 2, xr), (1, 1, xr), (1, 3, xr), (0, 1, xi), (0, 3, xi), (2, 0, xi), (2, 2, xi)]
                    started = set()
                    for mi, bi, dst in seq:
                        first = id(dst) not in started
                        started.add(id(dst))
                        mm(dst[:, 0:W6], f2m[:, mi, :], blk(bi), first, False, [f2m, Tt], [dst])
                    return xr, xi

                kf_view = sc["kfilt"].rearrange("c (n1 n2) -> (c n1) n2", n2=128).rearrange("(u p) n2 -> p u n2", p=128)

                def filt_b(b):
                    xr, xi = stage_b_mm(b)
                    ks = ksp[b % 2]
                    act(ks[:, 0, :], xr[:, 0:W6], AF.Copy, [xr], [ks])
                    act(ks[:, 1, :], xi[:, 0:W6], AF.Copy, [xi], [ks])
                    st(sc["kspec"][b], ks[:, :, :], reads=[ks], writes=[ksbuf[b]])

                ksbuf = [Buf(f"kspec{b}") for b in range(NB)]
                for t in range(NB + 1):
                    if t < NB:
                        stage_a(t, kf_view[:, 6 * t:6 * t + 6, :], M1f, 128)
                    if t >= 1:
                        filt_b(t - 1)
                s_view = sc["sT"].rearrange("c (n1 n2) -> (c n1) n2", n2=128).rearrange("(u p) n2 -> p u n2", p=64)
                c_view = sc["conv"].rearrange("c (n1 n2) -> (c n1) n2", n2=128).rearrange("(u p) n2 -> p u n2", p=64)

                def sig_a(b):
                    ks = ksp[b % 2]
                    ld(ks[:, :, :], sc["kspec"][b], reads=[ksbuf[b]], writes=[ks])
                    stage_a(b, s_view[:, 6 * b:6 * b + 6, :], M1, 64)

                def sig_b(b):
                    ks = ksp[b % 2]
                    xr, xi = stage_b_mm(b)
                    Pt = Pb[b % 2]
                    tt("dve", Pt[:, 0, :], xr[:, 0:W6], ks[:, 0, :], ALU.mult, [xr, ks], [Pt])
                    tt("dve", Pt[:, 2, :], xr[:, 0:W6], ks[:, 1, :], ALU.mult, [xr, ks], [Pt])
                    tt("dve", Pt[:, 1, :], xi[:, 0:W6], ks[:, 1, :], ALU.mult, [xi, ks], [Pt])
                    tt("dve", Pt[:, 3, :], xi[:, 0:W6], ks[:, 0, :], ALU.mult, [xi, ks], [Pt])

                def sig_c(b):
                    Pt = Pb[b % 2]; Qt = Qb[b % 2]
                    for g in range(3):
                        pc = pb[4 + g]
                        for j in range(2):
                            u = 2 * g + j
                            for bi, mi in ((0, 0), (1, 1), (2, 2), (3, 2)):
                                mm(pc[0:RK, 256 * j:256 * j + 256], Pt[:, bi, RK * u:RK * u + RK], imat[:, mi, :],
                                   bi == 0, bi == 3, [Pt, imat], [pc])
                        in0 = pc[0:RK, :].rearrange("p (u r n) -> p u r n", u=2, r=2)
                        oA = bass.AP(Qt.t, 2 * g * 128, [[4 * 6 * 128, RK], [128, 2], [6 * 128, 2], [1, 128]])
                        iA = bass.AP(tiA.t, 0, [[128, RK], [0, 2], [0, 2], [1, 128]])
                        tt("dve", oA, in0, iA, ALU.mult, [pc, tiA], [Qt])
                        oB = bass.AP(Qt.t, 2 * 6 * 128 + 2 * g * 128, [[4 * 6 * 128, RK], [128, 2], [6 * 128, 2], [1, 128]])
                        iB = bass.AP(tiB.t, 0, [[128, RK], [0, 2], [0, 2], [1, 128]])
                        tt("dve", oB, in0, iB, ALU.mult, [pc, tiB], [Qt])

                def sig_d(b):
                    Qt = Qb[b % 2]
                    ys = yst[b % 2]
                    py = pb[7]
                    for hlf in range(2):
                        for bi, gi in ((0, 0), (3, 1), (2, 2), (1, 2)):
                            mm(py[0:64, 0:384], G3[:, gi, :], Qt[:, bi, 3 * hlf:3 * hlf + 3, :].rearrange("p u n -> p (u n)"),
                               bi == 0, bi == 1, [G3, Qt], [py])
                        act(ys[:, 3 * hlf:3 * hlf + 3, :], py[0:64, 0:384].rearrange("p (u n) -> p u n", u=3), AF.Copy, [py], [ys])
                    st(c_view[:, 6 * b:6 * b + 6, :], ys[:, :, :], reads=[ys])

                for t in range(NB + 3):
                    if t < NB:
                        sig_a(t)
                    if 0 <= t - 1 < NB:
                        sig_b(t - 1)
                    if 0 <= t - 2 < NB:
                        sig_c(t - 2)
                    if 0 <= t - 3 < NB:
                        sig_d(t - 3)
            S.barrier()
            with contextlib.ExitStack() as es:
                def tmp(name, shape, dt):
                    return T(es.enter_context(nc.sbuf_tensor(uniq(name), list(shape), dt)))
                PC = 2048
                zp = [tmp(f"zq{i}", [128, PC + 2], BF16) for i in range(3)]
                uc = [tmp(f"ue{i}", [128, PC], F32) for i in range(3)]
                sq_ = [tmp(f"sq{i}", [128, PC], BF16) for i in range(3)]
                gq_ = [tmp(f"gq{i}", [128, PC], BF16) for i in range(3)]
                cq_ = [tmp(f"cq{i}", [128, PC], F32) for i in range(3)]
                mq_ = [tmp(f"mq{i}", [128, PC], BF16) for i in range(3)]
                n2e = 0
                for ct in range(3):
                    for piece in range(L // PC):
                        a = PC * piece
                        z = zp[n2e % 3]; u = uc[n2e % 3]; sv = sq_[n2e % 3]; gv = gq_[n2e % 3]; cv = cq_[n2e % 3]; mv = mq_[n2e % 3]
                        n2e += 1
                        ld(z[:, 0:PC], sc["zhy"][128 * ct:128 * ct + 128, a:a + PC], writes=[z])
                        ld(sv[:, :], sc["sT"][128 * ct:128 * ct + 128, a:a + PC], writes=[sv])
                        ld(gv[:, :], sc["gT"][128 * ct:128 * ct + 128, a:a + PC], writes=[gv])
                        ld(cv[:, :], sc["conv"][128 * ct:128 * ct + 128, a:a + PC], writes=[cv])
                        stt("dve", cv[:, :], sv[:, :], skipv[:, ct:ct + 1], cv[:, :], ALU.mult, ALU.add, [sv, skipv, cv], [cv])
                        tt("pool", u[:, :], z[:, 0:PC], cv[:, :], ALU.mult, [z, cv], [u])
                        tt("dve", mv[:, :], u[:, :], gv[:, :], ALU.mult, [u, gv], [mv])
                        st(sc["mixed"][128 * ct:128 * ct + 128, a:a + PC], mv[:, :], reads=[mv])
            S.barrier()

    if 3 in phases:
        with contextlib.ExitStack() as es:
            def tmp(name, shape, dt):
                return T(es.enter_context(nc.sbuf_tensor(uniq(name), list(shape), dt)))
            if not eb_built[0]:
                eb_built[0] = True
                with contextlib.ExitStack() as es2:
                    for sl in build_eb_tiles(lambda name, shape, dt: T(es2.enter_context(nc.sbuf_tensor(uniq(name), list(shape), dt)))):
                        sl()
                    S.barrier()
            eb = {}
            for key, idx in EBIDX.items():
                t = tmp(f"eb{idx}", [128, 4, 128], BF16)
                eb[key] = t
                ld(t[:, :, :].rearrange("p a b -> p (a b)"), ebd_d[idx], writes=[t])
            SR = 2048
            qsb = tmp("qsb", [128, 3, SR], BF16)
            ksb = tmp("ksb", [128, 3, 3 * SR], BF16)
            gsb = tmp("gsb", [128, 3, SR], BF16)
            acc = tmp("acc", [128, 3, 2, SR], F32)
            dal = tmp("dal", [128, SR], F32)
            msb = tmp("msb", [128, SR], BF16)
            NV = 8
            vsb = [tmp(f"vsb{i}", [128, 768], BF16) for i in range(NV)]
            pra = [tmp(f"pra{i}", [128, 512], BF16) for i in range(4)]
            ptb = [tmp(f"ptb{i}", [128, 512], BF16) for i in range(4)]
            vn_ctr = [0]
            it = [0]
            for sn, L in run_seqs:
                sc = scr[sn]
                for sr in range(L // SR):
                    T0 = sr * SR
                    w0 = max(0, T0 - 2048); w1 = min(L, T0 + SR + 2048)
                    ld(qsb[:, :, :], sc["qT"][:, T0:T0 + SR].rearrange("(j p) t -> p j t", p=128), writes=[qsb])
                    ld(ksb[:, :, 0:w1 - w0], sc["kT"][:, w0:w1].rearrange("(j p) t -> p j t", p=128), writes=[ksb])
                    ld(gsb[:, :, :], sc["gT"][384:768, T0:T0 + SR].rearrange("(j p) t -> p j t", p=128), writes=[gsb])
                    jobs = []
                    for ci, dil in enumerate(DILS):
                        n = L // dil
                        for r in range(dil):
                            for jt in range(SR // dil // 128):
                                bq = T0 // dil + 128 * jt
                                if n == 128:
                                    vn, kstart, nkt = "only", 0, 1
                                elif bq == 0:
                                    vn, kstart, nkt = "first", 0, 2
                                elif bq + 128 == n:
                                    vn, kstart, nkt = "last", n - 256, 2
                                else:
                                    vn, kstart, nkt = "int", bq - 64, 2
                                for pr in range(3):
                                    jobs.append((ci, dil, r, bq, vn, kstart, nkt, pr))
                    vcache = {}
                    state = {}

                    def s_stage(j):
                        ci, dil, r, bq, vn, kstart, nkt, pr = jobs[j]
                        vts = []
                        for kt in range(nkt):
                            key = (ci, r, kstart + 128 * kt)
                            if key not in vcache:
                                vt = vsb[vn_ctr[0] % NV]; vn_ctr[0] += 1
                                for kk in [k for k, v in vcache.items() if v is vt]:
                                    del vcache[kk]
                                tok0 = r + dil * (kstart + 128 * kt)
                                src = bass.AP(sc["vtok"].tensor, tok0 * 768, [[dil * 768, 128], [1, 768]])
                                ld(vt[:, :], src, writes=[vt])
                                vcache[key] = vt
                            vts.append(vcache[key])
                        qcol0 = r + dil * (bq - T0 // dil)
                        qsl = slice(qcol0, qcol0 + dil * 127 + 1, dil)
                        i = it[0]; it[0] += 1
                        psab = (pb[2 * (i % 3)], pb[2 * (i % 3) + 1])
                        pr_raw = pra[i % 4]; pt = ptb[i % 4]
                        W = 128 * nkt
                        for kt in range(nkt):
                            for ab in range(2):
                                kc0 = r + dil * (kstart + 128 * kt) - w0
                                assert kc0 >= 0 and kc0 + dil * 127 < w1 - w0
                                ksl = slice(kc0, kc0 + dil * 127 + 1, dil)
                                mm(psab[ab][:, 128 * kt:128 * kt + 128],
                                   ksb[64 * ab:64 * ab + 64, pr, ksl], qsb[64 * ab:64 * ab + 64, pr, qsl],
                                   True, True, [ksb, qsb], [psab[ab]])
                        for ab in range(2):
                            act(pr_raw[:, W * ab:W * ab + W], psab[ab][:, 0:W], AF.Exp, [psab[ab]], [pr_raw])
                        e = eb[(ci, vn, pr)]
                        tt("dve", pt[:, 0:2 * W], pr_raw[:, 0:2 * W], e[:, :, :].rearrange("p a b -> p (a b)")[:, 0:2 * W], ALU.mult,
                           [pr_raw, e], [pt])
                        state[j] = (vts, pt, qcol0, i)

                    def pv_stage(j):
                        ci, dil, r, bq, vn, kstart, nkt, pr = jobs[j]
                        vts, pt, qcol0, i = state.pop(j)
                        pnd = pb[6 + i % 2]
                        first = True
                        for ab in range(2):
                            for kt in range(nkt):
                                bi = nkt * ab + kt
                                hcol = (2 * pr + ab) * 128
                                mm(pnd[:, 128 * ab:128 * ab + 128], vts[kt][:, hcol:hcol + 128], pt[:, 128 * bi:128 * bi + 128],
                                   first, False, [vts[kt], pt], [pnd])
                                first = False
                        av = bass.AP(acc.t, pr * 2 * SR + qcol0, [[3 * 2 * SR, 128], [SR, 2], [dil, 128]])
                        pv2 = pnd[:, 0:256].rearrange("p (a q) -> p a q", a=2)
                        if ci == 0:
                            act(av, pv2, AF.Copy, [pnd], [acc])
                        else:
                            tt("dve", av, pv2, av, ALU.add, [pnd, acc], [acc])

                    s_stage(0)
                    s_stage(1)
                    for j in range(len(jobs)):
                        if j + 2 < len(jobs):
                            s_stage(j + 2)
                        pv_stage(j)
                    for pr in range(3):
                        ld(dal[0:64, :], acc[64:128, pr, 0, :], reads=[acc], writes=[dal])
                        ld(dal[64:128, :], acc[0:64, pr, 1, :], reads=[acc], writes=[dal])
                        act(dal[:, :], dal[:, :], AF.Ln, [dal], [dal])
                        act(dal[:, :], dal[:, :], AF.Exp, [dal], [dal], scale=-1.0)
                        tt("dve", dal[0:64, :], acc[0:64, pr, 0, :], dal[0:64, :], ALU.mult, [acc, dal], [dal])
                        tt("pool", dal[64:128, :], acc[64:128, pr, 1, :], dal[64:128, :], ALU.mult, [acc, dal], [dal])
                        tt("dve", msb[:, :], dal[:, :], gsb[:, pr, :], ALU.mult, [dal, gsb], [msb])
                        st(sc["mixed"][384 + 128 * pr:384 + 128 * pr + 128, T0:T0 + SR], msb[:, :], reads=[msb])
            S.barrier()

    if 4 in phases or 5 in phases:
        with contextlib.ExitStack() as es:
            def tmp(name, shape, dt):
                return T(es.enter_context(nc.sbuf_tensor(uniq(name), list(shape), dt)))
            mqs = [tmp(f"mqs{i}", [128, 2, 512], BF16) for i in range(2)]
            gms = [tmp(f"gms{i}", [128, 2, 512], BF16) for i in range(2)]
            pms = [tmp(f"pms{i}", [128, 4, 512], BF16) for i in range(2)]
            rdn = [tmp(f"rdn{i}", [128, 512], F32) for i in range(2)]
            mx = [tmp(f"mx{i}", [128, 8, 512], BF16) for i in range(2)]
            xr = [tmp(f"xr{i}", [128, D], F32) for i in range(3)]
            yo = [tmp(f"yo{i}", [128, D], F32) for i in range(2)]
            chunks = [(sn, L, c) for sn, L in run_seqs for c in range(L // 512)]
            n5 = [0]

            def p4_s(ci):
                sn, L, c = chunks[ci]
                sc = scr[sn]; c0 = 512 * c
                mq = mqs[ci % 2]; gm = gms[ci % 2]; m = mx[ci % 2]
                ld(mq[:, :, :], sc["mqT"][:, c0:c0 + 512].rearrange("(j p) t -> p j t", p=128), writes=[mq])
                ld(gm[:, :, :], sc["gT"][768:1024, c0:c0 + 512].rearrange("(j p) t -> p j t", p=128), writes=[gm])
                ld(m[:, 0:6, :], sc["mixed"][0:768, c0:c0 + 512].rearrange("(k p) t -> p k t", p=128), writes=[m])

            def p4_qk(ci, pr):
                sn, L, c = chunks[ci]
                mq = mqs[ci % 2]; pm = pms[pr]
                for ab in range(2):
                    for mt in range(2):
                        ps = pb[2 * ab + mt]
                        mm(ps[:, :], kmT[sn][64 * ab:64 * ab + 64, pr, 128 * mt:128 * mt + 128], mq[64 * ab:64 * ab + 64, pr, :],
                           True, True, [kmT[sn], mq], [ps])
                        act(pm[:, 2 * ab + mt, :], ps[:, :], AF.Exp, [ps], [pm])

            def p4_pv(ci, pr):
                sn, L, c = chunks[ci]
                gm = gms[ci % 2]; m = mx[ci % 2]
                pm = pms[pr]; rd = rdn[pr]
                pn = pb[4]; pd = pb[5]
                for ab in range(2):
                    for mt in range(2):
                        mm(pn[64 * ab:64 * ab + 64, :], vm[sn][:, mt, 128 * pr + 64 * ab:128 * pr + 64 * ab + 64], pm[:, 2 * ab + mt, :],
                           mt == 0, mt == 1, [vm[sn], pm], [pn])
                        mm(pd[64 * ab:64 * ab + 64, :], ones_bf[:, :], pm[:, 2 * ab + mt, :], mt == 0, mt == 1, [ones_bf, pm], [pd])
                act(rd[:, :], pd[:, :], AF.Ln, [pd], [rd])
                act(rd[:, :], rd[:, :], AF.Exp, [rd], [rd], scale=-1.0)
                tt("dve", rd[:, :], pn[:, :], rd[:, :], ALU.mult, [pn, rd], [rd])
                tt("pool", m[:, 6 + pr, :], rd[:, :], gm[:, pr, :], ALU.mult, [rd, gm], [m])

            def p5_tile(ci, i):
                sn, L, c = chunks[ci]
                c0 = 512 * c
                m = mx[ci % 2]
                r0 = c0 + 128 * i
                x = xr[n5[0] % 3]; y = yo[n5[0] % 2]
                ld(x[:, :], x_d[sn][r0:r0 + 128, :], writes=[x])
                for hf in range(2):
                    pz = pb[6 + hf]
                    for k in range(8):
                        mm(pz[:, :], m[:, k, 128 * i:128 * i + 128], w_out_bf[:, k, 512 * hf:512 * hf + 512], k == 0, k == 7,
                           [m, w_out_bf], [pz])
                    tt("dve", y[:, 512 * hf:512 * hf + 512], pz[:, :], x[:, 512 * hf:512 * hf + 512], ALU.add, [pz, x], [y])
                st(y_d[sn][r0:r0 + 128, :], y[:, :], reads=[y], final=True)
                n5[0] += 1

            p4_s(0)
            for pr in range(2):
                p4_qk(0, pr)
                p4_pv(0, pr)
            for ci in range(len(chunks)):
                nxt = ci + 1 < len(chunks)
                if nxt:
                    p4_s(ci + 1)
                    p4_qk(ci + 1, 0)
                p5_tile(ci, 0)
                p5_tile(ci, 1)
                if nxt:
                    p4_pv(ci + 1, 0)
                    p4_qk(ci + 1, 1)
                p5_tile(ci, 2)
                p5_tile(ci, 3)
                if nxt:
                    p4_pv(ci + 1, 1)

    S.emit_all()
    return nc, dbg_outs


def make_in_maps(inp):
    f2, im = f2_consts()
    shared = {}
    shared["w_in"] = _f32(inp["w_in"][0]); shared["w_out"] = _f32(inp["w_out"][0]); shared["w_mem_kv"] = _f32(inp["w_mem_kv"][0])
    shared["g_in"] = _f32(np.asarray(inp["norm_in"][0]).reshape(8, 128).T)
    shared["g_mem"] = _f32(np.asarray(inp["mem_norm"][0]).reshape(8, 128).T)
    gn = np.stack([np.tile(np.asarray(inp[k][0]), 2) for k in ("att_q_norm", "att_k_norm", "mem_q_norm", "mem_k_norm")], axis=1)
    shared["gains"] = _f32(gn)
    cw = np.asarray(inp["hy_conv_w"][0])
    shared["convw"] = _f32(cw.T.reshape(9, 128, 3).transpose(1, 0, 2))
    shared["convb"] = _f32(np.asarray(inp["hy_conv_b"][0]).reshape(9, 128).T)
    shared["skipv"] = _f32(np.asarray(inp["hy_skip"][0]).reshape(3, 128).T)
    shared["fw1"] = _f32(inp["hy_filt_w1"][0]); shared["fw2"] = _f32(inp["hy_filt_w2"][0]); shared["fw3"] = _f32(inp["hy_filt_w3"][0])
    shared["fvec"] = _f32(np.stack([np.asarray(inp["hy_filt_b1"][0]), np.asarray(inp["hy_filt_freq"][0]),
                                    np.asarray(inp["hy_filt_b2"][0])], axis=1))
    shared["rel_bias"] = _f32(inp["rel_bias"])
    shared["ident"] = _bf(np.eye(128)); shared["jmat"] = _bf(np.eye(128)[::-1])
    ob = np.zeros((128, 128)); ob[:64, :64] = 1; ob[64:, 64:] = 1
    shared["onesblk"] = _bf(ob)
    shared["f2"] = f2; shared["imat"] = im
    shared["bias_oh"] = _f32(bias_onehot())
    for sn, L in (("p", LP), ("s", LS)):
        c = fft_consts(L)
        shared[f"M1_{sn}"] = c["M1"]; shared[f"M1f_{sn}"] = c["M1full"]
        shared[f"twA_{sn}"] = c["twA"]; shared[f"twB_{sn}"] = c["twB"]
        shared[f"tiA_{sn}"] = c["tiA"]; shared[f"tiB_{sn}"] = c["tiB"]; shared[f"G3_{sn}"] = c["G3"]
        ft, dec = filter_consts(L)
        shared[f"feats_{sn}"] = ft; shared[f"dec_{sn}"] = dec
    maps = []
    for i in range(NCORES):
        m = dict(shared)
        m["x_p"] = _f32(inp["x_prompt"][i]); m["x_s"] = _f32(inp["x_sample"][i])
        m["mem_p"] = _f32(inp["mem_prompt"][i]); m["mem_s"] = _f32(inp["mem_sample"][i])
        maps.append(m)
    return maps


_CACHE = {}


def kernel(**inputs):
    inp = {k: np.asarray(v) for k, v in inputs.items()}
    maps = make_in_maps(inp)
    if "nc" not in _CACHE:
        _CACHE["nc"] = build_program()[0]
    res = run_bass_kernel_spmd(_CACHE["nc"], maps, core_ids=list(range(NCORES)))
    y_p = np.stack([np.asarray(res.results[i]["y_p"], dtype=np.float32) for i in range(NCORES)], axis=0)
    y_s = np.stack([np.asarray(res.results[i]["y_s"], dtype=np.float32) for i in range(NCORES)], axis=0)
    return (y_p, y_s)
```

```python
import contextlib
import math
import numpy as np
import ml_dtypes
import concourse.bass as bass
import concourse.mybir as mybir
from concourse.bass_utils import run_bass_kernel_spmd

F32 = mybir.dt.float32
BF16 = mybir.dt.bfloat16
I32 = mybir.dt.int32
ALU = mybir.AluOpType
AF = mybir.ActivationFunctionType

NCORES = 8
D = 1024
DIN = 3584
DHY = 384
LP = 8192
LS = 2048
NMEM = 256
TWO_PI = 2.0 * math.pi


class Buf:
    __slots__ = ("name", "w", "r")

    def __init__(self, name=""):
        self.name = name
        self.w = None
        self.r = {}


class T:
    def __init__(self, handle, buf=None):
        self.t = handle
        self.b = buf if buf is not None else Buf(getattr(handle, "name", ""))

    def __getitem__(self, key):
        return self.t[key]


def _b(x):
    return x.b if isinstance(x, T) else x


class Sched:
    NDMA_SEMS = 20

    def __init__(self, nc):
        self.nc = nc
        self.engs = {n: dict(ops=[], count=0, seen={}, pend={}) for n in ("pe", "act", "dve", "pool", "sp")}
        self.sems = {}
        self.dma_pool = {}
        self.dma_rr = {}
        self.final = {}

    def _sem(self, key):
        if key not in self.sems:
            self.sems[key] = self.nc.alloc_semaphore(f"s_{key}")
        return self.sems[key]

    @staticmethod
    def _deps(reads, writes):
        need = {}
        for b in reads:
            b = _b(b)
            if b.w is not None:
                k, v = b.w
                if need.get(k, 0) < v:
                    need[k] = v
        for b in writes:
            b = _b(b)
            if b.w is not None:
                k, v = b.w
                if need.get(k, 0) < v:
                    need[k] = v
            for k, v in b.r.items():
                if need.get(k, 0) < v:
                    need[k] = v
        return need

    def _waits(self, e, need):
        for k, v in e["pend"].items():
            if need.get(k, 0) < v:
                need[k] = v
        e["pend"] = {}
        waits = []
        for k, v in need.items():
            if e["seen"].get(k, 0) >= v:
                continue
            e["seen"][k] = v
            waits.append((k, v))
        return waits

    EPOCH = 4000

    def op(self, eng, emit, reads=(), writes=()):
        e = self.engs[eng]
        need = self._deps(reads, writes)
        if eng == "pe":
            need = {k: v for k, v in need.items() if not k.startswith("pe")}
        waits = self._waits(e, need)
        ep = e["count"] // self.EPOCH
        e["count"] += 1
        idx = e["count"] - ep * self.EPOCH
        key = eng if ep == 0 else f"{eng}{ep}"
        e["cur"] = (key, idx)
        e["ops"].append((waits, emit, (key, 1)))
        for b in reads:
            _b(b).r[key] = idx
        for b in writes:
            b = _b(b)
            b.w = (key, idx)
            b.r = {}
        return idx

    def dma(self, queue, emit, reads=(), writes=(), final=False):
        e = self.engs[queue]
        pool = self.dma_pool.setdefault(queue, [[f"d{queue}{i}", 0] for i in range(self.NDMA_SEMS)])
        i = self.dma_rr.get(queue, 0)
        self.dma_rr[queue] = (i + 1) % self.NDMA_SEMS
        slot = pool[i]
        key = slot[0]
        need = self._deps(reads, writes)
        if slot[1] > 0:
            need[key] = max(need.get(key, 0), slot[1] * 16)
        waits = self._waits(e, need)
        slot[1] += 1
        val = slot[1] * 16
        e["ops"].append((waits, emit, (key, 16)))
        for b in reads:
            _b(b).r[key] = val
        for b in writes:
            b = _b(b)
            b.w = (key, val)
            b.r = {}
        if final:
            self.final[key] = max(self.final.get(key, 0), val)
        return key, val

    def barrier(self):
        state = {}
        for n, e in self.engs.items():
            if e["count"] > 0:
                k, v = e["cur"]
                state[k] = v
        for q, pool in self.dma_pool.items():
            for key, uses in pool:
                if uses > 0:
                    state[key] = uses * 16
        for n, e in self.engs.items():
            for k, v in state.items():
                if e["pend"].get(k, 0) < v:
                    e["pend"][k] = v

    def emit_all(self):
        nc = self.nc
        handles = {"pe": "tensor", "act": "scalar", "dve": "vector", "pool": "gpsimd", "sp": "sync"}
        with nc.Block() as block:
            for name, attr in handles.items():
                ops = self.engs[name]["ops"]
                extra = list(self.final.items()) if name == "sp" else []

                def body(eng, ops=ops, extra=extra):
                    for waits, emit, (skey, inc) in ops:
                        for k, v in waits:
                            eng.wait_ge(self._sem(k), v)
                        inst = emit(eng)
                        inst.then_inc(self._sem(skey), inc)
                    for k, v in extra:
                        eng.wait_ge(self._sem(k), v)

                getattr(block, attr)(body)


def _bf(a):
    return np.ascontiguousarray(np.asarray(a, dtype=np.float32)).astype(ml_dtypes.bfloat16)


def _f32(a):
    return np.ascontiguousarray(np.asarray(a, dtype=np.float32))


def fft_consts(L):
    N = 2 * L
    N1 = N // 128
    N1nz = N1 // 2
    CG = 64 // N1nz
    NK1 = N1 // 2 + 1
    RK = CG * NK1
    k1 = np.arange(NK1, dtype=np.float64)
    n1 = np.arange(N1, dtype=np.float64)
    n2 = np.arange(128, dtype=np.float64)
    th1 = TWO_PI * np.outer(n1, k1) / N1
    m1re = np.zeros((CG * N1, RK)); m1im = np.zeros((CG * N1, RK))
    m1re_nz = np.zeros((64, RK)); m1im_nz = np.zeros((64, RK))
    for c in range(CG):
        m1re[c * N1:(c + 1) * N1, c * NK1:(c + 1) * NK1] = np.cos(th1)
        m1im[c * N1:(c + 1) * N1, c * NK1:(c + 1) * NK1] = -np.sin(th1)
        m1re_nz[c * N1nz:(c + 1) * N1nz, c * NK1:(c + 1) * NK1] = np.cos(th1[:N1nz])
        m1im_nz[c * N1nz:(c + 1) * N1nz, c * NK1:(c + 1) * NK1] = -np.sin(th1[:N1nz])
    M1 = np.concatenate([m1re_nz, m1im_nz], axis=1)
    M1full = np.concatenate([m1re, m1im], axis=1)
    thw = TWO_PI * np.outer(n2, k1) / N
    thw = np.tile(thw, (1, CG))
    twA = np.cos(thw)
    twB = np.stack([-np.sin(thw), np.sin(thw)], axis=1)
    tiA = np.cos(thw).T.copy()
    tiB = np.sin(thw).T.copy()
    cw = np.full(NK1, 2.0); cw[0] = 1.0; cw[-1] = 1.0
    thi = TWO_PI * np.outer(k1, n1[:N1nz]) / N1
    GR = np.zeros((RK, 64)); GI = np.zeros((RK, 64))
    for c in range(CG):
        GR[c * NK1:(c + 1) * NK1, c * N1nz:(c + 1) * N1nz] = (cw[:, None] / N) * np.cos(thi)
        GI[c * NK1:(c + 1) * NK1, c * N1nz:(c + 1) * N1nz] = (cw[:, None] / N) * np.sin(thi)
    G3 = np.stack([GR, -GR, -GI], axis=1)
    return dict(N=N, N1=N1, N1nz=N1nz, CG=CG, NK1=NK1, RK=RK, U=DHY // CG,
                M1=_bf(M1), M1full=_bf(M1full), twA=_f32(twA), twB=_f32(twB),
                tiA=_f32(tiA), tiB=_f32(tiB), G3=_bf(G3))


def f2_consts():
    n = np.arange(128, dtype=np.float64)
    th = TWO_PI * np.outer(n, n) / 128
    C = np.cos(th); S = np.sin(th)
    F2 = np.stack([C, S, -S], axis=1)
    IM = np.stack([np.concatenate([C, S], 1), np.concatenate([-C, -S], 1), np.concatenate([-S, C], 1)], axis=1)
    return _bf(F2), _bf(IM)


def filter_consts(L):
    pos = np.arange(L, dtype=np.float64)
    bands = np.linspace(1e-4, 15.0, 16)

    def feats(p):
        t = p / max(L - 1, 1)
        ang = (TWO_PI / L) * p[:, None] * bands[None, :]
        return np.concatenate([t[:, None], np.cos(ang), -np.sin(ang)], axis=1)
    prev = (L - pos) % L
    ff = feats(pos).T
    fr = feats(prev).T
    deltas = np.abs(np.linspace(math.log(1e-2) / 1.5, math.log(1e-2) / 0.3, DHY))
    t = pos / max(L - 1, 1)
    dec_f = np.exp(-t[None, :] * deltas[:, None])
    dec_r = np.exp(-(prev / max(L - 1, 1))[None, :] * deltas[:, None])
    dec_r[:, 0] = 0.0
    return _f32(np.concatenate([ff, fr], axis=1)), _f32(np.concatenate([dec_f, dec_r], axis=1))


OFFS = (-128, -64, 0, 64, 128)
DILS = (1, 4, 16)


def t5_bucket_np(rel):
    half = 16
    max_exact = 8
    ret = np.where(rel > 0, half, 0)
    n = np.abs(rel)
    large = max_exact + (np.log(np.maximum(n, 1).astype(np.float32) / max_exact)
                         / math.log(1024 / max_exact) * (half - max_exact)).astype(np.int32)
    large = np.minimum(large, half - 1)
    return ret + np.where(n < max_exact, n, large)


def bias_onehot():
    oh = np.zeros((33, 3, 512), dtype=np.float32)
    for ci, dil in enumerate(DILS):
        v = np.arange(512)
        rel = 255 - v
        valid = np.abs(rel) <= 64
        bk = t5_bucket_np(rel * dil)
        for vv in range(512):
            if valid[vv]:
                oh[bk[vv], ci, vv] = 1.0
            else:
                oh[32, ci, vv] = -10000.0
    return oh.reshape(33, 3 * 512)


def build_program(debug=False, phases=(0, 1, 2, 3, 4, 5), only_seq=None, sub2=(1, 2, 3, 4, 5), cfgs=(0, 1, 2), p3=9):
    nc = bass.Bass("TRN2", target_bir_lowering=False)
    S = Sched(nc)
    dbg_outs = []

    def din(name, shape, dt=F32):
        return nc.dram_tensor(name, list(shape), dt, kind="ExternalInput").ap()

    def dscr(name, shape, dt):
        kind = "ExternalOutput" if debug else "Internal"
        if debug:
            dbg_outs.append(name)
        return nc.dram_tensor(name, list(shape), dt, kind=kind).ap()

    seqs = [("p", LP), ("s", LS)]
    run_seqs = [q for q in seqs if only_seq is None or q[0] == only_seq]
    FC = {L: fft_consts(L) for _, L in seqs}

    x_d = {"p": din("x_p", [LP, D]), "s": din("x_s", [LS, D])}
    mem_d = {"p": din("mem_p", [NMEM, D]), "s": din("mem_s", [NMEM, D])}
    y_d = {"p": nc.dram_tensor("y_p", [LP, D], F32, kind="ExternalOutput").ap(),
           "s": nc.dram_tensor("y_s", [LS, D], F32, kind="ExternalOutput").ap()}
    w_in_d = din("w_in", [D, DIN])
    w_out_d = din("w_out", [D, D])
    wkv_d = din("w_mem_kv", [D, 512])
    g_in_d = din("g_in", [128, 8]); g_mem_d = din("g_mem", [128, 8])
    gains_d = din("gains", [128, 4])
    convw_d = din("convw", [128, 9, 3]); convb_d = din("convb", [128, 9]); skip_d = din("skipv", [128, 3])
    fw1_d = din("fw1", [33, 64]); fw2_d = din("fw2", [64, 64]); fw3_d = din("fw3", [64, 768])
    fvec_d = din("fvec", [64, 3])
    relb_d = din("rel_bias", [32, 6])
    ident_d = din("ident", [128, 128], BF16); jmat_d = din("jmat", [128, 128], BF16)
    onesblk_d = din("onesblk", [128, 128], BF16)
    f2_d = din("f2", [128, 3, 128], BF16); im_d = din("imat", [128, 3, 256], BF16)
    oh_d = din("bias_oh", [33, 1536])
    fcd = {}
    for sn, L in seqs:
        c = FC[L]
        RK = c["RK"]
        fcd[sn] = dict(M1=din(f"M1_{sn}", [64, 2 * RK], BF16), M1full=din(f"M1f_{sn}", [128, 2 * RK], BF16),
                       twA=din(f"twA_{sn}", [128, RK]), twB=din(f"twB_{sn}", [128, 2, RK]),
                       tiA=din(f"tiA_{sn}", [RK, 128]), tiB=din(f"tiB_{sn}", [RK, 128]),
                       G3=din(f"G3_{sn}", [RK, 3, 64], BF16),
                       feats=din(f"feats_{sn}", [33, 2 * L]), dec=din(f"dec_{sn}", [DHY, 2 * L]))
    scr = {}
    for sn, L in seqs:
        c = FC[L]
        nb = c["U"] // 6
        scr[sn] = dict(
            zhy=dscr(f"zhy_{sn}", [1152, L], BF16), qT=dscr(f"qT_{sn}", [384, L], BF16),
            kT=dscr(f"kT_{sn}", [384, L], BF16), vtok=dscr(f"vtok_{sn}", [L, 768], BF16),
            mqT=dscr(f"mqT_{sn}", [256, L], BF16), gT=dscr(f"gT_{sn}", [1024, L], BF16),
            sT=dscr(f"sT_{sn}", [384, L], BF16), kfilt=dscr(f"kfilt_{sn}", [384, 2 * L], BF16),
            kspec=dscr(f"kspec_{sn}", [nb, 128, 2, 6 * c["RK"]], F32),
            conv=dscr(f"conv_{sn}", [384, L], F32), mixed=dscr(f"mixed_{sn}", [1024, L], BF16))
    htab_d = dscr("htab", [6, 1536], BF16)
    ebd_d = dscr("ebd", [36, 128, 512], BF16)

    def sb(name, shape, dt):
        return T(nc.alloc_sbuf_tensor(name, list(shape), dt))

    _uid = [0]

    def uniq(name):
        _uid[0] += 1
        return f"{name}_{_uid[0]}"

    w_out_bf = sb("w_out_bf", [128, 8, D], BF16)
    ident = sb("ident_sb", [128, 128], BF16); jmat = sb("jmat_sb", [128, 128], BF16)
    onesblk = sb("onesblk_sb", [128, 128], BF16)
    ones_bf = sb("ones_bf", [128, 64], BF16)
    g_in = sb("g_in_sb", [128, 8], F32); g_mem = sb("g_mem_sb", [128, 8], F32)
    gains = sb("gains_sb", [128, 4], F32)
    convw = sb("convw_sb", [128, 9, 3], F32); convb = sb("convb_sb", [128, 9], F32); skipv = sb("skip_sb", [128, 3], F32)
    cst = sb("cst_sb", [128, 4], F32)
    fw1 = sb("fw1_sb", [33, 64], F32); fw2 = sb("fw2_sb", [64, 64], F32); fw3 = sb("fw3_sb", [64, 768], BF16)
    fvec = sb("fvec_sb", [64, 3], F32); fab = sb("fab_sb", [64, 3], F32)
    kmT = {sn: sb(f"kmT_{sn}", [128, 2, NMEM], BF16) for sn, _ in seqs}
    vm = {sn: sb(f"vm_{sn}", [128, 2, 256], BF16) for sn, _ in seqs}
    f2m = sb("f2_sb", [128, 3, 128], BF16); imat = sb("im_sb", [128, 3, 256], BF16)
    pb = [T(nc.alloc_psum_tensor(f"pb{i}", [128, 512], F32)) for i in range(8)]
    pb16 = [T(p.t.bitcast(BF16), p.b) for p in pb]
    es01 = contextlib.ExitStack()
    w_in_bf = T(es01.enter_context(nc.sbuf_tensor("w_in_bf", [128, 8, DIN], BF16)))
    wkv_bf = T(es01.enter_context(nc.sbuf_tensor("wkv_bf", [128, 8, 512], BF16)))

    LD = "sp"
    ST = "pool"

    def ld(out, in_, reads=(), writes=(), q=LD):
        S.dma(q, lambda e: e.dma_start(out=out, in_=in_), reads=reads, writes=writes)

    def st(out, in_, reads=(), writes=(), final=False, q=ST, slow=False):
        if slow:
            S.dma(q, lambda e: e.dma_start(out=out, in_=in_, allow_slow_non_contiguous=True), reads=reads, writes=writes, final=final)
        else:
            S.dma(q, lambda e: e.dma_start(out=out, in_=in_), reads=reads, writes=writes, final=final)

    def act(out, in_, func, reads, writes, bias=None, scale=None, accum_out=None):
        kw = {}
        if bias is not None:
            kw["bias"] = bias
        if scale is not None:
            kw["scale"] = scale
        if accum_out is not None:
            kw["accum_out"] = accum_out
        S.op("act", lambda e: e.activation(out=out, in_=in_, func=func, **kw), reads=reads, writes=writes)

    def tsc(eng, out, in0, s1, s2, op0, op1, reads, writes):
        if s2 is None:
            S.op(eng, lambda e: e.tensor_scalar(out=out, in0=in0, scalar1=s1, scalar2=None, op0=op0), reads=reads, writes=writes)
        else:
            S.op(eng, lambda e: e.tensor_scalar(out=out, in0=in0, scalar1=s1, scalar2=s2, op0=op0, op1=op1), reads=reads, writes=writes)

    def tt(eng, out, in0, in1, op, reads, writes):
        S.op(eng, lambda e: e.tensor_tensor(out=out, in0=in0, in1=in1, op=op), reads=reads, writes=writes)

    def stt(eng, out, in0, scalar, in1, op0, op1, reads, writes):
        S.op(eng, lambda e: e.scalar_tensor_tensor(out=out, in0=in0, scalar=scalar, in1=in1, op0=op0, op1=op1),
             reads=reads, writes=writes)

    def recip(eng, out, in_, reads, writes):
        S.op(eng, lambda e: e.reciprocal(out=out, in_=in_), reads=reads, writes=writes)

    def cp(eng, out, in_, reads, writes):
        S.op(eng, lambda e: e.tensor_copy(out=out, in_=in_), reads=reads, writes=writes)

    def mset(eng, ap, val, writes):
        S.op(eng, lambda e: e.memset(ap, val), writes=writes)

    def rsum(eng, out, in_, reads, writes):
        S.op(eng, lambda e: e.reduce_sum(out=out, in_=in_, axis=mybir.AxisListType.X), reads=reads, writes=writes)

    def mm(out, lhsT, rhs, start, stop, reads, writes):
        def emit(e):
            try:
                return e.matmul(out=out, lhsT=lhsT, rhs=rhs, start=start, stop=stop, skip_group_check=True)
            except Exception:
                print("MATMUL FAIL out", out, "\nlhsT", lhsT, "\nrhs", rhs)
                raise
        S.op("pe", emit, reads=reads, writes=writes)

    def tr(out, in_, reads, writes):
        S.op("pe", lambda e: e.transpose(out=out, in_=in_, identity=ident[:, :]), reads=list(reads) + [ident], writes=writes)

    EBVAR = {"int": (1, 3), "first": (2, 4), "last": (0, 2), "only": (2, 2)}
    EBIDX = {}
    for ci_ in range(3):
        for vn_ in EBVAR:
            for pr_ in range(3):
                EBIDX[(ci_, vn_, pr_)] = len(EBIDX)

    eb_built = [False]

    def build_eb_tiles(tmp):
        items = [(ci, vn, offs, pr) for ci in range(3) for vn, offs in EBVAR.items() for pr in range(3)]
        hm = {}
        for ci in range(3):
            for hd in range(6):
                t = tmp(f"hm{ci}{hd}", [128, 384], BF16)
                hm[(ci, hd)] = t
                ld(t[:, :], bass.AP(htab_d.tensor, hd * 1536 + ci * 512, [[1, 128], [1, 384]]), writes=[t])
        ebs = [tmp(f"ebs{i}", [128, 512], BF16) for i in range(4)]

        def slot(k):
            ci, vn, offs, pr = items[k]
            nk = 1 if vn == "only" else 2
            W2 = 128 * 2 * nk
            pz = pb[5 + k % 3]
            t = ebs[k % 4]
            for ab in range(2):
                for kt in range(nk):
                    sft = 128 - OFFS[offs[kt]]
                    bi = nk * ab + kt
                    h = hm[(ci, 2 * pr + ab)]
                    mm(pz[:, 128 * bi:128 * bi + 128], jmat[:, :], h[:, sft:sft + 128], True, True, [jmat, h], [pz])
            if W2 < 512:
                mset("pool", t[:, W2:512], 0.0, [t])
            act(t[:, 0:W2], pz[:, 0:W2], AF.Exp, [pz], [t])
            st(ebd_d[EBIDX[(ci, vn, pr)]], t[:, :], reads=[t])
        return [(lambda k=k: slot(k)) for k in range(len(items))]

    with contextlib.ExitStack() as es:
        def tmp(name, shape, dt):
            return T(es.enter_context(nc.sbuf_tensor(uniq(name), list(shape), dt)))

        for dst, src in ((ident, ident_d), (jmat, jmat_d), (onesblk, onesblk_d), (g_in, g_in_d), (g_mem, g_mem_d),
                         (gains, gains_d), (convb, convb_d), (skipv, skip_d), (fw1, fw1_d), (fw2, fw2_d), (fvec, fvec_d)):
            ld(dst[:, :], src[:, :], writes=[dst])
        ld(convw[:, :, :], convw_d[:, :, :], writes=[convw])
        ld(f2m[:, :, :], f2_d[:, :, :], writes=[f2m])
        ld(imat[:, :, :], im_d[:, :, :], writes=[imat])
        mset("pool", ones_bf[:, :], 1.0, [ones_bf])
        mset("pool", cst[:, 0:1], 1e-6, [cst])
        mset("pool", cst[:, 1:2], -math.pi, [cst])
        mset("pool", cst[:, 2:3], 1e-12, [cst])
        mset("pool", cst[:, 3:4], 0.0, [cst])
        tsc("dve", gains[:, 0:1], gains[:, 0:1], 0.125, None, ALU.mult, None, [gains], [gains])
        tsc("dve", gains[:, 2:3], gains[:, 2:3], 0.125, None, ALU.mult, None, [gains], [gains])
        tsc("dve", fab[:, 0:1], fvec[:, 1:2], 1.0 / TWO_PI, None, ALU.mult, None, [fvec], [fab])
        tt("dve", fab[:, 1:2], fvec[:, 0:1], fab[:, 0:1], ALU.mult, [fvec, fab], [fab])
        tt("dve", fab[:, 2:3], fvec[:, 2:3], fab[:, 0:1], ALU.mult, [fvec, fab], [fab])
        w3st = tmp("w3st", [64, 768], F32)
        ld(w3st[:, :], fw3_d[:, :], writes=[w3st])
        cp("dve", fw3[:, :], w3st[:, :], [w3st], [fw3])
        wst = [tmp(f"wst{i}", [128, DIN], F32) for i in range(2)]
        n = 0
        for k in range(8):
            w = wst[n % 2]; n += 1
            ld(w[:, :], w_in_d[128 * k:128 * k + 128, :], writes=[w])
            if k % 2 == 0:
                tsc("dve", w_in_bf[:, k, :], w[:, :], g_in[:, k:k + 1], None, ALU.mult, None, [w, g_in], [w_in_bf])
            else:
                act(w_in_bf[:, k, :], w[:, :], AF.Copy, [w, g_in], [w_in_bf], scale=g_in[:, k:k + 1])
        for k in range(8):
            w = wst[n % 2]; n += 1
            ld(w[:, 0:D], w_out_d[128 * k:128 * k + 128, :], writes=[w])
            ld(w[:, D:D + 512], wkv_d[128 * k:128 * k + 128, :], writes=[w])
            cp("dve", w_out_bf[:, k, :], w[:, 0:D], [w], [w_out_bf])
            act(wkv_bf[:, k, :], w[:, D:D + 512], AF.Copy, [w, g_mem], [wkv_bf], scale=g_mem[:, k:k + 1])
        relb = tmp("relb", [33, 6], F32)
        ohs = tmp("ohs", [33, 1536], F32)
        hts = tmp("hts", [6, 1536], BF16)
        mset("pool", relb[:, :], 1.0, [relb])
        ld(relb[0:32, :], relb_d[:, :], writes=[relb])
        ld(ohs[:, :], oh_d[:, :], writes=[ohs])
        for j in range(3):
            mm(pb[j % 2][0:6, 0:512], relb[:, :], ohs[:, 512 * j:512 * j + 512], True, True, [relb, ohs], [pb[j % 2]])
            act(hts[:, 512 * j:512 * j + 512], pb[j % 2][0:6, 0:512], AF.Copy, [pb[j % 2]], [hts])
        st(htab_d[:, :], hts[:, :], reads=[hts])
        S.barrier()

    if 1 in phases:
        with contextlib.ExitStack() as es:
            def tmp(name, shape, dt):
                return T(es.enter_context(nc.sbuf_tensor(uniq(name), list(shape), dt)))

            xin = [tmp(f"xin{i}", [128, D], F32) for i in range(2)]
            ssq = [tmp(f"ssq{i}", [128, 1], F32) for i in range(3)]
            xs = [tmp(f"xs{i}", [128, D], BF16) for i in range(4)]
            xT = [tmp(f"xT{i}", [128, 8, 512], BF16) for i in range(2)]
            Zb = [tmp(f"Zb{j}", [128, 514], F32) for j in range(9)]
            u1b = [tmp(f"u1b{i}", [128, 512], F32) for i in range(3)]
            uvb = [tmp(f"uvb{i}", [128, 512], F32) for i in range(2)]
            x0_st = [tmp(f"x0st{i}", [128, 3, 512], BF16) for i in range(2)]
            s_st = [tmp(f"sst{i}", [128, 3, 512], BF16) for i in range(2)]
            ulast = tmp("ulast", [128, 9], F32)
            lst = tmp("lst", [128, 6, 1], BF16)
            qk_st = [tmp(f"qkst{i}", [128, 6, 512], BF16) for i in range(2)]
            mq_st = [tmp(f"mqst{i}", [128, 2, 512], BF16) for i in range(2)]
            g_st = [tmp(f"gst{i}", [128, 8, 512], BF16) for i in range(2)]
            v_st = [tmp(f"vst{i}", [128, 4, 768], BF16) for i in range(1)]
            for v_ in v_st:
                mset("pool", v_[:, :, :], 1.0, [v_])
            sqb = [tmp(f"sqb{i}", [128, 512], BF16) for i in range(2)]
            rrb = [tmp(f"rrb{i}", [128, 512], F32) for i in range(2)]
            cnt = dict(x=0, pz=0, hn=0, pt=0, uv=0)

            def prep_a(src_rows, slot):
                i = cnt["x"]; cnt["x"] += 1
                xi = xin[i % 2]; sq = ssq[i % 3]; xsb = xs[slot]
                ld(xi[:, :], src_rows, writes=[xi])
                mset("pool", sq[:, :], 0.0, [sq])
                act(xsb[:, :], xi[:, :], AF.Square, [xi], [xsb, sq], accum_out=sq[:, :])
                act(sq[:, :], sq[:, :], AF.Ln, [sq, cst], [sq], bias=cst[:, 0:1], scale=1.0 / D)
                act(sq[:, :], sq[:, :], AF.Exp, [sq], [sq], scale=-0.5)
                tsc("dve", xsb[:, :], xi[:, :], sq[:, 0:1], None, ALU.mult, None, [xi, sq], [xsb])

            def prep_b(slot, xT_t, col0):
                xsb = xs[slot]
                p = cnt["pt"] % 2; cnt["pt"] += 1
                for k in range(8):
                    tr(pb16[p][:, 128 * k:128 * k + 128], xsb[:, 128 * k:128 * k + 128], [xsb], [pb16[p]])
                act(xT_t[:, :, col0:col0 + 128], pb16[p][:, :].rearrange("p (k t) -> p k t", k=8), AF.Copy, [pb16[p]], [xT_t])

            def prep_tile(src_rows, xT_t, col0, ncols_total):
                prep_a(src_rows, cnt["x"] % 4)
                prep_b((cnt["x"] - 1) % 4, xT_t, col0)

            pending = []

            def headnorm(pz, gcol, out_ap, ncols, out_t):
                h = cnt["hn"] % 2; cnt["hn"] += 1
                sq = sqb[h]; rr = rrb[h]; ph = pb[6 + h]
                act(sq[:, 0:ncols], pz[:, 0:ncols], AF.Square, [pz], [sq])

                def part_b():
                    mm(ph[:, 0:ncols], onesblk[:, :], sq[:, 0:ncols], True, True, [onesblk, sq], [ph])
                    act(rr[:, 0:ncols], ph[:, 0:ncols], AF.Ln, [ph, cst], [rr], bias=cst[:, 0:1], scale=1.0 / 64)
                    act(rr[:, 0:ncols], rr[:, 0:ncols], AF.Exp, [rr], [rr], scale=-0.5)
                    stt("dve", out_ap, pz[:, 0:ncols], gains[:, gcol:gcol + 1], rr[:, 0:ncols], ALU.mult, ALU.mult,
                        [pz, gains, rr], [out_t])
                pending.append(part_b)

            def flush_pending():
                while pending:
                    pending.pop(0)()

            def next_pz():
                p = pb[2 + cnt["pz"] % 4]; cnt["pz"] += 1
                return p

            for sn, L in run_seqs:
                sc = scr[sn]
                mT = xT[0]
                for i in range(2):
                    prep_tile(mem_d[sn][128 * i:128 * i + 128, :], mT, 128 * i, 256)
                for j in range(2):
                    pz = next_pz()
                    for k in range(8):
                        mm(pz[:, 0:256], wkv_bf[:, k, 128 * j:128 * j + 128], mT[:, k, 0:256], k == 0, k == 7, [wkv_bf, mT], [pz])
                    headnorm(pz, 3, kmT[sn][:, j, :], 256, kmT[sn])
                    flush_pending()
                for i in range(2):
                    pz = next_pz()
                    for k in range(8):
                        mm(pz[:, 0:256], mT[:, k, 128 * i:128 * i + 128], wkv_bf[:, k, 256:512], k == 0, k == 7, [wkv_bf, mT], [pz])
                    act(vm[sn][:, i, :], pz[:, 0:256], AF.Copy, [pz], [vm[sn]])
                nch = L // 512

                def prep_a_tile(c, i):
                    r0 = 512 * c + 128 * i
                    prep_a(x_d[sn][r0:r0 + 128, :], i)

                def prep_b_chunk(c):
                    for i in range(4):
                        prep_b(i, xT[(c + 1) % 2], 128 * i)

                def main_chunk(c):
                    xt = xT[(c + 1) % 2]
                    c0 = 512 * c
                    x0s = x0_st[c % 2]; sst = s_st[c % 2]
                    qs = qk_st[c % 2]; ms = mq_st[c % 2]; gs = g_st[c % 2]; vs = v_st[0]
                    order = [9, 0, 10, 1, 11, 2, 12, 3, 13, 4, 14, 5, 18, 6, 19, 7, 8] + list(range(20, 28))
                    for jn, j in enumerate(order):
                        if jn in (3, 9, 15, 21) and c + 2 < nch:
                            prep_a_tile(c + 2, (jn - 3) // 6)
                        pz = next_pz()
                        for k in range(8):
                            mm(pz[:, :], w_in_bf[:, k, 128 * j:128 * j + 128], xt[:, k, :], k == 0, k == 7, [w_in_bf, xt], [pz])
                        flush_pending()
                        if j < 9:
                            zb = Zb[j]
                            act(zb[:, 2:514], pz[:, :], AF.Copy, [pz], [zb])
                            if j < 3:
                                u = uvb[cnt["uv"] % 2]; cnt["uv"] += 1
                            elif j < 6:
                                u = u1b[j - 3]
                            else:
                                u = uvb[cnt["uv"] % 2]; cnt["uv"] += 1
                            if j % 3 == 1:
                                tsc("dve", u[:, :], zb[:, 1:513], convw[:, j, 1:2], convb[:, j:j + 1], ALU.mult, ALU.add, [zb, convw, convb], [u])
                            else:
                                act(u[:, :], zb[:, 1:513], AF.Identity, [zb, convw, convb], [u], bias=convb[:, j:j + 1], scale=convw[:, j, 1:2])
                            stt("dve", u[:, :], zb[:, 0:512], convw[:, j, 0:1], u[:, :], ALU.mult, ALU.add, [zb, convw, u], [u])
                            if j < 3:
                                stt("dve", x0s[:, j, :], zb[:, 2:514], convw[:, j, 2:3], u[:, :], ALU.mult, ALU.add, [zb, convw, u], [x0s])
                            else:
                                stt("dve", u[:, :], zb[:, 2:514], convw[:, j, 2:3], u[:, :], ALU.mult, ALU.add, [zb, convw, u], [u])
                            if j >= 6:
                                tt("pool", sst[:, j - 6, :], u1b[j - 6][:, :], u[:, :], ALU.mult, [u1b[j - 6], u], [sst])
                            cp("dve", zb[:, 0:2], zb[:, 512:514], [zb], [zb])
                        elif j < 12:
                            headnorm(pz, 0, qs[:, j - 9, :], 512, qs)
                        elif j < 15:
                            headnorm(pz, 1, qs[:, j - 9, :], 512, qs)
                        elif j < 20:
                            headnorm(pz, 2, ms[:, j - 18, :], 512, ms)
                        else:
                            act(gs[:, j - 20, :], pz[:, :], AF.Silu, [pz], [gs])
                    flush_pending()
                    for i in range(4):
                        pz = next_pz()
                        for k in range(8):
                            mm(pz[:, 0:384], xt[:, k, 128 * i:128 * i + 128], w_in_bf[:, k, 1920:2304], k == 0, k == 7, [w_in_bf, xt], [pz])
                        vo = bass.AP(vs.t, i * 768, [[4 * 768, 128], [256, 3], [192, 2], [1, 64]])
                        act(vo, pz[:, 0:384].rearrange("p (a b e) -> p a b e", a=3, b=2), AF.Copy, [pz], [vs])
                    if c == 0:
                        st(sc["zhy"][0:384, 0:511].rearrange("(j p) t -> p j t", p=128), x0s[:, :, 1:512], reads=[x0s])
                        st(sc["sT"][:, 0:511].rearrange("(j p) t -> p j t", p=128), sst[:, :, 1:512], reads=[sst])
                    else:
                        st(sc["zhy"][0:384, c0 - 1:c0 + 511].rearrange("(j p) t -> p j t", p=128), x0s[:, :, :], reads=[x0s])
                        st(sc["sT"][:, c0 - 1:c0 + 511].rearrange("(j p) t -> p j t", p=128), sst[:, :, :], reads=[sst])
                    st(sc["qT"][:, c0:c0 + 512].rearrange("(j p) t -> p j t", p=128), qs[:, 0:3, :], reads=[qs])
                    st(sc["kT"][:, c0:c0 + 512].rearrange("(j p) t -> p j t", p=128), qs[:, 3:6, :], reads=[qs])
                    st(sc["mqT"][:, c0:c0 + 512].rearrange("(j p) t -> p j t", p=128), ms[:, :, :], reads=[ms])
                    st(sc["gT"][:, c0:c0 + 512].rearrange("(j p) t -> p j t", p=128), gs[:, :, :], reads=[gs])
                    st(sc["vtok"][c0:c0 + 512, :].rearrange("(i p) e -> p i e", p=128), vs[:, :, :], reads=[vs])

                for zb in Zb:
                    mset("pool", zb[:, 0:2], 0.0, [zb])
                for i in range(4):
                    prep_a_tile(0, i)
                prep_b_chunk(0)
                if nch > 1:
                    for i in range(4):
                        prep_a_tile(1, i)
                for c in range(nch):
                    if c + 1 < nch:
                        prep_b_chunk(c + 1)
                    main_chunk(c)
                for j in range(9):
                    act(ulast[:, j:j + 1], Zb[j][:, 1:2], AF.Identity, [Zb[j], convw, convb], [ulast],
                        bias=convb[:, j:j + 1], scale=convw[:, j, 1:2])
                    stt("dve", ulast[:, j:j + 1], Zb[j][:, 0:1], convw[:, j, 0:1], ulast[:, j:j + 1], ALU.mult, ALU.add,
                        [Zb[j], convw, ulast], [ulast])
                cp("dve", lst[:, 0:3, 0], ulast[:, 0:3], [ulast], [lst])
                tt("dve", lst[:, 3:6, 0], ulast[:, 3:6], ulast[:, 6:9], ALU.mult, [ulast], [lst])
                st(sc["zhy"][0:384, L - 1:L].rearrange("(j p) o -> p j o", p=128), lst[:, 0:3, :], reads=[lst], slow=True)
                st(sc["sT"][:, L - 1:L].rearrange("(j p) o -> p j o", p=128), lst[:, 3:6, :], reads=[lst], slow=True)
            S.barrier()

    es01.close()

    if 2 in phases:
        for sn, L in run_seqs:
            sc = scr[sn]; fc = FC[L]; fd = fcd[sn]
            RK = fc["RK"]; U = fc["U"]; NB = U // 6; N2L = 2 * L
            with contextlib.ExitStack() as es:
                def tmp(name, shape, dt):
                    return T(es.enter_context(nc.sbuf_tensor(uniq(name), list(shape), dt)))
                h2 = tmp("h2", [64, N2L], BF16)
                fts = [tmp(f"fts{i}", [33, 512], F32) for i in range(2)]
                ysb = [tmp(f"ysb{i}", [64, 512], F32) for i in range(2)]
                kib = [tmp(f"kib{i}", [64, 512], I32) for i in range(2)]
                msk = [tmp(f"msk{i}", [64, 512], F32) for i in range(2)]
                h1 = [tmp(f"h1{i}", [64, 512], F32) for i in range(2)]
                kraw = tmp("kraw", [128, N2L], F32)
                dcs = [tmp(f"dcs{i}", [128, 2048], F32) for i in range(3)]
                ndc = [0]
                kst = [tmp(f"kst{i}", [128, 2048], BF16) for i in range(2)]
                nrm = tmp("nrm", [128, 16], F32)
                nch2 = N2L // 512

                ysb2 = [tmp(f"ysc{i}", [64, 512], F32) for i in range(2)]
                kib2 = [tmp(f"kic{i}", [64, 512], I32) for i in range(2)]
                msk2 = [tmp(f"msc{i}", [64, 512], F32) for i in range(2)]

                def sin_ops(pz, bcol, out_ap, out_t, y, ki, m):
                    return [
                        lambda: tsc("dve", y[:, :], pz[0:64, :], fab[:, 0:1], fab[:, bcol:bcol + 1], ALU.mult, ALU.add, [pz, fab], [y]),
                        lambda: cp("dve", ki[:, :], y[:, :], [y], [ki]),
                        lambda: tt("dve", y[:, :], y[:, :], ki[:, :], ALU.subtract, [y, ki], [y]),
                        lambda: act(out_ap, y[:, :], AF.Sin, [y], [out_t], scale=6.28318),
                    ]

                def l1_ops(c):
                    f = fts[c % 2]; pz = pb[c % 2]
                    return [lambda: (ld(f[:, :], fd["feats"][:, 512 * c:512 * c + 512], writes=[f]),
                                     mm(pz[0:64, :], fw1[:, :], f[:, :], True, True, [fw1, f], [pz]))] + \
                        sin_ops(pz, 1, h1[c % 2][:, :], h1[c % 2], ysb[c % 2], kib[c % 2], msk[c % 2])

                def l2_ops(c):
                    pz2 = pb[2 + c % 2]
                    return [lambda: mm(pz2[0:64, :], fw2[:, :], h1[c % 2][:, :], True, True, [fw2, h1[c % 2]], [pz2])] + \
                        sin_ops(pz2, 2, h2[:, 512 * c:512 * c + 512], h2, ysb2[c % 2], kib2[c % 2], msk2[c % 2])

                for op_ in l1_ops(0):
                    op_()
                for c in range(nch2):
                    A = l1_ops(c + 1) if c + 1 < nch2 else []
                    B = l2_ops(c)
                    for i in range(max(len(A), len(B))):
                        if i < len(A):
                            A[i]()
                        if i < len(B):
                            B[i]()
                for ct in range(3):
                    for c in range(nch2):
                        dc = dcs[(ndc[0] + c // 4) % 3]
                        if c % 4 == 0:
                            ld(dc[:, :], fd["dec"][128 * ct:128 * ct + 128, 512 * c:512 * c + 2048], writes=[dc])
                        pz = pb[c % 4]
                        col = 128 * ct if c < nch2 // 2 else 384 + 128 * ct
                        mm(pz[:, :], fw3[:, col:col + 128], h2[:, 512 * c:512 * c + 512], True, True, [fw3, h2], [pz])
                        tt("dve", kraw[:, 512 * c:512 * c + 512], pz[:, :], dc[:, 512 * (c % 4):512 * (c % 4) + 512], ALU.mult, [pz, dc], [kraw])
                    ndc[0] += nch2 // 4
                    pz = pb[6]
                    mm(pz[:, 0:8], fw3[:, 384 + 128 * ct:384 + 128 * ct + 128], h2[:, 0:8], True, True, [fw3, h2], [pz])
                    tt("dve", kraw[:, 0:1], kraw[:, 0:1], pz[:, 0:1], ALU.add, [kraw, pz], [kraw])
                    npc = N2L // 2048
                    mset("pool", nrm[:, :], 0.0, [nrm])
                    for q in range(npc):
                        act(kst[q % 2][:, :], kraw[:, 2048 * q:2048 * q + 2048], AF.Square, [kraw], [kst[q % 2], nrm],
                            accum_out=nrm[:, q:q + 1])
                    rsum("dve", nrm[:, 15:16], nrm[:, 0:npc], [nrm], [nrm])
                    act(nrm[:, 14:15], nrm[:, 15:16], AF.Sqrt, [nrm, cst], [nrm], bias=cst[:, 2:3], scale=1.0)
                    recip("dve", nrm[:, 13:14], nrm[:, 14:15], [nrm], [nrm])
                    for q in range(npc):
                        ks = kst[q % 2]
                        if q % 2 == 0:
                            tsc("dve", ks[:, :], kraw[:, 2048 * q:2048 * q + 2048], nrm[:, 13:14], None, ALU.mult, None, [kraw, nrm], [ks])
                        else:
                            act(ks[:, :], kraw[:, 2048 * q:2048 * q + 2048], AF.Copy, [kraw, nrm], [ks], scale=nrm[:, 13:14])
                        st(sc["kfilt"][128 * ct:128 * ct + 128, 2048 * q:2048 * q + 2048], ks[:, :], reads=[ks])
            S.barrier()
            with contextlib.ExitStack() as es:
                def tmp(name, shape, dt):
                    return T(es.enter_context(nc.sbuf_tensor(uniq(name), list(shape), dt)))
                M1 = tmp("M1", [64, 2 * RK], BF16); M1f = tmp("M1f", [128, 2 * RK], BF16)
                twA = tmp("twA", [128, RK], F32); twB = tmp("twB", [128, 2, RK], F32)
                tiA = tmp("tiA", [RK, 128], F32); tiB = tmp("tiB", [RK, 128], F32)
                G3 = tmp("G3", [RK, 3, 64], BF16)
                for dst, key in ((M1, "M1"), (M1f, "M1full"), (twA, "twA"), (tiA, "tiA"), (tiB, "tiB")):
                    ld(dst[:, :], fd[key][:, :], writes=[dst])
                ld(twB[:, :, :], fd["twB"][:, :, :], writes=[twB])
                ld(G3[:, :, :], fd["G3"][:, :, :], writes=[G3])
                xu = [tmp(f"xu{i}", [128, 6, 128], BF16) for i in range(2)]
                Tb = [tmp(f"Tb{i}", [128, 4, 6, RK], BF16) for i in range(2)]
                ksp = [tmp(f"ksp{i}", [128, 2, 6 * RK], F32) for i in range(2)]
                Pb = [tmp(f"Pb{i}", [128, 4, 6 * RK], BF16) for i in range(2)]
                Qb = [tmp(f"Qb{i}", [RK, 4, 6, 128], BF16) for i in range(2)]
                yst = [tmp(f"yst{i}", [64, 6, 128], F32) for i in range(2)]
                W6 = 6 * RK

                def bc_ap(t, dims):
                    h = t.t
                    return bass.AP(h, 0, [[int(np.prod(h.shape[1:])), int(h.shape[0])]] + [[s, n] for s, n in dims])

                def stage_a(b, src_view, Mmat, npart):
                    x = xu[b % 2]; Tt = Tb[b % 2]
                    ld(x[0:npart, :, :], src_view, writes=[x])
                    for g in range(2):
                        pa = pb[g]
                        for j in range(3):
                            u = 3 * g + j
                            mm(pa[:, 2 * RK * j:2 * RK * (j + 1)], x[0:npart, u, :], Mmat[0:npart, :], True, True, [x, Mmat], [pa])
                        in0 = pa[:, 0:6 * RK].rearrange("p (u r k) -> p u r k", u=3, r=2)
                        o1 = bass.AP(Tt.t, 3 * g * RK, [[4 * 6 * RK, 128], [RK, 3], [3 * 6 * RK, 2], [1, RK]])
                        i1 = bass.AP(twA.t, 0, [[RK, 128], [0, 3], [0, 2], [1, RK]])
                        tt("dve", o1, in0, i1, ALU.mult, [pa, twA], [Tt])
                        o2 = bass.AP(Tt.t, 6 * RK + 3 * g * RK, [[4 * 6 * RK, 128], [RK, 3], [6 * RK, 2], [1, RK]])
                        i2 = bass.AP(twB.t, 0, [[2 * RK, 128], [0, 3], [RK, 2], [1, RK]])
                        tt("dve", o2, in0, i2, ALU.mult, [pa, twB], [Tt])

                def stage_b_mm(b):
                    Tt = Tb[b % 2]
                    xr = pb[2]; xi = pb[3]

                    def blk(i):
                        return Tt[:, i, :, :].rearrange("p u k -> p (u k)")
                    seq = [(0, 0, xr), (0, 2, xr), (1, 1, xr), (1, 3, xr), (0, 1, xi), (0, 3, xi), (2, 0, xi), (2, 2, xi)]
                    started = set()
                    for mi, bi, dst in seq:
                        first = id(dst) not in started
                        started.add(id(dst))
                        mm(dst[:, 0:W6], f2m[:, mi, :], blk(bi), first, False, [f2m, Tt], [dst])
                    return xr, xi

                kf_view = sc["kfilt"].rearrange("c (n1 n2) -> (c n1) n2", n2=128).rearrange("(u p) n2 -> p u n2", p=128)

                def filt_b(b):
                    xr, xi = stage_b_mm(b)
                    ks = ksp[b % 2]
                    act(ks[:, 0, :], xr[:, 0:W6], AF.Copy, [xr], [ks])
                    act(ks[:, 1, :], xi[:, 0:W6], AF.Copy, [xi], [ks])
                    st(sc["kspec"][b], ks[:, :, :], reads=[ks], writes=[ksbuf[b]])

                ksbuf = [Buf(f"kspec{b}") for b in range(NB)]
                for t in range(NB + 1):
                    if t < NB:
                        stage_a(t, kf_view[:, 6 * t:6 * t + 6, :], M1f, 128)
                    if t >= 1:
                        filt_b(t - 1)
                s_view = sc["sT"].rearrange("c (n1 n2) -> (c n1) n2", n2=128).rearrange("(u p) n2 -> p u n2", p=64)
                c_view = sc["conv"].rearrange("c (n1 n2) -> (c n1) n2", n2=128).rearrange("(u p) n2 -> p u n2", p=64)

                def sig_a(b):
                    ks = ksp[b % 2]
                    ld(ks[:, :, :], sc["kspec"][b], reads=[ksbuf[b]], writes=[ks])
                    stage_a(b, s_view[:, 6 * b:6 * b + 6, :], M1, 64)

                def sig_b(b):
                    ks = ksp[b % 2]
                    xr, xi = stage_b_mm(b)
                    Pt = Pb[b % 2]
                    tt("dve", Pt[:, 0, :], xr[:, 0:W6], ks[:, 0, :], ALU.mult, [xr, ks], [Pt])
                    tt("dve", Pt[:, 2, :], xr[:, 0:W6], ks[:, 1, :], ALU.mult, [xr, ks], [Pt])
                    tt("dve", Pt[:, 1, :], xi[:, 0:W6], ks[:, 1, :], ALU.mult, [xi, ks], [Pt])
                    tt("dve", Pt[:, 3, :], xi[:, 0:W6], ks[:, 0, :], ALU.mult, [xi, ks], [Pt])

                def sig_c(b):
                    Pt = Pb[b % 2]; Qt = Qb[b % 2]
                    for g in range(3):
                        pc = pb[4 + g]
                        for j in range(2):
                            u = 2 * g + j
                            for bi, mi in ((0, 0), (1, 1), (2, 2), (3, 2)):
                                mm(pc[0:RK, 256 * j:256 * j + 256], Pt[:, bi, RK * u:RK * u + RK], imat[:, mi, :],
                                   bi == 0, bi == 3, [Pt, imat], [pc])
                        in0 = pc[0:RK, :].rearrange("p (u r n) -> p u r n", u=2, r=2)
                        oA = bass.AP(Qt.t, 2 * g * 128, [[4 * 6 * 128, RK], [128, 2], [6 * 128, 2], [1, 128]])
                        iA = bass.AP(tiA.t, 0, [[128, RK], [0, 2], [0, 2], [1, 128]])
                        tt("dve", oA, in0, iA, ALU.mult, [pc, tiA], [Qt])
                        oB = bass.AP(Qt.t, 2 * 6 * 128 + 2 * g * 128, [[4 * 6 * 128, RK], [128, 2], [6 * 128, 2], [1, 128]])
                        iB = bass.AP(tiB.t, 0, [[128, RK], [0, 2], [0, 2], [1, 128]])
                        tt("dve", oB, in0, iB, ALU.mult, [pc, tiB], [Qt])

                def sig_d(b):
                    Qt = Qb[b % 2]
                    ys = yst[b % 2]
                    py = pb[7]
                    for hlf in range(2):
                        for bi, gi in ((0, 0), (3, 1), (2, 2), (1, 2)):
                            mm(py[0:64, 0:384], G3[:, gi, :], Qt[:, bi, 3 * hlf:3 * hlf + 3, :].rearrange("p u n -> p (u n)"),
                               bi == 0, bi == 1, [G3, Qt], [py])
                        act(ys[:, 3 * hlf:3 * hlf + 3, :], py[0:64, 0:384].rearrange("p (u n) -> p u n", u=3), AF.Copy, [py], [ys])
                    st(c_view[:, 6 * b:6 * b + 6, :], ys[:, :, :], reads=[ys])

                for t in range(NB + 3):
                    if t < NB:
                        sig_a(t)
                    if 0 <= t - 1 < NB:
                        sig_b(t - 1)
                    if 0 <= t - 2 < NB:
                        sig_c(t - 2)
                    if 0 <= t - 3 < NB:
                        sig_d(t - 3)
            S.barrier()
            with contextlib.ExitStack() as es:
                def tmp(name, shape, dt):
                    return T(es.enter_context(nc.sbuf_tensor(uniq(name), list(shape), dt)))
                PC = 2048
                zp = [tmp(f"zq{i}", [128, PC + 2], BF16) for i in range(3)]
                uc = [tmp(f"ue{i}", [128, PC], F32) for i in range(3)]
                sq_ = [tmp(f"sq{i}", [128, PC], BF16) for i in range(3)]
                gq_ = [tmp(f"gq{i}", [128, PC], BF16) for i in range(3)]
                cq_ = [tmp(f"cq{i}", [128, PC], F32) for i in range(3)]
                mq_ = [tmp(f"mq{i}", [128, PC], BF16) for i in range(3)]
                n2e = 0
                for ct in range(3):
                    for piece in range(L // PC):
                        a = PC * piece
                        z = zp[n2e % 3]; u = uc[n2e % 3]; sv = sq_[n2e % 3]; gv = gq_[n2e % 3]; cv = cq_[n2e % 3]; mv = mq_[n2e % 3]
                        n2e += 1
                        ld(z[:, 0:PC], sc["zhy"][128 * ct:128 * ct + 128, a:a + PC], writes=[z])
                        ld(sv[:, :], sc["sT"][128 * ct:128 * ct + 128, a:a + PC], writes=[sv])
                        ld(gv[:, :], sc["gT"][128 * ct:128 * ct + 128, a:a + PC], writes=[gv])
                        ld(cv[:, :], sc["conv"][128 * ct:128 * ct + 128, a:a + PC], writes=[cv])
                        stt("dve", cv[:, :], sv[:, :], skipv[:, ct:ct + 1], cv[:, :], ALU.mult, ALU.add, [sv, skipv, cv], [cv])
                        tt("pool", u[:, :], z[:, 0:PC], cv[:, :], ALU.mult, [z, cv], [u])
                        tt("dve", mv[:, :], u[:, :], gv[:, :], ALU.mult, [u, gv], [mv])
                        st(sc["mixed"][128 * ct:128 * ct + 128, a:a + PC], mv[:, :], reads=[mv])
            S.barrier()

    if 3 in phases:
        with contextlib.ExitStack() as es:
            def tmp(name, shape, dt):
                return T(es.enter_context(nc.sbuf_tensor(uniq(name), list(shape), dt)))
            if not eb_built[0]:
                eb_built[0] = True
                with contextlib.ExitStack() as es2:
                    for sl in build_eb_tiles(lambda name, shape, dt: T(es2.enter_context(nc.sbuf_tensor(uniq(name), list(shape), dt)))):
                        sl()
                    S.barrier()
            eb = {}
            for key, idx in EBIDX.items():
                t = tmp(f"eb{idx}", [128, 4, 128], BF16)
                eb[key] = t
                ld(t[:, :, :].rearrange("p a b -> p (a b)"), ebd_d[idx], writes=[t])
            SR = 2048
            qsb = tmp("qsb", [128, 3, SR], BF16)
            ksb = tmp("ksb", [128, 3, 3 * SR], BF16)
            gsb = tmp("gsb", [128, 3, SR], BF16)
            acc = tmp("acc", [128, 3, 2, SR], F32)
            dal = tmp("dal", [128, SR], F32)
            msb = tmp("msb", [128, SR], BF16)
            NV = 8
            vsb = [tmp(f"vsb{i}", [128, 768], BF16) for i in range(NV)]
            pra = [tmp(f"pra{i}", [128, 512], BF16) for i in range(4)]
            ptb = [tmp(f"ptb{i}", [128, 512], BF16) for i in range(4)]
            vn_ctr = [0]
            it = [0]
            for sn, L in run_seqs:
                sc = scr[sn]
                for sr in range(L // SR):
                    T0 = sr * SR
                    w0 = max(0, T0 - 2048); w1 = min(L, T0 + SR + 2048)
                    ld(qsb[:, :, :], sc["qT"][:, T0:T0 + SR].rearrange("(j p) t -> p j t", p=128), writes=[qsb])
                    ld(ksb[:, :, 0:w1 - w0], sc["kT"][:, w0:w1].rearrange("(j p) t -> p j t", p=128), writes=[ksb])
                    ld(gsb[:, :, :], sc["gT"][384:768, T0:T0 + SR].rearrange("(j p) t -> p j t", p=128), writes=[gsb])
                    jobs = []
                    for ci, dil in enumerate(DILS):
                        n = L // dil
                        for r in range(dil):
                            for jt in range(SR // dil // 128):
                                bq = T0 // dil + 128 * jt
                                if n == 128:
                                    vn, kstart, nkt = "only", 0, 1
                                elif bq == 0:
                                    vn, kstart, nkt = "first", 0, 2
                                elif bq + 128 == n:
                                    vn, kstart, nkt = "last", n - 256, 2
                                else:
                                    vn, kstart, nkt = "int", bq - 64, 2
                                for pr in range(3):
                                    jobs.append((ci, dil, r, bq, vn, kstart, nkt, pr))
                    vcache = {}
                    state = {}

                    def s_stage(j):
                        ci, dil, r, bq, vn, kstart, nkt, pr = jobs[j]
                        vts = []
                        for kt in range(nkt):
                            key = (ci, r, kstart + 128 * kt)
                            if key not in vcache:
                                vt = vsb[vn_ctr[0] % NV]; vn_ctr[0] += 1
                                for kk in [k for k, v in vcache.items() if v is vt]:
                                    del vcache[kk]
                                tok0 = r + dil * (kstart + 128 * kt)
                                src = bass.AP(sc["vtok"].tensor, tok0 * 768, [[dil * 768, 128], [1, 768]])
                                ld(vt[:, :], src, writes=[vt])
                                vcache[key] = vt
                            vts.append(vcache[key])
                        qcol0 = r + dil * (bq - T0 // dil)
                        qsl = slice(qcol0, qcol0 + dil * 127 + 1, dil)
                        i = it[0]; it[0] += 1
                        psab = (pb[2 * (i % 3)], pb[2 * (i % 3) + 1])
                        pr_raw = pra[i % 4]; pt = ptb[i % 4]
                        W = 128 * nkt
                        for kt in range(nkt):
                            for ab in range(2):
                                kc0 = r + dil * (kstart + 128 * kt) - w0
                                assert kc0 >= 0 and kc0 + dil * 127 < w1 - w0
                                ksl = slice(kc0, kc0 + dil * 127 + 1, dil)
                                mm(psab[ab][:, 128 * kt:128 * kt + 128],
                                   ksb[64 * ab:64 * ab + 64, pr, ksl], qsb[64 * ab:64 * ab + 64, pr, qsl],
                                   True, True, [ksb, qsb], [psab[ab]])
                        for ab in range(2):
                            act(pr_raw[:, W * ab:W * ab + W], psab[ab][:, 0:W], AF.Exp, [psab[ab]], [pr_raw])
                        e = eb[(ci, vn, pr)]
                        tt("dve", pt[:, 0:2 * W], pr_raw[:, 0:2 * W], e[:, :, :].rearrange("p a b -> p (a b)")[:, 0:2 * W], ALU.mult,
                           [pr_raw, e], [pt])
                        state[j] = (vts, pt, qcol0, i)

                    def pv_stage(j):
                        ci, dil, r, bq, vn, kstart, nkt, pr = jobs[j]
                        vts, pt, qcol0, i = state.pop(j)
                        pnd = pb[6 + i % 2]
                        first = True
                        for ab in range(2):
                            for kt in range(nkt):
                                bi = nkt * ab + kt
                                hcol = (2 * pr + ab) * 128
                                mm(pnd[:, 128 * ab:128 * ab + 128], vts[kt][:, hcol:hcol + 128], pt[:, 128 * bi:128 * bi + 128],
                                   first, False, [vts[kt], pt], [pnd])
                                first = False
                        av = bass.AP(acc.t, pr * 2 * SR + qcol0, [[3 * 2 * SR, 128], [SR, 2], [dil, 128]])
                        pv2 = pnd[:, 0:256].rearrange("p (a q) -> p a q", a=2)
                        if ci == 0:
                            act(av, pv2, AF.Copy, [pnd], [acc])
                        else:
                            tt("dve", av, pv2, av, ALU.add, [pnd, acc], [acc])

                    LA = 3
                    for j in range(min(LA, len(jobs))):
                        s_stage(j)
                    for j in range(len(jobs)):
                        if j + LA < len(jobs):
                            s_stage(j + LA)
                        pv_stage(j)
                    for pr in range(3):
                        ld(dal[0:64, :], acc[64:128, pr, 0, :], reads=[acc], writes=[dal])
                        ld(dal[64:128, :], acc[0:64, pr, 1, :], reads=[acc], writes=[dal])
                        act(dal[:, :], dal[:, :], AF.Ln, [dal], [dal])
                        act(dal[:, :], dal[:, :], AF.Exp, [dal], [dal], scale=-1.0)
                        tt("dve", dal[0:64, :], acc[0:64, pr, 0, :], dal[0:64, :], ALU.mult, [acc, dal], [dal])
                        tt("pool", dal[64:128, :], acc[64:128, pr, 1, :], dal[64:128, :], ALU.mult, [acc, dal], [dal])
                        tt("dve", msb[:, :], dal[:, :], gsb[:, pr, :], ALU.mult, [dal, gsb], [msb])
                        st(sc["mixed"][384 + 128 * pr:384 + 128 * pr + 128, T0:T0 + SR], msb[:, :], reads=[msb])
            S.barrier()

    if 4 in phases or 5 in phases:
        with contextlib.ExitStack() as es:
            def tmp(name, shape, dt):
                return T(es.enter_context(nc.sbuf_tensor(uniq(name), list(shape), dt)))
            mqs = [tmp(f"mqs{i}", [128, 2, 512], BF16) for i in range(2)]
            gms = [tmp(f"gms{i}", [128, 2, 512], BF16) for i in range(2)]
            pms = [tmp(f"pms{i}", [128, 4, 512], BF16) for i in range(2)]
            rdn = [tmp(f"rdn{i}", [128, 512], F32) for i in range(2)]
            mx = [tmp(f"mx{i}", [128, 8, 512], BF16) for i in range(2)]
            xr = [tmp(f"xr{i}", [128, D], F32) for i in range(3)]
            yo = [tmp(f"yo{i}", [128, D], F32) for i in range(2)]
            chunks = [(sn, L, c) for sn, L in run_seqs for c in range(L // 512)]
            n5 = [0]

            def p4_s(ci):
                sn, L, c = chunks[ci]
                sc = scr[sn]; c0 = 512 * c
                mq = mqs[ci % 2]; gm = gms[ci % 2]; m = mx[ci % 2]
                ld(mq[:, :, :], sc["mqT"][:, c0:c0 + 512].rearrange("(j p) t -> p j t", p=128), writes=[mq])
                ld(gm[:, :, :], sc["gT"][768:1024, c0:c0 + 512].rearrange("(j p) t -> p j t", p=128), writes=[gm])
                ld(m[:, 0:6, :], sc["mixed"][0:768, c0:c0 + 512].rearrange("(k p) t -> p k t", p=128), writes=[m])

            def p4_qk(ci, pr):
                sn, L, c = chunks[ci]
                mq = mqs[ci % 2]; pm = pms[pr]
                for ab in range(2):
                    for mt in range(2):
                        ps = pb[2 * ab + mt]
                        mm(ps[:, :], kmT[sn][64 * ab:64 * ab + 64, pr, 128 * mt:128 * mt + 128], mq[64 * ab:64 * ab + 64, pr, :],
                           True, True, [kmT[sn], mq], [ps])
                        act(pm[:, 2 * ab + mt, :], ps[:, :], AF.Exp, [ps], [pm])

            def p4_pv(ci, pr):
                sn, L, c = chunks[ci]
                gm = gms[ci % 2]; m = mx[ci % 2]
                pm = pms[pr]; rd = rdn[pr]
                pn = pb[4]; pd = pb[5]
                for ab in range(2):
                    for mt in range(2):
                        mm(pn[64 * ab:64 * ab + 64, :], vm[sn][:, mt, 128 * pr + 64 * ab:128 * pr + 64 * ab + 64], pm[:, 2 * ab + mt, :],
                           mt == 0, mt == 1, [vm[sn], pm], [pn])
                        mm(pd[64 * ab:64 * ab + 64, :], ones_bf[:, :], pm[:, 2 * ab + mt, :], mt == 0, mt == 1, [ones_bf, pm], [pd])
                act(rd[:, :], pd[:, :], AF.Ln, [pd], [rd])
                act(rd[:, :], rd[:, :], AF.Exp, [rd], [rd], scale=-1.0)
                tt("dve", rd[:, :], pn[:, :], rd[:, :], ALU.mult, [pn, rd], [rd])
                tt("pool", m[:, 6 + pr, :], rd[:, :], gm[:, pr, :], ALU.mult, [rd, gm], [m])

            def p5_tile(ci, i):
                sn, L, c = chunks[ci]
                c0 = 512 * c
                m = mx[ci % 2]
                r0 = c0 + 128 * i
                x = xr[n5[0] % 3]; y = yo[n5[0] % 2]
                ld(x[:, :], x_d[sn][r0:r0 + 128, :], writes=[x])
                for hf in range(2):
                    pz = pb[6 + hf]
                    for k in range(8):
                        mm(pz[:, :], m[:, k, 128 * i:128 * i + 128], w_out_bf[:, k, 512 * hf:512 * hf + 512], k == 0, k == 7,
                           [m, w_out_bf], [pz])
                    tt("dve", y[:, 512 * hf:512 * hf + 512], pz[:, :], x[:, 512 * hf:512 * hf + 512], ALU.add, [pz, x], [y])
                st(y_d[sn][r0:r0 + 128, :], y[:, :], reads=[y], final=True)
                n5[0] += 1

            p4_s(0)
            for pr in range(2):
                p4_qk(0, pr)
                p4_pv(0, pr)
            for ci in range(len(chunks)):
                nxt = ci + 1 < len(chunks)
                if nxt:
                    p4_s(ci + 1)
                    p4_qk(ci + 1, 0)
                p5_tile(ci, 0)
                p5_tile(ci, 1)
                if nxt:
                    p4_pv(ci + 1, 0)
                    p4_qk(ci + 1, 1)
                p5_tile(ci, 2)
                p5_tile(ci, 3)
                if nxt:
                    p4_pv(ci + 1, 1)

    S.emit_all()
    return nc, dbg_outs


def make_in_maps(inp):
    f2, im = f2_consts()
    shared = {}
    shared["w_in"] = _f32(inp["w_in"][0]); shared["w_out"] = _f32(inp["w_out"][0]); shared["w_mem_kv"] = _f32(inp["w_mem_kv"][0])
    shared["g_in"] = _f32(np.asarray(inp["norm_in"][0]).reshape(8, 128).T)
    shared["g_mem"] = _f32(np.asarray(inp["mem_norm"][0]).reshape(8, 128).T)
    gn = np.stack([np.tile(np.asarray(inp[k][0]), 2) for k in ("att_q_norm", "att_k_norm", "mem_q_norm", "mem_k_norm")], axis=1)
    shared["gains"] = _f32(gn)
    cw = np.asarray(inp["hy_conv_w"][0])
    shared["convw"] = _f32(cw.T.reshape(9, 128, 3).transpose(1, 0, 2))
    shared["convb"] = _f32(np.asarray(inp["hy_conv_b"][0]).reshape(9, 128).T)
    shared["skipv"] = _f32(np.asarray(inp["hy_skip"][0]).reshape(3, 128).T)
    shared["fw1"] = _f32(inp["hy_filt_w1"][0]); shared["fw2"] = _f32(inp["hy_filt_w2"][0]); shared["fw3"] = _f32(inp["hy_filt_w3"][0])
    shared["fvec"] = _f32(np.stack([np.asarray(inp["hy_filt_b1"][0]), np.asarray(inp["hy_filt_freq"][0]),
                                    np.asarray(inp["hy_filt_b2"][0])], axis=1))
    shared["rel_bias"] = _f32(inp["rel_bias"])
    shared["ident"] = _bf(np.eye(128)); shared["jmat"] = _bf(np.eye(128)[::-1])
    ob = np.zeros((128, 128)); ob[:64, :64] = 1; ob[64:, 64:] = 1
    shared["onesblk"] = _bf(ob)
    shared["f2"] = f2; shared["imat"] = im
    shared["bias_oh"] = _f32(bias_onehot())
    for sn, L in (("p", LP), ("s", LS)):
        c = fft_consts(L)
        shared[f"M1_{sn}"] = c["M1"]; shared[f"M1f_{sn}"] = c["M1full"]
        shared[f"twA_{sn}"] = c["twA"]; shared[f"twB_{sn}"] = c["twB"]
        shared[f"tiA_{sn}"] = c["tiA"]; shared[f"tiB_{sn}"] = c["tiB"]; shared[f"G3_{sn}"] = c["G3"]
        ft, dec = filter_consts(L)
        shared[f"feats_{sn}"] = ft; shared[f"dec_{sn}"] = dec
    maps = []
    for i in range(NCORES):
        m = dict(shared)
        m["x_p"] = _f32(inp["x_prompt"][i]); m["x_s"] = _f32(inp["x_sample"][i])
        m["mem_p"] = _f32(inp["mem_prompt"][i]); m["mem_s"] = _f32(inp["mem_sample"][i])
        maps.append(m)
    return maps


_CACHE = {}


def kernel(**inputs):
    inp = {k: np.asarray(v) for k, v in inputs.items()}
    maps = make_in_maps(inp)
    if "nc" not in _CACHE:
        _CACHE["nc"] = build_program()[0]
    res = run_bass_kernel_spmd(_CACHE["nc"], maps, core_ids=list(range(NCORES)))
    y_p = np.stack([np.asarray(res.results[i]["y_p"], dtype=np.float32) for i in range(NCORES)], axis=0)
    y_s = np.stack([np.asarray(res.results[i]["y_s"], dtype=np.float32) for i in range(NCORES)], axis=0)
    return (y_p, y_s)
```

```python
import contextlib
import math
import numpy as np
import ml_dtypes
import concourse.bass as bass
import concourse.mybir as mybir
from concourse.bass_utils import run_bass_kernel_spmd

F32 = mybir.dt.float32
BF16 = mybir.dt.bfloat16
I32 = mybir.dt.int32
ALU = mybir.AluOpType
AF = mybir.ActivationFunctionType

NCORES = 8
D = 1024
DIN = 3584
DHY = 384
LP = 8192
LS = 2048
NMEM = 256
TWO_PI = 2.0 * math.pi


class Buf:
    __slots__ = ("name", "w", "r")

    def __init__(self, name=""):
        self.name = name
        self.w = None
        self.r = {}


class T:
    def __init__(self, handle, buf=None):
        self.t = handle
        self.b = buf if buf is not None else Buf(getattr(handle, "name", ""))

    def __getitem__(self, key):
        return self.t[key]


def _b(x):
    return x.b if isinstance(x, T) else x


class Sched:
    NDMA_SEMS = 20

    def __init__(self, nc):
        self.nc = nc
        self.engs = {n: dict(ops=[], count=0, seen={}, pend={}) for n in ("pe", "act", "dve", "pool", "sp")}
        self.sems = {}
        self.dma_pool = {}
        self.dma_rr = {}
        self.final = {}

    def _sem(self, key):
        if key not in self.sems:
            self.sems[key] = self.nc.alloc_semaphore(f"s_{key}")
        return self.sems[key]

    @staticmethod
    def _deps(reads, writes):
        need = {}
        for b in reads:
            b = _b(b)
            if b.w is not None:
                k, v = b.w
                if need.get(k, 0) < v:
                    need[k] = v
        for b in writes:
            b = _b(b)
            if b.w is not None:
                k, v = b.w
                if need.get(k, 0) < v:
                    need[k] = v
            for k, v in b.r.items():
                if need.get(k, 0) < v:
                    need[k] = v
        return need

    def _waits(self, e, need):
        for k, v in e["pend"].items():
            if need.get(k, 0) < v:
                need[k] = v
        e["pend"] = {}
        waits = []
        for k, v in need.items():
            if e["seen"].get(k, 0) >= v:
                continue
            e["seen"][k] = v
            waits.append((k, v))
        return waits

    EPOCH = 4000

    def op(self, eng, emit, reads=(), writes=()):
        e = self.engs[eng]
        need = self._deps(reads, writes)
        if eng == "pe":
            need = {k: v for k, v in need.items() if not k.startswith("pe")}
        waits = self._waits(e, need)
        ep = e["count"] // self.EPOCH
        e["count"] += 1
        idx = e["count"] - ep * self.EPOCH
        key = eng if ep == 0 else f"{eng}{ep}"
        e["cur"] = (key, idx)
        e["ops"].append((waits, emit, (key, 1)))
        for b in reads:
            _b(b).r[key] = idx
        for b in writes:
            b = _b(b)
            b.w = (key, idx)
            b.r = {}
        return idx

    def dma(self, queue, emit, reads=(), writes=(), final=False):
        e = self.engs[queue]
        pool = self.dma_pool.setdefault(queue, [[f"d{queue}{i}", 0] for i in range(self.NDMA_SEMS)])
        i = self.dma_rr.get(queue, 0)
        self.dma_rr[queue] = (i + 1) % self.NDMA_SEMS
        slot = pool[i]
        key = slot[0]
        need = self._deps(reads, writes)
        if slot[1] > 0:
            need[key] = max(need.get(key, 0), slot[1] * 16)
        waits = self._waits(e, need)
        slot[1] += 1
        val = slot[1] * 16
        e["ops"].append((waits, emit, (key, 16)))
        for b in reads:
            _b(b).r[key] = val
        for b in writes:
            b = _b(b)
            b.w = (key, val)
            b.r = {}
        if final:
            self.final[key] = max(self.final.get(key, 0), val)
        return key, val

    def barrier(self):
        state = {}
        for n, e in self.engs.items():
            if e["count"] > 0:
                k, v = e["cur"]
                state[k] = v
        for q, pool in self.dma_pool.items():
            for key, uses in pool:
                if uses > 0:
                    state[key] = uses * 16
        for n, e in self.engs.items():
            for k, v in state.items():
                if e["pend"].get(k, 0) < v:
                    e["pend"][k] = v

    def emit_all(self):
        nc = self.nc
        handles = {"pe": "tensor", "act": "scalar", "dve": "vector", "pool": "gpsimd", "sp": "sync"}
        with nc.Block() as block:
            for name, attr in handles.items():
                ops = self.engs[name]["ops"]
                extra = list(self.final.items()) if name == "sp" else []

                def body(eng, ops=ops, extra=extra):
                    for waits, emit, (skey, inc) in ops:
                        for k, v in waits:
                            eng.wait_ge(self._sem(k), v)
                        inst = emit(eng)
                        inst.then_inc(self._sem(skey), inc)
                    for k, v in extra:
                        eng.wait_ge(self._sem(k), v)

                getattr(block, attr)(body)


def _bf(a):
    return np.ascontiguousarray(np.asarray(a, dtype=np.float32)).astype(ml_dtypes.bfloat16)


def _f32(a):
    return np.ascontiguousarray(np.asarray(a, dtype=np.float32))


def fft_consts(L):
    N = 2 * L
    N1 = N // 128
    N1nz = N1 // 2
    CG = 64 // N1nz
    NK1 = N1 // 2 + 1
    RK = CG * NK1
    k1 = np.arange(NK1, dtype=np.float64)
    n1 = np.arange(N1, dtype=np.float64)
    n2 = np.arange(128, dtype=np.float64)
    th1 = TWO_PI * np.outer(n1, k1) / N1
    m1re = np.zeros((CG * N1, RK)); m1im = np.zeros((CG * N1, RK))
    m1re_nz = np.zeros((64, RK)); m1im_nz = np.zeros((64, RK))
    for c in range(CG):
        m1re[c * N1:(c + 1) * N1, c * NK1:(c + 1) * NK1] = np.cos(th1)
        m1im[c * N1:(c + 1) * N1, c * NK1:(c + 1) * NK1] = -np.sin(th1)
        m1re_nz[c * N1nz:(c + 1) * N1nz, c * NK1:(c + 1) * NK1] = np.cos(th1[:N1nz])
        m1im_nz[c * N1nz:(c + 1) * N1nz, c * NK1:(c + 1) * NK1] = -np.sin(th1[:N1nz])
    M1 = np.concatenate([m1re_nz, m1im_nz], axis=1)
    M1full = np.concatenate([m1re, m1im], axis=1)
    thw = TWO_PI * np.outer(n2, k1) / N
    thw = np.tile(thw, (1, CG))
    twA = np.cos(thw)
    twB = np.stack([-np.sin(thw), np.sin(thw)], axis=1)
    tiA = np.cos(thw).T.copy()
    tiB = np.sin(thw).T.copy()
    cw = np.full(NK1, 2.0); cw[0] = 1.0; cw[-1] = 1.0
    thi = TWO_PI * np.outer(k1, n1[:N1nz]) / N1
    GR = np.zeros((RK, 64)); GI = np.zeros((RK, 64))
    for c in range(CG):
        GR[c * NK1:(c + 1) * NK1, c * N1nz:(c + 1) * N1nz] = (cw[:, None] / N) * np.cos(thi)
        GI[c * NK1:(c + 1) * NK1, c * N1nz:(c + 1) * N1nz] = (cw[:, None] / N) * np.sin(thi)
    G3 = np.stack([GR, -GR, -GI], axis=1)
    return dict(N=N, N1=N1, N1nz=N1nz, CG=CG, NK1=NK1, RK=RK, U=DHY // CG,
                M1=_bf(M1), M1full=_bf(M1full), twA=_f32(twA), twB=_f32(twB),
                tiA=_f32(tiA), tiB=_f32(tiB), G3=_bf(G3))


def f2_consts():
    n = np.arange(128, dtype=np.float64)
    th = TWO_PI * np.outer(n, n) / 128
    C = np.cos(th); S = np.sin(th)
    F2 = np.stack([C, S, -S], axis=1)
    IM = np.stack([np.concatenate([C, S], 1), np.concatenate([-C, -S], 1), np.concatenate([-S, C], 1)], axis=1)
    return _bf(F2), _bf(IM)


def filter_consts(L):
    pos = np.arange(L, dtype=np.float64)
    bands = np.linspace(1e-4, 15.0, 16)

    def feats(p):
        t = p / max(L - 1, 1)
        ang = (TWO_PI / L) * p[:, None] * bands[None, :]
        return np.concatenate([t[:, None], np.cos(ang), -np.sin(ang)], axis=1)
    prev = (L - pos) % L
    ff = feats(pos).T
    fr = feats(prev).T
    deltas = np.abs(np.linspace(math.log(1e-2) / 1.5, math.log(1e-2) / 0.3, DHY))
    t = pos / max(L - 1, 1)
    dec_f = np.exp(-t[None, :] * deltas[:, None])
    dec_r = np.exp(-(prev / max(L - 1, 1))[None, :] * deltas[:, None])
    dec_r[:, 0] = 0.0
    return _f32(np.concatenate([ff, fr], axis=1)), _f32(np.concatenate([dec_f, dec_r], axis=1))


OFFS = (-128, -64, 0, 64, 128)
DILS = (1, 4, 16)


def t5_bucket_np(rel):
    half = 16
    max_exact = 8
    ret = np.where(rel > 0, half, 0)
    n = np.abs(rel)
    large = max_exact + (np.log(np.maximum(n, 1).astype(np.float32) / max_exact)
                         / math.log(1024 / max_exact) * (half - max_exact)).astype(np.int32)
    large = np.minimum(large, half - 1)
    return ret + np.where(n < max_exact, n, large)


def bias_onehot():
    oh = np.zeros((33, 3, 512), dtype=np.float32)
    for ci, dil in enumerate(DILS):
        v = np.arange(512)
        rel = 255 - v
        valid = np.abs(rel) <= 64
        bk = t5_bucket_np(rel * dil)
        for vv in range(512):
            if valid[vv]:
                oh[bk[vv], ci, vv] = 1.0
            else:
                oh[32, ci, vv] = -10000.0
    return oh.reshape(33, 3 * 512)


def build_program(debug=False, phases=(0, 1, 2, 3, 4, 5), only_seq=None, sub2=(1, 2, 3, 4, 5), cfgs=(0, 1, 2), p3=9):
    nc = bass.Bass("TRN2", target_bir_lowering=False)
    S = Sched(nc)
    dbg_outs = []

    def din(name, shape, dt=F32):
        return nc.dram_tensor(name, list(shape), dt, kind="ExternalInput").ap()

    def dscr(name, shape, dt):
        kind = "ExternalOutput" if debug else "Internal"
        if debug:
            dbg_outs.append(name)
        return nc.dram_tensor(name, list(shape), dt, kind=kind).ap()

    seqs = [("p", LP), ("s", LS)]
    run_seqs = [q for q in seqs if only_seq is None or q[0] == only_seq]
    FC = {L: fft_consts(L) for _, L in seqs}

    x_d = {"p": din("x_p", [LP, D]), "s": din("x_s", [LS, D])}
    mem_d = {"p": din("mem_p", [NMEM, D]), "s": din("mem_s", [NMEM, D])}
    y_d = {"p": nc.dram_tensor("y_p", [LP, D], F32, kind="ExternalOutput").ap(),
           "s": nc.dram_tensor("y_s", [LS, D], F32, kind="ExternalOutput").ap()}
    w_in_d = din("w_in", [D, DIN])
    w_out_d = din("w_out", [D, D])
    wkv_d = din("w_mem_kv", [D, 512])
    g_in_d = din("g_in", [128, 8]); g_mem_d = din("g_mem", [128, 8])
    gains_d = din("gains", [128, 4])
    convw_d = din("convw", [128, 9, 3]); convb_d = din("convb", [128, 9]); skip_d = din("skipv", [128, 3])
    fw1_d = din("fw1", [33, 64]); fw2_d = din("fw2", [64, 64]); fw3_d = din("fw3", [64, 768])
    fvec_d = din("fvec", [64, 3])
    relb_d = din("rel_bias", [32, 6])
    ident_d = din("ident", [128, 128], BF16); jmat_d = din("jmat", [128, 128], BF16)
    onesblk_d = din("onesblk", [128, 128], BF16)
    f2_d = din("f2", [128, 3, 128], BF16); im_d = din("imat", [128, 3, 256], BF16)
    oh_d = din("bias_oh", [33, 1536])
    fcd = {}
    for sn, L in seqs:
        c = FC[L]
        RK = c["RK"]
        fcd[sn] = dict(M1=din(f"M1_{sn}", [64, 2 * RK], BF16), M1full=din(f"M1f_{sn}", [128, 2 * RK], BF16),
                       twA=din(f"twA_{sn}", [128, RK]), twB=din(f"twB_{sn}", [128, 2, RK]),
                       tiA=din(f"tiA_{sn}", [RK, 128]), tiB=din(f"tiB_{sn}", [RK, 128]),
                       G3=din(f"G3_{sn}", [RK, 3, 64], BF16),
                       feats=din(f"feats_{sn}", [33, 2 * L]), dec=din(f"dec_{sn}", [DHY, 2 * L]))
    scr = {}
    for sn, L in seqs:
        c = FC[L]
        nb = c["U"] // 6
        scr[sn] = dict(
            zhy=dscr(f"zhy_{sn}", [1152, L], BF16), qT=dscr(f"qT_{sn}", [384, L], BF16),
            kT=dscr(f"kT_{sn}", [384, L], BF16), vtok=dscr(f"vtok_{sn}", [L, 768], BF16),
            mqT=dscr(f"mqT_{sn}", [256, L], BF16), gT=dscr(f"gT_{sn}", [1024, L], BF16),
            sT=dscr(f"sT_{sn}", [384, L], BF16), kfilt=dscr(f"kfilt_{sn}", [384, 2 * L], BF16),
            kspec=dscr(f"kspec_{sn}", [nb, 128, 2, 6 * c["RK"]], F32),
            conv=dscr(f"conv_{sn}", [384, L], F32), mixed=dscr(f"mixed_{sn}", [1024, L], BF16))
    htab_d = dscr("htab", [6, 1536], BF16)
    ebd_d = dscr("ebd", [36, 128, 512], BF16)

    def sb(name, shape, dt):
        return T(nc.alloc_sbuf_tensor(name, list(shape), dt))

    _uid = [0]

    def uniq(name):
        _uid[0] += 1
        return f"{name}_{_uid[0]}"

    w_out_bf = sb("w_out_bf", [128, 8, D], BF16)
    ident = sb("ident_sb", [128, 128], BF16); jmat = sb("jmat_sb", [128, 128], BF16)
    onesblk = sb("onesblk_sb", [128, 128], BF16)
    ones_bf = sb("ones_bf", [128, 64], BF16)
    g_in = sb("g_in_sb", [128, 8], F32); g_mem = sb("g_mem_sb", [128, 8], F32)
    gains = sb("gains_sb", [128, 4], F32)
    convw = sb("convw_sb", [128, 9, 3], F32); convb = sb("convb_sb", [128, 9], F32); skipv = sb("skip_sb", [128, 3], F32)
    cst = sb("cst_sb", [128, 4], F32)
    fw1 = sb("fw1_sb", [33, 64], F32); fw2 = sb("fw2_sb", [64, 64], F32); fw3 = sb("fw3_sb", [64, 768], BF16)
    fvec = sb("fvec_sb", [64, 3], F32); fab = sb("fab_sb", [64, 3], F32)
    kmT = {sn: sb(f"kmT_{sn}", [128, 2, NMEM], BF16) for sn, _ in seqs}
    vm = {sn: sb(f"vm_{sn}", [128, 2, 256], BF16) for sn, _ in seqs}
    f2m = sb("f2_sb", [128, 3, 128], BF16); imat = sb("im_sb", [128, 3, 256], BF16)
    pb = [T(nc.alloc_psum_tensor(f"pb{i}", [128, 512], F32)) for i in range(8)]
    pb16 = [T(p.t.bitcast(BF16), p.b) for p in pb]
    es01 = contextlib.ExitStack()
    w_in_bf = T(es01.enter_context(nc.sbuf_tensor("w_in_bf", [128, 8, DIN], BF16)))
    wkv_bf = T(es01.enter_context(nc.sbuf_tensor("wkv_bf", [128, 8, 512], BF16)))

    LD = "sp"
    ST = "pool"

    def ld(out, in_, reads=(), writes=(), q=LD):
        S.dma(q, lambda e: e.dma_start(out=out, in_=in_), reads=reads, writes=writes)

    def st(out, in_, reads=(), writes=(), final=False, q=ST, slow=False):
        if slow:
            S.dma(q, lambda e: e.dma_start(out=out, in_=in_, allow_slow_non_contiguous=True), reads=reads, writes=writes, final=final)
        else:
            S.dma(q, lambda e: e.dma_start(out=out, in_=in_), reads=reads, writes=writes, final=final)

    def act(out, in_, func, reads, writes, bias=None, scale=None, accum_out=None):
        kw = {}
        if bias is not None:
            kw["bias"] = bias
        if scale is not None:
            kw["scale"] = scale
        if accum_out is not None:
            kw["accum_out"] = accum_out
        S.op("act", lambda e: e.activation(out=out, in_=in_, func=func, **kw), reads=reads, writes=writes)

    def tsc(eng, out, in0, s1, s2, op0, op1, reads, writes):
        if s2 is None:
            S.op(eng, lambda e: e.tensor_scalar(out=out, in0=in0, scalar1=s1, scalar2=None, op0=op0), reads=reads, writes=writes)
        else:
            S.op(eng, lambda e: e.tensor_scalar(out=out, in0=in0, scalar1=s1, scalar2=s2, op0=op0, op1=op1), reads=reads, writes=writes)

    def tt(eng, out, in0, in1, op, reads, writes):
        S.op(eng, lambda e: e.tensor_tensor(out=out, in0=in0, in1=in1, op=op), reads=reads, writes=writes)

    def stt(eng, out, in0, scalar, in1, op0, op1, reads, writes):
        S.op(eng, lambda e: e.scalar_tensor_tensor(out=out, in0=in0, scalar=scalar, in1=in1, op0=op0, op1=op1),
             reads=reads, writes=writes)

    def recip(eng, out, in_, reads, writes):
        S.op(eng, lambda e: e.reciprocal(out=out, in_=in_), reads=reads, writes=writes)

    def cp(eng, out, in_, reads, writes):
        S.op(eng, lambda e: e.tensor_copy(out=out, in_=in_), reads=reads, writes=writes)

    def mset(eng, ap, val, writes):
        S.op(eng, lambda e: e.memset(ap, val), writes=writes)

    def rsum(eng, out, in_, reads, writes):
        S.op(eng, lambda e: e.reduce_sum(out=out, in_=in_, axis=mybir.AxisListType.X), reads=reads, writes=writes)

    def mm(out, lhsT, rhs, start, stop, reads, writes):
        def emit(e):
            try:
                return e.matmul(out=out, lhsT=lhsT, rhs=rhs, start=start, stop=stop, skip_group_check=True)
            except Exception:
                print("MATMUL FAIL out", out, "\nlhsT", lhsT, "\nrhs", rhs)
                raise
        S.op("pe", emit, reads=reads, writes=writes)

    def tr(out, in_, reads, writes):
        S.op("pe", lambda e: e.transpose(out=out, in_=in_, identity=ident[:, :]), reads=list(reads) + [ident], writes=writes)

    EBVAR = {"int": (1, 3), "first": (2, 4), "last": (0, 2), "only": (2, 2)}
    EBIDX = {}
    for ci_ in range(3):
        for vn_ in EBVAR:
            for pr_ in range(3):
                EBIDX[(ci_, vn_, pr_)] = len(EBIDX)

    eb_built = [False]

    def build_eb_tiles(tmp):
        items = [(ci, vn, offs, pr) for ci in range(3) for vn, offs in EBVAR.items() for pr in range(3)]
        hm = {}
        for ci in range(3):
            for hd in range(6):
                t = tmp(f"hm{ci}{hd}", [128, 384], BF16)
                hm[(ci, hd)] = t
                ld(t[:, :], bass.AP(htab_d.tensor, hd * 1536 + ci * 512, [[1, 128], [1, 384]]), writes=[t])
        ebs = [tmp(f"ebs{i}", [128, 512], BF16) for i in range(4)]

        def slot(k):
            ci, vn, offs, pr = items[k]
            nk = 1 if vn == "only" else 2
            W2 = 128 * 2 * nk
            pz = pb[5 + k % 3]
            t = ebs[k % 4]
            for ab in range(2):
                for kt in range(nk):
                    sft = 128 - OFFS[offs[kt]]
                    bi = nk * ab + kt
                    h = hm[(ci, 2 * pr + ab)]
                    mm(pz[:, 128 * bi:128 * bi + 128], jmat[:, :], h[:, sft:sft + 128], True, True, [jmat, h], [pz])
            if W2 < 512:
                mset("pool", t[:, W2:512], 0.0, [t])
            act(t[:, 0:W2], pz[:, 0:W2], AF.Exp, [pz], [t])
            st(ebd_d[EBIDX[(ci, vn, pr)]], t[:, :], reads=[t])
        return [(lambda k=k: slot(k)) for k in range(len(items))]

    with contextlib.ExitStack() as es:
        def tmp(name, shape, dt):
            return T(es.enter_context(nc.sbuf_tensor(uniq(name), list(shape), dt)))

        for dst, src in ((ident, ident_d), (jmat, jmat_d), (onesblk, onesblk_d), (g_in, g_in_d), (g_mem, g_mem_d),
                         (gains, gains_d), (convb, convb_d), (skipv, skip_d), (fw1, fw1_d), (fw2, fw2_d), (fvec, fvec_d)):
            ld(dst[:, :], src[:, :], writes=[dst])
        ld(convw[:, :, :], convw_d[:, :, :], writes=[convw])
        ld(f2m[:, :, :], f2_d[:, :, :], writes=[f2m])
        ld(imat[:, :, :], im_d[:, :, :], writes=[imat])
        mset("pool", ones_bf[:, :], 1.0, [ones_bf])
        mset("pool", cst[:, 0:1], 1e-6, [cst])
        mset("pool", cst[:, 1:2], -math.pi, [cst])
        mset("pool", cst[:, 2:3], 1e-12, [cst])
        mset("pool", cst[:, 3:4], 0.0, [cst])
        tsc("dve", gains[:, 0:1], gains[:, 0:1], 0.125, None, ALU.mult, None, [gains], [gains])
        tsc("dve", gains[:, 2:3], gains[:, 2:3], 0.125, None, ALU.mult, None, [gains], [gains])
        tsc("dve", fab[:, 0:1], fvec[:, 1:2], 1.0 / TWO_PI, None, ALU.mult, None, [fvec], [fab])
        tt("dve", fab[:, 1:2], fvec[:, 0:1], fab[:, 0:1], ALU.mult, [fvec, fab], [fab])
        tt("dve", fab[:, 2:3], fvec[:, 2:3], fab[:, 0:1], ALU.mult, [fvec, fab], [fab])
        w3st = tmp("w3st", [64, 768], F32)
        ld(w3st[:, :], fw3_d[:, :], writes=[w3st])
        cp("dve", fw3[:, :], w3st[:, :], [w3st], [fw3])
        wst = [tmp(f"wst{i}", [128, DIN], F32) for i in range(2)]
        n = 0
        for k in range(8):
            w = wst[n % 2]; n += 1
            ld(w[:, :], w_in_d[128 * k:128 * k + 128, :], writes=[w])
            if k % 2 == 0:
                tsc("dve", w_in_bf[:, k, :], w[:, :], g_in[:, k:k + 1], None, ALU.mult, None, [w, g_in], [w_in_bf])
            else:
                act(w_in_bf[:, k, :], w[:, :], AF.Copy, [w, g_in], [w_in_bf], scale=g_in[:, k:k + 1])
        for k in range(8):
            w = wst[n % 2]; n += 1
            ld(w[:, 0:D], w_out_d[128 * k:128 * k + 128, :], writes=[w])
            ld(w[:, D:D + 512], wkv_d[128 * k:128 * k + 128, :], writes=[w])
            cp("dve", w_out_bf[:, k, :], w[:, 0:D], [w], [w_out_bf])
            act(wkv_bf[:, k, :], w[:, D:D + 512], AF.Copy, [w, g_mem], [wkv_bf], scale=g_mem[:, k:k + 1])
        relb = tmp("relb", [33, 6], F32)
        ohs = tmp("ohs", [33, 1536], F32)
        hts = tmp("hts", [6, 1536], BF16)
        mset("pool", relb[:, :], 1.0, [relb])
        ld(relb[0:32, :], relb_d[:, :], writes=[relb])
        ld(ohs[:, :], oh_d[:, :], writes=[ohs])
        for j in range(3):
            mm(pb[j % 2][0:6, 0:512], relb[:, :], ohs[:, 512 * j:512 * j + 512], True, True, [relb, ohs], [pb[j % 2]])
            act(hts[:, 512 * j:512 * j + 512], pb[j % 2][0:6, 0:512], AF.Copy, [pb[j % 2]], [hts])
        st(htab_d[:, :], hts[:, :], reads=[hts])
        S.barrier()

    if 1 in phases:
        with contextlib.ExitStack() as es:
            def tmp(name, shape, dt):
                return T(es.enter_context(nc.sbuf_tensor(uniq(name), list(shape), dt)))

            xin = [tmp(f"xin{i}", [128, D], F32) for i in range(2)]
            ssq = [tmp(f"ssq{i}", [128, 1], F32) for i in range(3)]
            xs = [tmp(f"xs{i}", [128, D], BF16) for i in range(4)]
            xT = [tmp(f"xT{i}", [128, 8, 512], BF16) for i in range(2)]
            Zb = [tmp(f"Zb{j}", [128, 514], F32) for j in range(9)]
            u1b = [tmp(f"u1b{i}", [128, 512], F32) for i in range(3)]
            uvb = [tmp(f"uvb{i}", [128, 512], F32) for i in range(2)]
            x0_st = [tmp(f"x0st{i}", [128, 3, 512], BF16) for i in range(2)]
            s_st = [tmp(f"sst{i}", [128, 3, 512], BF16) for i in range(2)]
            ulast = tmp("ulast", [128, 9], F32)
            lst = tmp("lst", [128, 6, 1], BF16)
            qk_st = [tmp(f"qkst{i}", [128, 6, 512], BF16) for i in range(2)]
            mq_st = [tmp(f"mqst{i}", [128, 2, 512], BF16) for i in range(2)]
            g_st = [tmp(f"gst{i}", [128, 8, 512], BF16) for i in range(2)]
            v_st = [tmp(f"vst{i}", [128, 4, 768], BF16) for i in range(1)]
            for v_ in v_st:
                mset("pool", v_[:, :, :], 1.0, [v_])
            sqb = [tmp(f"sqb{i}", [128, 512], BF16) for i in range(2)]
            rrb = [tmp(f"rrb{i}", [128, 512], F32) for i in range(2)]
            cnt = dict(x=0, pz=0, hn=0, pt=0, uv=0)

            def prep_a(src_rows, slot):
                i = cnt["x"]; cnt["x"] += 1
                xi = xin[i % 2]; sq = ssq[i % 3]; xsb = xs[slot]
                ld(xi[:, :], src_rows, writes=[xi])
                mset("pool", sq[:, :], 0.0, [sq])
                act(xsb[:, :], xi[:, :], AF.Square, [xi], [xsb, sq], accum_out=sq[:, :])
                act(sq[:, :], sq[:, :], AF.Ln, [sq, cst], [sq], bias=cst[:, 0:1], scale=1.0 / D)
                act(sq[:, :], sq[:, :], AF.Exp, [sq], [sq], scale=-0.5)
                tsc("dve", xsb[:, :], xi[:, :], sq[:, 0:1], None, ALU.mult, None, [xi, sq], [xsb])

            def prep_b(slot, xT_t, col0):
                xsb = xs[slot]
                p = cnt["pt"] % 2; cnt["pt"] += 1
                for k in range(8):
                    tr(pb16[p][:, 128 * k:128 * k + 128], xsb[:, 128 * k:128 * k + 128], [xsb], [pb16[p]])
                act(xT_t[:, :, col0:col0 + 128], pb16[p][:, :].rearrange("p (k t) -> p k t", k=8), AF.Copy, [pb16[p]], [xT_t])

            def prep_tile(src_rows, xT_t, col0, ncols_total):
                prep_a(src_rows, cnt["x"] % 4)
                prep_b((cnt["x"] - 1) % 4, xT_t, col0)

            pending = []

            def headnorm(pz, gcol, out_ap, ncols, out_t):
                h = cnt["hn"] % 2; cnt["hn"] += 1
                sq = sqb[h]; rr = rrb[h]; ph = pb[6 + h]
                act(sq[:, 0:ncols], pz[:, 0:ncols], AF.Square, [pz], [sq])

                def part_b():
                    mm(ph[:, 0:ncols], onesblk[:, :], sq[:, 0:ncols], True, True, [onesblk, sq], [ph])
                    act(rr[:, 0:ncols], ph[:, 0:ncols], AF.Ln, [ph, cst], [rr], bias=cst[:, 0:1], scale=1.0 / 64)
                    act(rr[:, 0:ncols], rr[:, 0:ncols], AF.Exp, [rr], [rr], scale=-0.5)
                    stt("dve", out_ap, pz[:, 0:ncols], gains[:, gcol:gcol + 1], rr[:, 0:ncols], ALU.mult, ALU.mult,
                        [pz, gains, rr], [out_t])
                pending.append(part_b)

            def flush_pending():
                while pending:
                    pending.pop(0)()

            def next_pz():
                p = pb[2 + cnt["pz"] % 4]; cnt["pz"] += 1
                return p

            for sn, L in run_seqs:
                sc = scr[sn]
                mT = xT[0]
                for i in range(2):
                    prep_tile(mem_d[sn][128 * i:128 * i + 128, :], mT, 128 * i, 256)
                for j in range(2):
                    pz = next_pz()
                    for k in range(8):
                        mm(pz[:, 0:256], wkv_bf[:, k, 128 * j:128 * j + 128], mT[:, k, 0:256], k == 0, k == 7, [wkv_bf, mT], [pz])
                    headnorm(pz, 3, kmT[sn][:, j, :], 256, kmT[sn])
                    flush_pending()
                for i in range(2):
                    pz = next_pz()
                    for k in range(8):
                        mm(pz[:, 0:256], mT[:, k, 128 * i:128 * i + 128], wkv_bf[:, k, 256:512], k == 0, k == 7, [wkv_bf, mT], [pz])
                    act(vm[sn][:, i, :], pz[:, 0:256], AF.Copy, [pz], [vm[sn]])
                nch = L // 512

                def prep_a_tile(c, i):
                    r0 = 512 * c + 128 * i
                    prep_a(x_d[sn][r0:r0 + 128, :], i)

                def prep_b_chunk(c):
                    for i in range(4):
                        prep_b(i, xT[(c + 1) % 2], 128 * i)

                def main_chunk(c):
                    xt = xT[(c + 1) % 2]
                    c0 = 512 * c
                    x0s = x0_st[c % 2]; sst = s_st[c % 2]
                    qs = qk_st[c % 2]; ms = mq_st[c % 2]; gs = g_st[c % 2]; vs = v_st[0]
                    order = [9, 0, 10, 1, 11, 2, 12, 3, 13, 4, 14, 5, 18, 6, 19, 7, 8] + list(range(20, 28))
                    for jn, j in enumerate(order):
                        if jn in (3, 9, 15, 21) and c + 2 < nch:
                            prep_a_tile(c + 2, (jn - 3) // 6)
                        pz = next_pz()
                        for k in range(8):
                            mm(pz[:, :], w_in_bf[:, k, 128 * j:128 * j + 128], xt[:, k, :], k == 0, k == 7, [w_in_bf, xt], [pz])
                        flush_pending()
                        if j < 9:
                            zb = Zb[j]
                            act(zb[:, 2:514], pz[:, :], AF.Copy, [pz], [zb])
                            if j < 3:
                                u = uvb[cnt["uv"] % 2]; cnt["uv"] += 1
                            elif j < 6:
                                u = u1b[j - 3]
                            else:
                                u = uvb[cnt["uv"] % 2]; cnt["uv"] += 1
                            if j % 3 == 1:
                                tsc("dve", u[:, :], zb[:, 1:513], convw[:, j, 1:2], convb[:, j:j + 1], ALU.mult, ALU.add, [zb, convw, convb], [u])
                            else:
                                act(u[:, :], zb[:, 1:513], AF.Identity, [zb, convw, convb], [u], bias=convb[:, j:j + 1], scale=convw[:, j, 1:2])
                            stt("dve", u[:, :], zb[:, 0:512], convw[:, j, 0:1], u[:, :], ALU.mult, ALU.add, [zb, convw, u], [u])
                            if j < 3:
                                stt("dve", x0s[:, j, :], zb[:, 2:514], convw[:, j, 2:3], u[:, :], ALU.mult, ALU.add, [zb, convw, u], [x0s])
                            else:
                                stt("dve", u[:, :], zb[:, 2:514], convw[:, j, 2:3], u[:, :], ALU.mult, ALU.add, [zb, convw, u], [u])
                            if j >= 6:
                                tt("pool", sst[:, j - 6, :], u1b[j - 6][:, :], u[:, :], ALU.mult, [u1b[j - 6], u], [sst])
                            cp("dve", zb[:, 0:2], zb[:, 512:514], [zb], [zb])
                        elif j < 12:
                            headnorm(pz, 0, qs[:, j - 9, :], 512, qs)
                        elif j < 15:
                            headnorm(pz, 1, qs[:, j - 9, :], 512, qs)
                        elif j < 20:
                            headnorm(pz, 2, ms[:, j - 18, :], 512, ms)
                        else:
                            act(gs[:, j - 20, :], pz[:, :], AF.Silu, [pz], [gs])
                    flush_pending()
                    for i in range(4):
                        pz = next_pz()
                        for k in range(8):
                            mm(pz[:, 0:384], xt[:, k, 128 * i:128 * i + 128], w_in_bf[:, k, 1920:2304], k == 0, k == 7, [w_in_bf, xt], [pz])
                        vo = bass.AP(vs.t, i * 768, [[4 * 768, 128], [256, 3], [192, 2], [1, 64]])
                        act(vo, pz[:, 0:384].rearrange("p (a b e) -> p a b e", a=3, b=2), AF.Copy, [pz], [vs])
                    if c == 0:
                        st(sc["zhy"][0:384, 0:511].rearrange("(j p) t -> p j t", p=128), x0s[:, :, 1:512], reads=[x0s])
                        st(sc["sT"][:, 0:511].rearrange("(j p) t -> p j t", p=128), sst[:, :, 1:512], reads=[sst])
                    else:
                        st(sc["zhy"][0:384, c0 - 1:c0 + 511].rearrange("(j p) t -> p j t", p=128), x0s[:, :, :], reads=[x0s])
                        st(sc["sT"][:, c0 - 1:c0 + 511].rearrange("(j p) t -> p j t", p=128), sst[:, :, :], reads=[sst])
                    st(sc["qT"][:, c0:c0 + 512].rearrange("(j p) t -> p j t", p=128), qs[:, 0:3, :], reads=[qs])
                    st(sc["kT"][:, c0:c0 + 512].rearrange("(j p) t -> p j t", p=128), qs[:, 3:6, :], reads=[qs])
                    st(sc["mqT"][:, c0:c0 + 512].rearrange("(j p) t -> p j t", p=128), ms[:, :, :], reads=[ms])
                    st(sc["gT"][:, c0:c0 + 512].rearrange("(j p) t -> p j t", p=128), gs[:, :, :], reads=[gs])
                    st(sc["vtok"][c0:c0 + 512, :].rearrange("(i p) e -> p i e", p=128), vs[:, :, :], reads=[vs])

                for zb in Zb:
                    mset("pool", zb[:, 0:2], 0.0, [zb])
                for i in range(4):
                    prep_a_tile(0, i)
                prep_b_chunk(0)
                if nch > 1:
                    for i in range(4):
                        prep_a_tile(1, i)
                for c in range(nch):
                    if c + 1 < nch:
                        prep_b_chunk(c + 1)
                    main_chunk(c)
                for j in range(9):
                    act(ulast[:, j:j + 1], Zb[j][:, 1:2], AF.Identity, [Zb[j], convw, convb], [ulast],
                        bias=convb[:, j:j + 1], scale=convw[:, j, 1:2])
                    stt("dve", ulast[:, j:j + 1], Zb[j][:, 0:1], convw[:, j, 0:1], ulast[:, j:j + 1], ALU.mult, ALU.add,
                        [Zb[j], convw, ulast], [ulast])
                cp("dve", lst[:, 0:3, 0], ulast[:, 0:3], [ulast], [lst])
                tt("dve", lst[:, 3:6, 0], ulast[:, 3:6], ulast[:, 6:9], ALU.mult, [ulast], [lst])
                st(sc["zhy"][0:384, L - 1:L].rearrange("(j p) o -> p j o", p=128), lst[:, 0:3, :], reads=[lst], slow=True)
                st(sc["sT"][:, L - 1:L].rearrange("(j p) o -> p j o", p=128), lst[:, 3:6, :], reads=[lst], slow=True)
            S.barrier()

    es01.close()

    if 2 in phases:
        for sn, L in run_seqs:
            sc = scr[sn]; fc = FC[L]; fd = fcd[sn]
            RK = fc["RK"]; U = fc["U"]; NB = U // 6; N2L = 2 * L
            with contextlib.ExitStack() as es:
                def tmp(name, shape, dt):
                    return T(es.enter_context(nc.sbuf_tensor(uniq(name), list(shape), dt)))
                h2 = tmp("h2", [64, N2L], BF16)
                fts = [tmp(f"fts{i}", [33, 512], F32) for i in range(2)]
                ysb = [tmp(f"ysb{i}", [64, 512], F32) for i in range(2)]
                kib = [tmp(f"kib{i}", [64, 512], I32) for i in range(2)]
                msk = [tmp(f"msk{i}", [64, 512], F32) for i in range(2)]
                h1 = [tmp(f"h1{i}", [64, 512], F32) for i in range(2)]
                kraw = tmp("kraw", [128, N2L], F32)
                dcs = [tmp(f"dcs{i}", [128, 2048], F32) for i in range(3)]
                ndc = [0]
                kst = [tmp(f"kst{i}", [128, 2048], BF16) for i in range(2)]
                nrm = tmp("nrm", [128, 16], F32)
                nch2 = N2L // 512

                ysb2 = [tmp(f"ysc{i}", [64, 512], F32) for i in range(2)]
                kib2 = [tmp(f"kic{i}", [64, 512], I32) for i in range(2)]
                msk2 = [tmp(f"msc{i}", [64, 512], F32) for i in range(2)]

                def sin_ops(pz, bcol, out_ap, out_t, y, ki, m):
                    return [
                        lambda: tsc("dve", y[:, :], pz[0:64, :], fab[:, 0:1], fab[:, bcol:bcol + 1], ALU.mult, ALU.add, [pz, fab], [y]),
                        lambda: cp("dve", ki[:, :], y[:, :], [y], [ki]),
                        lambda: tt("dve", y[:, :], y[:, :], ki[:, :], ALU.subtract, [y, ki], [y]),
                        lambda: act(out_ap, y[:, :], AF.Sin, [y], [out_t], scale=6.28318),
                    ]

                def l1_ops(c):
                    f = fts[c % 2]; pz = pb[c % 2]
                    return [lambda: (ld(f[:, :], fd["feats"][:, 512 * c:512 * c + 512], writes=[f]),
                                     mm(pz[0:64, :], fw1[:, :], f[:, :], True, True, [fw1, f], [pz]))] + \
                        sin_ops(pz, 1, h1[c % 2][:, :], h1[c % 2], ysb[c % 2], kib[c % 2], msk[c % 2])

                def l2_ops(c):
                    pz2 = pb[2 + c % 2]
                    return [lambda: mm(pz2[0:64, :], fw2[:, :], h1[c % 2][:, :], True, True, [fw2, h1[c % 2]], [pz2])] + \
                        sin_ops(pz2, 2, h2[:, 512 * c:512 * c + 512], h2, ysb2[c % 2], kib2[c % 2], msk2[c % 2])

                for op_ in l1_ops(0):
                    op_()
                for c in range(nch2):
                    A = l1_ops(c + 1) if c + 1 < nch2 else []
                    B = l2_ops(c)
                    for i in range(max(len(A), len(B))):
                        if i < len(A):
                            A[i]()
                        if i < len(B):
                            B[i]()
                for ct in range(3):
                    for c in range(nch2):
                        dc = dcs[(ndc[0] + c // 4) % 3]
                        if c % 4 == 0:
                            ld(dc[:, :], fd["dec"][128 * ct:128 * ct + 128, 512 * c:512 * c + 2048], writes=[dc])
                        pz = pb[c % 4]
                        col = 128 * ct if c < nch2 // 2 else 384 + 128 * ct
                        mm(pz[:, :], fw3[:, col:col + 128], h2[:, 512 * c:512 * c + 512], True, True, [fw3, h2], [pz])
                        tt("dve", kraw[:, 512 * c:512 * c + 512], pz[:, :], dc[:, 512 * (c % 4):512 * (c % 4) + 512], ALU.mult, [pz, dc], [kraw])
                    ndc[0] += nch2 // 4
                    pz = pb[6]
                    mm(pz[:, 0:8], fw3[:, 384 + 128 * ct:384 + 128 * ct + 128], h2[:, 0:8], True, True, [fw3, h2], [pz])
                    tt("dve", kraw[:, 0:1], kraw[:, 0:1], pz[:, 0:1], ALU.add, [kraw, pz], [kraw])
                    npc = N2L // 2048
                    mset("pool", nrm[:, :], 0.0, [nrm])
                    for q in range(npc):
                        act(kst[q % 2][:, :], kraw[:, 2048 * q:2048 * q + 2048], AF.Square, [kraw], [kst[q % 2], nrm],
                            accum_out=nrm[:, q:q + 1])
                    rsum("dve", nrm[:, 15:16], nrm[:, 0:npc], [nrm], [nrm])
                    act(nrm[:, 14:15], nrm[:, 15:16], AF.Ln, [nrm, cst], [nrm], bias=cst[:, 2:3], scale=1.0)
                    act(nrm[:, 13:14], nrm[:, 14:15], AF.Exp, [nrm], [nrm], scale=-0.5)
                    for q in range(npc):
                        ks = kst[q % 2]
                        if q % 2 == 0:
                            tsc("dve", ks[:, :], kraw[:, 2048 * q:2048 * q + 2048], nrm[:, 13:14], None, ALU.mult, None, [kraw, nrm], [ks])
                        else:
                            act(ks[:, :], kraw[:, 2048 * q:2048 * q + 2048], AF.Copy, [kraw, nrm], [ks], scale=nrm[:, 13:14])
                        st(sc["kfilt"][128 * ct:128 * ct + 128, 2048 * q:2048 * q + 2048], ks[:, :], reads=[ks])
            S.barrier()
            with contextlib.ExitStack() as es:
                def tmp(name, shape, dt):
                    return T(es.enter_context(nc.sbuf_tensor(uniq(name), list(shape), dt)))
                M1 = tmp("M1", [64, 2 * RK], BF16); M1f = tmp("M1f", [128, 2 * RK], BF16)
                twA = tmp("twA", [128, RK], F32); twB = tmp("twB", [128, 2, RK], F32)
                tiA = tmp("tiA", [RK, 128], F32); tiB = tmp("tiB", [RK, 128], F32)
                G3 = tmp("G3", [RK, 3, 64], BF16)
                for dst, key in ((M1, "M1"), (M1f, "M1full"), (twA, "twA"), (tiA, "tiA"), (tiB, "tiB")):
                    ld(dst[:, :], fd[key][:, :], writes=[dst])
                ld(twB[:, :, :], fd["twB"][:, :, :], writes=[twB])
                ld(G3[:, :, :], fd["G3"][:, :, :], writes=[G3])
                xu = [tmp(f"xu{i}", [128, 6, 128], BF16) for i in range(2)]
                Tb = [tmp(f"Tb{i}", [128, 4, 6, RK], BF16) for i in range(2)]
                ksp = [tmp(f"ksp{i}", [128, 2, 6 * RK], F32) for i in range(2)]
                Pb = [tmp(f"Pb{i}", [128, 4, 6 * RK], BF16) for i in range(2)]
                Qb = [tmp(f"Qb{i}", [RK, 4, 6, 128], BF16) for i in range(2)]
                yst = [tmp(f"yst{i}", [64, 6, 128], F32) for i in range(2)]
                W6 = 6 * RK

                def bc_ap(t, dims):
                    h = t.t
                    return bass.AP(h, 0, [[int(np.prod(h.shape[1:])), int(h.shape[0])]] + [[s, n] for s, n in dims])

                def stage_a(b, src_view, Mmat, npart):
                    x = xu[b % 2]; Tt = Tb[b % 2]
                    ld(x[0:npart, :, :], src_view, writes=[x])
                    for g in range(2):
                        pa = pb[g]
                        for j in range(3):
                            u = 3 * g + j
                            mm(pa[:, 2 * RK * j:2 * RK * (j + 1)], x[0:npart, u, :], Mmat[0:npart, :], True, True, [x, Mmat], [pa])
                        in0 = pa[:, 0:6 * RK].rearrange("p (u r k) -> p u r k", u=3, r=2)
                        o1 = bass.AP(Tt.t, 3 * g * RK, [[4 * 6 * RK, 128], [RK, 3], [3 * 6 * RK, 2], [1, RK]])
                        i1 = bass.AP(twA.t, 0, [[RK, 128], [0, 3], [0, 2], [1, RK]])
                        tt("dve", o1, in0, i1, ALU.mult, [pa, twA], [Tt])
                        o2 = bass.AP(Tt.t, 6 * RK + 3 * g * RK, [[4 * 6 * RK, 128], [RK, 3], [6 * RK, 2], [1, RK]])
                        i2 = bass.AP(twB.t, 0, [[2 * RK, 128], [0, 3], [RK, 2], [1, RK]])
                        tt("dve", o2, in0, i2, ALU.mult, [pa, twB], [Tt])

                def stage_b_mm(b):
                    Tt = Tb[b % 2]
                    xr = pb[2]; xi = pb[3]

                    def blk(i):
                        return Tt[:, i, :, :].rearrange("p u k -> p (u k)")
                    seq = [(0, 0, xr), (0, 2, xr), (1, 1, xr), (1, 3, xr), (0, 1, xi), (0, 3, xi), (2, 0, xi), (2, 2, xi)]
                    started = set()
                    for mi, bi, dst in seq:
                        first = id(dst) not in started
                        started.add(id(dst))
                        mm(dst[:, 0:W6], f2m[:, mi, :], blk(bi), first, False, [f2m, Tt], [dst])
                    return xr, xi

                kf_view = sc["kfilt"].rearrange("c (n1 n2) -> (c n1) n2", n2=128).rearrange("(u p) n2 -> p u n2", p=128)

                def filt_b(b):
                    xr, xi = stage_b_mm(b)
                    ks = ksp[b % 2]
                    act(ks[:, 0, :], xr[:, 0:W6], AF.Copy, [xr], [ks])
                    act(ks[:, 1, :], xi[:, 0:W6], AF.Copy, [xi], [ks])
                    st(sc["kspec"][b], ks[:, :, :], reads=[ks], writes=[ksbuf[b]])

                ksbuf = [Buf(f"kspec{b}") for b in range(NB)]
                for t in range(NB + 1):
                    if t < NB:
                        stage_a(t, kf_view[:, 6 * t:6 * t + 6, :], M1f, 128)
                    if t >= 1:
                        filt_b(t - 1)
                s_view = sc["sT"].rearrange("c (n1 n2) -> (c n1) n2", n2=128).rearrange("(u p) n2 -> p u n2", p=64)
                c_view = sc["conv"].rearrange("c (n1 n2) -> (c n1) n2", n2=128).rearrange("(u p) n2 -> p u n2", p=64)

                def sig_a(b):
                    ks = ksp[b % 2]
                    ld(ks[:, :, :], sc["kspec"][b], reads=[ksbuf[b]], writes=[ks])
                    stage_a(b, s_view[:, 6 * b:6 * b + 6, :], M1, 64)

                def sig_b(b):
                    ks = ksp[b % 2]
                    xr, xi = stage_b_mm(b)
                    Pt = Pb[b % 2]
                    tt("dve", Pt[:, 0, :], xr[:, 0:W6], ks[:, 0, :], ALU.mult, [xr, ks], [Pt])
                    tt("dve", Pt[:, 2, :], xr[:, 0:W6], ks[:, 1, :], ALU.mult, [xr, ks], [Pt])
                    tt("dve", Pt[:, 1, :], xi[:, 0:W6], ks[:, 1, :], ALU.mult, [xi, ks], [Pt])
                    tt("dve", Pt[:, 3, :], xi[:, 0:W6], ks[:, 0, :], ALU.mult, [xi, ks], [Pt])

                def sig_c(b):
                    Pt = Pb[b % 2]; Qt = Qb[b % 2]
                    for g in range(3):
                        pc = pb[4 + g]
                        for j in range(2):
                            u = 2 * g + j
                            for bi, mi in ((0, 0), (1, 1), (2, 2), (3, 2)):
                                mm(pc[0:RK, 256 * j:256 * j + 256], Pt[:, bi, RK * u:RK * u + RK], imat[:, mi, :],
                                   bi == 0, bi == 3, [Pt, imat], [pc])
                        in0 = pc[0:RK, :].rearrange("p (u r n) -> p u r n", u=2, r=2)
                        oA = bass.AP(Qt.t, 2 * g * 128, [[4 * 6 * 128, RK], [128, 2], [6 * 128, 2], [1, 128]])
                        iA = bass.AP(tiA.t, 0, [[128, RK], [0, 2], [0, 2], [1, 128]])
                        tt("dve", oA, in0, iA, ALU.mult, [pc, tiA], [Qt])
                        oB = bass.AP(Qt.t, 2 * 6 * 128 + 2 * g * 128, [[4 * 6 * 128, RK], [128, 2], [6 * 128, 2], [1, 128]])
                        iB = bass.AP(tiB.t, 0, [[128, RK], [0, 2], [0, 2], [1, 128]])
                        tt("dve", oB, in0, iB, ALU.mult, [pc, tiB], [Qt])

                def sig_d(b):
                    Qt = Qb[b % 2]
                    ys = yst[b % 2]
                    py = pb[7]
                    for hlf in range(2):
                        for bi, gi in ((0, 0), (3, 1), (2, 2), (1, 2)):
                            mm(py[0:64, 0:384], G3[:, gi, :], Qt[:, bi, 3 * hlf:3 * hlf + 3, :].rearrange("p u n -> p (u n)"),
                               bi == 0, bi == 1, [G3, Qt], [py])
                        act(ys[:, 3 * hlf:3 * hlf + 3, :], py[0:64, 0:384].rearrange("p (u n) -> p u n", u=3), AF.Copy, [py], [ys])
                    st(c_view[:, 6 * b:6 * b + 6, :], ys[:, :, :], reads=[ys])

                for t in range(NB + 3):
                    if t < NB:
                        sig_a(t)
                    if 0 <= t - 1 < NB:
                        sig_b(t - 1)
                    if 0 <= t - 2 < NB:
                        sig_c(t - 2)
                    if 0 <= t - 3 < NB:
                        sig_d(t - 3)
            S.barrier()
            with contextlib.ExitStack() as es:
                def tmp(name, shape, dt):
                    return T(es.enter_context(nc.sbuf_tensor(uniq(name), list(shape), dt)))
                PC = 2048
                zp = [tmp(f"zq{i}", [128, PC + 2], BF16) for i in range(3)]
                uc = [tmp(f"ue{i}", [128, PC], F32) for i in range(3)]
                sq_ = [tmp(f"sq{i}", [128, PC], BF16) for i in range(3)]
                gq_ = [tmp(f"gq{i}", [128, PC], BF16) for i in range(3)]
                cq_ = [tmp(f"cq{i}", [128, PC], F32) for i in range(3)]
                mq_ = [tmp(f"mq{i}", [128, PC], BF16) for i in range(3)]
                n2e = 0
                for ct in range(3):
                    for piece in range(L // PC):
                        a = PC * piece
                        z = zp[n2e % 3]; u = uc[n2e % 3]; sv = sq_[n2e % 3]; gv = gq_[n2e % 3]; cv = cq_[n2e % 3]; mv = mq_[n2e % 3]
                        n2e += 1
                        ld(z[:, 0:PC], sc["zhy"][128 * ct:128 * ct + 128, a:a + PC], writes=[z])
                        ld(sv[:, :], sc["sT"][128 * ct:128 * ct + 128, a:a + PC], writes=[sv])
                        ld(gv[:, :], sc["gT"][128 * ct:128 * ct + 128, a:a + PC], writes=[gv])
                        ld(cv[:, :], sc["conv"][128 * ct:128 * ct + 128, a:a + PC], writes=[cv])
                        stt("dve", cv[:, :], sv[:, :], skipv[:, ct:ct + 1], cv[:, :], ALU.mult, ALU.add, [sv, skipv, cv], [cv])
                        tt("pool", u[:, :], z[:, 0:PC], cv[:, :], ALU.mult, [z, cv], [u])
                        tt("dve", mv[:, :], u[:, :], gv[:, :], ALU.mult, [u, gv], [mv])
                        st(sc["mixed"][128 * ct:128 * ct + 128, a:a + PC], mv[:, :], reads=[mv])
            S.barrier()

    if 3 in phases:
        with contextlib.ExitStack() as es:
            def tmp(name, shape, dt):
                return T(es.enter_context(nc.sbuf_tensor(uniq(name), list(shape), dt)))
            if not eb_built[0]:
                eb_built[0] = True
                with contextlib.ExitStack() as es2:
                    for sl in build_eb_tiles(lambda name, shape, dt: T(es2.enter_context(nc.sbuf_tensor(uniq(name), list(shape), dt)))):
                        sl()
                    S.barrier()
            eb = {}
            for key, idx in EBIDX.items():
                t = tmp(f"eb{idx}", [128, 4, 128], BF16)
                eb[key] = t
                ld(t[:, :, :].rearrange("p a b -> p (a b)"), ebd_d[idx], writes=[t])
            SR = 2048
            qsb = tmp("qsb", [128, 3, SR], BF16)
            ksb = tmp("ksb", [128, 3, 3 * SR], BF16)
            gsb = tmp("gsb", [128, 3, SR], BF16)
            acc = tmp("acc", [128, 3, 2, SR], F32)
            dal = tmp("dal", [128, SR], F32)
            msb = tmp("msb", [128, SR], BF16)
            NV = 8
            vsb = [tmp(f"vsb{i}", [128, 768], BF16) for i in range(NV)]
            pra = [tmp(f"pra{i}", [128, 512], BF16) for i in range(4)]
            ptb = [tmp(f"ptb{i}", [128, 512], BF16) for i in range(4)]
            vn_ctr = [0]
            it = [0]
            for sn, L in run_seqs:
                sc = scr[sn]
                for sr in range(L // SR):
                    T0 = sr * SR
                    w0 = max(0, T0 - 2048); w1 = min(L, T0 + SR + 2048)
                    ld(qsb[:, :, :], sc["qT"][:, T0:T0 + SR].rearrange("(j p) t -> p j t", p=128), writes=[qsb])
                    ld(ksb[:, :, 0:w1 - w0], sc["kT"][:, w0:w1].rearrange("(j p) t -> p j t", p=128), writes=[ksb])
                    ld(gsb[:, :, :], sc["gT"][384:768, T0:T0 + SR].rearrange("(j p) t -> p j t", p=128), writes=[gsb])
                    jobs = []
                    for ci, dil in enumerate(DILS):
                        n = L // dil
                        for r in range(dil):
                            for jt in range(SR // dil // 128):
                                bq = T0 // dil + 128 * jt
                                if n == 128:
                                    vn, kstart, nkt = "only", 0, 1
                                elif bq == 0:
                                    vn, kstart, nkt = "first", 0, 2
                                elif bq + 128 == n:
                                    vn, kstart, nkt = "last", n - 256, 2
                                else:
                                    vn, kstart, nkt = "int", bq - 64, 2
                                for pr in range(3):
                                    jobs.append((ci, dil, r, bq, vn, kstart, nkt, pr))
                    vcache = {}
                    state = {}

                    def s_stage(j):
                        ci, dil, r, bq, vn, kstart, nkt, pr = jobs[j]
                        vts = []
                        for kt in range(nkt):
                            key = (ci, r, kstart + 128 * kt)
                            if key not in vcache:
                                vt = vsb[vn_ctr[0] % NV]; vn_ctr[0] += 1
                                for kk in [k for k, v in vcache.items() if v is vt]:
                                    del vcache[kk]
                                tok0 = r + dil * (kstart + 128 * kt)
                                src = bass.AP(sc["vtok"].tensor, tok0 * 768, [[dil * 768, 128], [1, 768]])
                                ld(vt[:, :], src, writes=[vt])
                                vcache[key] = vt
                            vts.append(vcache[key])
                        qcol0 = r + dil * (bq - T0 // dil)
                        qsl = slice(qcol0, qcol0 + dil * 127 + 1, dil)
                        i = it[0]; it[0] += 1
                        psab = (pb[2 * (i % 3)], pb[2 * (i % 3) + 1])
                        pr_raw = pra[i % 4]; pt = ptb[i % 4]
                        W = 128 * nkt
                        for kt in range(nkt):
                            for ab in range(2):
                                kc0 = r + dil * (kstart + 128 * kt) - w0
                                assert kc0 >= 0 and kc0 + dil * 127 < w1 - w0
                                ksl = slice(kc0, kc0 + dil * 127 + 1, dil)
                                mm(psab[ab][:, 128 * kt:128 * kt + 128],
                                   ksb[64 * ab:64 * ab + 64, pr, ksl], qsb[64 * ab:64 * ab + 64, pr, qsl],
                                   True, True, [ksb, qsb], [psab[ab]])
                        for ab in range(2):
                            act(pr_raw[:, W * ab:W * ab + W], psab[ab][:, 0:W], AF.Exp, [psab[ab]], [pr_raw])
                        e = eb[(ci, vn, pr)]
                        tt("dve", pt[:, 0:2 * W], pr_raw[:, 0:2 * W], e[:, :, :].rearrange("p a b -> p (a b)")[:, 0:2 * W], ALU.mult,
                           [pr_raw, e], [pt])
                        state[j] = (vts, pt, qcol0, i)

                    def pv_stage(j):
                        ci, dil, r, bq, vn, kstart, nkt, pr = jobs[j]
                        vts, pt, qcol0, i = state.pop(j)
                        pnd = pb[6 + i % 2]
                        first = True
                        for ab in range(2):
                            for kt in range(nkt):
                                bi = nkt * ab + kt
                                hcol = (2 * pr + ab) * 128
                                mm(pnd[:, 128 * ab:128 * ab + 128], vts[kt][:, hcol:hcol + 128], pt[:, 128 * bi:128 * bi + 128],
                                   first, False, [vts[kt], pt], [pnd])
                                first = False
                        av = bass.AP(acc.t, pr * 2 * SR + qcol0, [[3 * 2 * SR, 128], [SR, 2], [dil, 128]])
                        pv2 = pnd[:, 0:256].rearrange("p (a q) -> p a q", a=2)
                        if ci == 0:
                            act(av, pv2, AF.Copy, [pnd], [acc])
                        else:
                            tt("dve", av, pv2, av, ALU.add, [pnd, acc], [acc])

                    s_stage(0)
                    s_stage(1)
                    for j in range(len(jobs)):
                        if j + 2 < len(jobs):
                            s_stage(j + 2)
                        pv_stage(j)
                    for pr in range(3):
                        ld(dal[0:64, :], acc[64:128, pr, 0, :], reads=[acc], writes=[dal])
                        ld(dal[64:128, :], acc[0:64, pr, 1, :], reads=[acc], writes=[dal])
                        act(dal[:, :], dal[:, :], AF.Ln, [dal], [dal])
                        act(dal[:, :], dal[:, :], AF.Exp, [dal], [dal], scale=-1.0)
                        tt("dve", dal[0:64, :], acc[0:64, pr, 0, :], dal[0:64, :], ALU.mult, [acc, dal], [dal])
                        tt("pool", dal[64:128, :], acc[64:128, pr, 1, :], dal[64:128, :], ALU.mult, [acc, dal], [dal])
                        tt("dve", msb[:, :], dal[:, :], gsb[:, pr, :], ALU.mult, [dal, gsb], [msb])
                        st(sc["mixed"][384 + 128 * pr:384 + 128 * pr + 128, T0:T0 + SR], msb[:, :], reads=[msb])
            S.barrier()

    if 4 in phases or 5 in phases:
        with contextlib.ExitStack() as es:
            def tmp(name, shape, dt):
                return T(es.enter_context(nc.sbuf_tensor(uniq(name), list(shape), dt)))
            mqs = [tmp(f"mqs{i}", [128, 2, 512], BF16) for i in range(2)]
            gms = [tmp(f"gms{i}", [128, 2, 512], BF16) for i in range(2)]
            pms = [tmp(f"pms{i}", [128, 4, 512], BF16) for i in range(2)]
            rdn = [tmp(f"rdn{i}", [128, 512], F32) for i in range(2)]
            mx = [tmp(f"mx{i}", [128, 8, 512], BF16) for i in range(2)]
            xr = [tmp(f"xr{i}", [128, D], F32) for i in range(3)]
            yo = [tmp(f"yo{i}", [128, D], F32) for i in range(2)]
            chunks = [(sn, L, c) for sn, L in run_seqs for c in range(L // 512)]
            n5 = [0]

            def p4_s(ci):
                sn, L, c = chunks[ci]
                sc = scr[sn]; c0 = 512 * c
                mq = mqs[ci % 2]; gm = gms[ci % 2]; m = mx[ci % 2]
                ld(mq[:, :, :], sc["mqT"][:, c0:c0 + 512].rearrange("(j p) t -> p j t", p=128), writes=[mq])
                ld(gm[:, :, :], sc["gT"][768:1024, c0:c0 + 512].rearrange("(j p) t -> p j t", p=128), writes=[gm])
                ld(m[:, 0:6, :], sc["mixed"][0:768, c0:c0 + 512].rearrange("(k p) t -> p k t", p=128), writes=[m])

            def p4_qk(ci, pr):
                sn, L, c = chunks[ci]
                mq = mqs[ci % 2]; pm = pms[pr]
                for ab in range(2):
                    for mt in range(2):
                        ps = pb[2 * ab + mt]
                        mm(ps[:, :], kmT[sn][64 * ab:64 * ab + 64, pr, 128 * mt:128 * mt + 128], mq[64 * ab:64 * ab + 64, pr, :],
                           True, True, [kmT[sn], mq], [ps])
                        act(pm[:, 2 * ab + mt, :], ps[:, :], AF.Exp, [ps], [pm])

            def p4_pv(ci, pr):
                sn, L, c = chunks[ci]
                gm = gms[ci % 2]; m = mx[ci % 2]
                pm = pms[pr]; rd = rdn[pr]
                pn = pb[4]; pd = pb[5]
                for ab in range(2):
                    for mt in range(2):
                        mm(pn[64 * ab:64 * ab + 64, :], vm[sn][:, mt, 128 * pr + 64 * ab:128 * pr + 64 * ab + 64], pm[:, 2 * ab + mt, :],
                           mt == 0, mt == 1, [vm[sn], pm], [pn])
                        mm(pd[64 * ab:64 * ab + 64, :], ones_bf[:, :], pm[:, 2 * ab + mt, :], mt == 0, mt == 1, [ones_bf, pm], [pd])
                act(rd[:, :], pd[:, :], AF.Ln, [pd], [rd])
                act(rd[:, :], rd[:, :], AF.Exp, [rd], [rd], scale=-1.0)
                tt("dve", rd[:, :], pn[:, :], rd[:, :], ALU.mult, [pn, rd], [rd])
                tt("pool", m[:, 6 + pr, :], rd[:, :], gm[:, pr, :], ALU.mult, [rd, gm], [m])

            def p5_tile(ci, i):
                sn, L, c = chunks[ci]
                c0 = 512 * c
                m = mx[ci % 2]
                r0 = c0 + 128 * i
                x = xr[n5[0] % 3]; y = yo[n5[0] % 2]
                ld(x[:, :], x_d[sn][r0:r0 + 128, :], writes=[x])
                for hf in range(2):
                    pz = pb[6 + hf]
                    for k in range(8):
                        mm(pz[:, :], m[:, k, 128 * i:128 * i + 128], w_out_bf[:, k, 512 * hf:512 * hf + 512], k == 0, k == 7,
                           [m, w_out_bf], [pz])
                    tt("dve", y[:, 512 * hf:512 * hf + 512], pz[:, :], x[:, 512 * hf:512 * hf + 512], ALU.add, [pz, x], [y])
                st(y_d[sn][r0:r0 + 128, :], y[:, :], reads=[y], final=True)
                n5[0] += 1

            p4_s(0)
            for pr in range(2):
                p4_qk(0, pr)
                p4_pv(0, pr)
            for ci in range(len(chunks)):
                nxt = ci + 1 < len(chunks)
                if nxt:
                    p4_s(ci + 1)
                    p4_qk(ci + 1, 0)
                p5_tile(ci, 0)
                p5_tile(ci, 1)
                if nxt:
                    p4_pv(ci + 1, 0)
                    p4_qk(ci + 1, 1)
                p5_tile(ci, 2)
                p5_tile(ci, 3)
                if nxt:
                    p4_pv(ci + 1, 1)

    S.emit_all()
    return nc, dbg_outs


def make_in_maps(inp):
    f2, im = f2_consts()
    shared = {}
    shared["w_in"] = _f32(inp["w_in"][0]); shared["w_out"] = _f32(inp["w_out"][0]); shared["w_mem_kv"] = _f32(inp["w_mem_kv"][0])
    shared["g_in"] = _f32(np.asarray(inp["norm_in"][0]).reshape(8, 128).T)
    shared["g_mem"] = _f32(np.asarray(inp["mem_norm"][0]).reshape(8, 128).T)
    gn = np.stack([np.tile(np.asarray(inp[k][0]), 2) for k in ("att_q_norm", "att_k_norm", "mem_q_norm", "mem_k_norm")], axis=1)
    shared["gains"] = _f32(gn)
    cw = np.asarray(inp["hy_conv_w"][0])
    shared["convw"] = _f32(cw.T.reshape(9, 128, 3).transpose(1, 0, 2))
    shared["convb"] = _f32(np.asarray(inp["hy_conv_b"][0]).reshape(9, 128).T)
    shared["skipv"] = _f32(np.asarray(inp["hy_skip"][0]).reshape(3, 128).T)
    shared["fw1"] = _f32(inp["hy_filt_w1"][0]); shared["fw2"] = _f32(inp["hy_filt_w2"][0]); shared["fw3"] = _f32(inp["hy_filt_w3"][0])
    shared["fvec"] = _f32(np.stack([np.asarray(inp["hy_filt_b1"][0]), np.asarray(inp["hy_filt_freq"][0]),
                                    np.asarray(inp["hy_filt_b2"][0])], axis=1))
    shared["rel_bias"] = _f32(inp["rel_bias"])
    shared["ident"] = _bf(np.eye(128)); shared["jmat"] = _bf(np.eye(128)[::-1])
    ob = np.zeros((128, 128)); ob[:64, :64] = 1; ob[64:, 64:] = 1
    shared["onesblk"] = _bf(ob)
    shared["f2"] = f2; shared["imat"] = im
    shared["bias_oh"] = _f32(bias_onehot())
    for sn, L in (("p", LP), ("s", LS)):
        c = fft_consts(L)
        shared[f"M1_{sn}"] = c["M1"]; shared[f"M1f_{sn}"] = c["M1full"]
        shared[f"twA_{sn}"] = c["twA"]; shared[f"twB_{sn}"] = c["twB"]
        shared[f"tiA_{sn}"] = c["tiA"]; shared[f"tiB_{sn}"] = c["tiB"]; shared[f"G3_{sn}"] = c["G3"]
        ft, dec = filter_consts(L)
        shared[f"feats_{sn}"] = ft; shared[f"dec_{sn}"] = dec
    maps = []
    for i in range(NCORES):
        m = dict(shared)
        m["x_p"] = _f32(inp["x_prompt"][i]); m["x_s"] = _f32(inp["x_sample"][i])
        m["mem_p"] = _f32(inp["mem_prompt"][i]); m["mem_s"] = _f32(inp["mem_sample"][i])
        maps.append(m)
    return maps


_CACHE = {}


def kernel(**inputs):
    inp = {k: np.asarray(v) for k, v in inputs.items()}
    maps = make_in_maps(inp)
    if "nc" not in _CACHE:
        _CACHE["nc"] = build_program()[0]
    res = run_bass_kernel_spmd(_CACHE["nc"], maps, core_ids=list(range(NCORES)))
    y_p = np.stack([np.asarray(res.results[i]["y_p"], dtype=np.float32) for i in range(NCORES)], axis=0)
    y_s = np.stack([np.asarray(res.results[i]["y_s"], dtype=np.float32) for i in range(NCORES)], axis=0)
    return (y_p, y_s)
```
